# Optimizing a Trainium2 kernel written in Bass

```python
import math
import jax
import jax.numpy as jnp
from jax import lax
import numpy as np

D_MODEL = 1024
BATCH = 16
SEQ = 2048
DEPTH = 1
DEC_BATCH = 4
DEC_SEQ = 4096
PAST_LEN = 128

D_FF = 2816
CHUNK = 128
A_WIDTH = 512
A_GROUPS = 8
A_GROUP_DIM = A_WIDTH // A_GROUPS
B_HEADS = 4
HEAD_DIM = 64
B_QK = B_HEADS * 2 * HEAD_DIM
B_V = B_HEADS * 2 * HEAD_DIM
Q_BLOCK = 128
N_BUCKETS = 32
MAX_DISTANCE = 128
IN_COLS = 2 * A_WIDTH + 2 * B_QK + B_V + 2 * D_MODEL
EPS = 1e-6

kernel_name = "hybrid_sgu_diffattn_macaron_encoder"


def rms_norm(x, g):
    xf = x.astype(jnp.float32)
    y = xf * lax.rsqrt(jnp.mean(xf * xf, axis=-1, keepdims=True) + EPS)
    return (y * g.astype(jnp.float32)).astype(x.dtype)


def swiglu_ffn(x, w_in, w_out):
    gate, up = jnp.split(x @ w_in, 2, axis=-1)
    return (jax.nn.silu(gate) * up) @ w_out


def rel_position_bucket(rel):
    half = N_BUCKETS // 2
    max_exact = half // 2
    bucket = jnp.where(rel > 0, half, 0).astype(jnp.int32)
    n = jnp.abs(rel)
    nf = jnp.maximum(n, 1).astype(jnp.float32)
    large = max_exact + (jnp.log(nf / max_exact) / math.log(MAX_DISTANCE / max_exact)
                         * (half - max_exact)).astype(jnp.int32)
    large = jnp.minimum(large, half - 1)
    return bucket + jnp.where(n < max_exact, n, large)


def spatial_gating(u, v, norm_g, w_s, b_s):
    B, S, _ = v.shape
    vc = rms_norm(v, norm_g).reshape(B, S // CHUNK, CHUNK, A_GROUPS, A_GROUP_DIM)
    s = jnp.einsum('gts,bcsgd->bctgd', w_s, vc) + b_s.T[None, None, :, :, None]
    return u * s.reshape(B, S, A_WIDTH)


def diff_attention(q, k, v, lam, rel_bias):
    B, S, H, _, dh = q.shape
    nq = S // Q_BLOCK
    qb = jnp.moveaxis(q.reshape(B, nq, Q_BLOCK, H, 2, dh), 1, 0)
    kpos = jnp.arange(S, dtype=jnp.int32)

    def one_block(args):
        qblk, start = args
        logits = jnp.einsum('bqhcd,bkhcd->bhcqk', qblk, k,
                            preferred_element_type=jnp.float32)
        qpos = start + jnp.arange(Q_BLOCK, dtype=jnp.int32)
        bias = rel_bias[rel_position_bucket(kpos[None, :] - qpos[:, None])].astype(jnp.float32)
        logits = logits + jnp.transpose(bias, (2, 0, 1))[None, :, None]
        p = jax.nn.softmax(logits, axis=-1)
        w = p[:, :, 0] - lam * p[:, :, 1]
        return jnp.einsum('bhqk,bkhe->bqhe', w.astype(v.dtype), v)

    starts = jnp.arange(nq, dtype=jnp.int32) * Q_BLOCK
    out = lax.map(one_block, (qb, starts))
    return jnp.moveaxis(out, 0, 1).reshape(B, S, H, 2 * dh)


def encoder_layer(x, layer_idx, rel_bias, ffn1_norm, ffn1_w_in, ffn1_w_out, mix_norm, w_in,
                  gate_bias, sgu_norm, sgu_w, sgu_b, q_norm, k_norm, lambda_q1, lambda_k1,
                  lambda_q2, lambda_k2, diff_subln, w_proj_a, w_proj_b, w_out,
                  ffn2_norm, ffn2_w_in, ffn2_w_out, final_norm):
    B, S, _ = x.shape
    x = x + 0.5 * swiglu_ffn(rms_norm(x, ffn1_norm), ffn1_w_in, ffn1_w_out)

    h = rms_norm(x, mix_norm)
    proj = h @ w_in
    c0 = 2 * A_WIDTH
    c1 = c0 + B_QK
    c2 = c1 + B_QK
    c3 = c2 + B_V
    uv, q, k, v, g = jnp.split(proj, [c0, c1, c2, c3], axis=-1)

    u, va = jnp.split(jax.nn.gelu(uv), 2, axis=-1)
    a_out = spatial_gating(u, va, sgu_norm, sgu_w, sgu_b)

    q = rms_norm(q.reshape(B, S, B_HEADS, 2, HEAD_DIM), q_norm) * (HEAD_DIM ** -0.5)
    k = rms_norm(k.reshape(B, S, B_HEADS, 2, HEAD_DIM), k_norm)
    v = v.reshape(B, S, B_HEADS, 2 * HEAD_DIM)
    lambda_init = 0.8 - 0.6 * math.exp(-0.3 * layer_idx)
    lam = (jnp.exp(jnp.sum(lambda_q1.astype(jnp.float32) * lambda_k1.astype(jnp.float32)))
           - jnp.exp(jnp.sum(lambda_q2.astype(jnp.float32) * lambda_k2.astype(jnp.float32)))
           + lambda_init)
    b_out = diff_attention(q, k, v, lam, rel_bias)
    b_out = (rms_norm(b_out, diff_subln) * (1.0 - lambda_init)).reshape(B, S, B_V)

    gates = jax.nn.sigmoid((g + gate_bias).astype(jnp.float32)).astype(x.dtype)
    g_a, g_b = jnp.split(gates, 2, axis=-1)
    merged = g_a * (a_out @ w_proj_a) + g_b * (b_out @ w_proj_b)
    x = x + merged @ w_out

    x = x + 0.5 * swiglu_ffn(rms_norm(x, ffn2_norm), ffn2_w_in, ffn2_w_out)
    return rms_norm(x, final_norm)


def setup_inputs(seed: int = 0) -> dict:
    key = jax.random.key(seed)
    ks = iter(jax.random.split(key, 40))
    nrm = lambda shape, s: jax.random.normal(next(ks), shape, jnp.float32) * s
    gain = lambda shape: 1.0 + nrm(shape, 0.01)
    L = DEPTH
    return {
        "x_prompt": nrm((BATCH, SEQ, D_MODEL), 1.0),
        "x_sample": nrm((DEC_BATCH, DEC_SEQ, D_MODEL), 1.0),
        "rel_bias": nrm((N_BUCKETS, B_HEADS), 0.5),
        "ffn1_norm": gain((L, D_MODEL)),
        "ffn1_w_in": nrm((L, D_MODEL, 2 * D_FF), D_MODEL ** -0.5),
        "ffn1_w_out": nrm((L, D_FF, D_MODEL), D_FF ** -0.5),
        "mix_norm": gain((L, D_MODEL)),
        "w_in": nrm((L, D_MODEL, IN_COLS), D_MODEL ** -0.5),
        "gate_bias": nrm((L, 2 * D_MODEL), 0.02),
        "sgu_norm": gain((L, A_WIDTH)),
        "sgu_w": nrm((L, A_GROUPS, CHUNK, CHUNK), CHUNK ** -0.5),
        "sgu_b": 1.0 + nrm((L, A_GROUPS, CHUNK), 0.02),
        "q_norm": gain((L, HEAD_DIM)),
        "k_norm": gain((L, HEAD_DIM)),
        "lambda_q1": nrm((L, HEAD_DIM), 0.1),
        "lambda_k1": nrm((L, HEAD_DIM), 0.1),
        "lambda_q2": nrm((L, HEAD_DIM), 0.1),
        "lambda_k2": nrm((L, HEAD_DIM), 0.1),
        "diff_subln": gain((L, 2 * HEAD_DIM)),
        "w_proj_a": nrm((L, A_WIDTH, D_MODEL), A_WIDTH ** -0.5),
        "w_proj_b": nrm((L, B_V, D_MODEL), B_V ** -0.5),
        "w_out": nrm((L, D_MODEL, D_MODEL), D_MODEL ** -0.5),
        "ffn2_norm": gain((L, D_MODEL)),
        "ffn2_w_in": nrm((L, D_MODEL, 2 * D_FF), D_MODEL ** -0.5),
        "ffn2_w_out": nrm((L, D_FF, D_MODEL), D_FF ** -0.5),
        "final_norm": gain((L, D_MODEL)),
    }


def reference(x_prompt, x_sample, rel_bias, ffn1_norm, ffn1_w_in, ffn1_w_out, mix_norm, w_in,
              gate_bias, sgu_norm, sgu_w, sgu_b, q_norm, k_norm, lambda_q1, lambda_k1,
              lambda_q2, lambda_k2, diff_subln, w_proj_a, w_proj_b, w_out,
              ffn2_norm, ffn2_w_in, ffn2_w_out, final_norm):
    y_prompt = x_prompt
    y_sample = x_sample
    for l in range(DEPTH):
        layer_params = (ffn1_norm[l], ffn1_w_in[l], ffn1_w_out[l], mix_norm[l], w_in[l],
                        gate_bias[l], sgu_norm[l], sgu_w[l], sgu_b[l], q_norm[l], k_norm[l],
                        lambda_q1[l], lambda_k1[l], lambda_q2[l], lambda_k2[l], diff_subln[l],
                        w_proj_a[l], w_proj_b[l], w_out[l], ffn2_norm[l], ffn2_w_in[l],
                        ffn2_w_out[l], final_norm[l])
        y_prompt = encoder_layer(y_prompt, l, rel_bias, *layer_params)
        y_sample = encoder_layer(y_sample, l, rel_bias, *layer_params)
    return (y_prompt, y_sample)
```

```python
import math
import numpy as np
import concourse.bass as bass
import concourse.mybir as mybir
from concourse.bass_utils import run_bass_kernel_spmd
from contextlib import ExitStack

F32 = mybir.dt.float32
BF16 = mybir.dt.bfloat16
AF = mybir.ActivationFunctionType
ALU = mybir.AluOpType
AX = mybir.AxisListType

D = 1024
DFF = 2816
NCH = 22
EPS = 1e-6
SQRT_GC = math.sqrt(0.044715)
GELU_S = 2.0 * math.sqrt(2.0 / math.pi)
LAMBDA_INIT = 0.8 - 0.6 * math.exp(0.0)


class Op:
    __slots__ = ("eng", "fn", "deps", "is_dma", "ch", "ch_idx", "signaled", "tick")

    def __init__(self, eng, fn, is_dma=False, ch=None):
        self.eng = eng
        self.fn = fn
        self.deps = []
        self.is_dma = is_dma
        self.ch = ch
        self.ch_idx = 0
        self.signaled = False
        self.tick = 0


class Prog:
    ENGS = ["pe", "act", "dve", "pool", "sp"]

    def __init__(self, nc):
        self.nc = nc
        self.ops = {e: [] for e in self.ENGS}
        self.last_w = {}
        self.readers = {}
        self.ch_ops = {}
        self.cur_barrier = None

    def add(self, eng, fn, reads=(), writes=(), dma_ch=None, war=()):
        op = Op(eng, fn, is_dma=dma_ch is not None, ch=dma_ch)
        deps = set()
        if self.cur_barrier is not None:
            deps.add(self.cur_barrier)
        for r in war:
            rd = self.readers.get(r)
            if rd:
                deps.update(rd.values())
        for r in reads:
            w = self.last_w.get(r)
            if w is not None:
                deps.add(w)
        for r in writes:
            w = self.last_w.get(r)
            if w is not None:
                deps.add(w)
            rd = self.readers.get(r)
            if rd:
                deps.update(rd.values())
        if dma_ch is not None:
            lst = self.ch_ops.setdefault(dma_ch, [])
            if lst:
                deps.add(lst[-1])
            lst.append(op)
            op.ch_idx = len(lst)
        for d in deps:
            if d.is_dma:
                op.deps.append(d)
                continue
            if d.eng == eng and eng == "pe" and not op.is_dma:
                continue
            d.signaled = True
            op.deps.append(d)
        for r in reads:
            key = ("d", dma_ch) if op.is_dma else eng
            self.readers.setdefault(r, {})[key] = op
        for r in writes:
            self.last_w[r] = op
            self.readers[r] = {}
        self.ops[eng].append(op)
        return op

    def barrier(self, eng, fn):
        op = Op(eng, fn)
        for e in self.ENGS:
            for o in reversed(self.ops[e]):
                if not o.is_dma and o.fn is not None:
                    if not (e == eng and eng == "pe"):
                        o.signaled = True
                        op.deps.append(o)
                    break
        for ch, lst in self.ch_ops.items():
            if lst:
                op.deps.append(lst[-1])
        op.signaled = True
        self.ops[eng].append(op)
        self.cur_barrier = op
        self.last_w = {}
        self.readers = {}
        return op

    def emit(self, stack):
        nc = self.nc
        esem = {}
        for e in self.ENGS:
            if e == "sp":
                continue
            esem[e] = stack.enter_context(nc.semaphore("S_" + e))
        chsem = {}
        for i, ch in enumerate(self.ch_ops):
            chsem[ch] = stack.enter_context(nc.semaphore("D%d" % i))
        for e in self.ENGS:
            t = 0
            for op in self.ops[e]:
                if op.signaled and not op.is_dma:
                    t += 1
                    op.tick = t
        block = stack.enter_context(nc.Block())
        ops = self.ops

        def run(e, eng):
            waited = {}
            for op in ops[e]:
                need = {}
                for d in op.deps:
                    if d.is_dma:
                        key = ("d", d.ch)
                        val = 16 * d.ch_idx
                    else:
                        key = ("e", d.eng)
                        val = d.tick
                    if val > need.get(key, 0):
                        need[key] = val
                for key, val in need.items():
                    if val <= waited.get(key, 0):
                        continue
                    waited[key] = val
                    sem = chsem[key[1]] if key[0] == "d" else esem[key[1]]
                    eng.wait_ge(sem, val)
                if op.fn is None:
                    continue
                ins = op.fn(eng)
                if op.is_dma:
                    ins.then_inc(chsem[op.ch], 16)
                elif op.signaled:
                    ins.then_inc(esem[e], 1)

        @block.tensor
        def _(eng):
            run("pe", eng)

        @block.scalar
        def _(eng):
            run("act", eng)

        @block.vector
        def _(eng):
            run("dve", eng)

        @block.gpsimd
        def _(eng):
            run("pool", eng)

        @block.sync
        def _(eng):
            run("sp", eng)


ARENA_BYTES = 210944


def build_program(debug=None):
    nc = bass.Bass("TRN2", target_bir_lowering=False)
    st = ExitStack()

    def din(name, shape):
        return nc.dram_tensor(name, list(shape), F32, kind="ExternalInput")

    H = {}
    for name, shape in [
        ("xp", (2, 2048, D)), ("xso", (2048, D)), ("xsx", (2048, D)), ("par", (128, 2)),
        ("rel_bias", (32, 4)), ("ffn1_norm", (1, D)), ("ffn1_w_in", (D, 2 * DFF)), ("ffn1_w_out", (DFF, D)),
        ("mix_norm", (1, D)), ("w_in", (D, 4608)), ("gate_bias", (1, 2048)), ("sgu_norm", (1, 512)),
        ("sgu_w", (8, 128, 128)), ("sgu_b", (8, 128)), ("q_norm", (1, 64)), ("k_norm", (1, 64)),
        ("lambda_q1", (1, 64)), ("lambda_k1", (1, 64)), ("lambda_q2", (1, 64)), ("lambda_k2", (1, 64)),
        ("diff_subln", (1, 128)), ("w_proj_a", (512, D)), ("w_proj_b", (512, D)), ("w_out", (D, D)),
        ("ffn2_norm", (1, D)), ("ffn2_w_in", (D, 2 * DFF)), ("ffn2_w_out", (DFF, D)), ("final_norm", (1, D)),
        ("ident", (128, 128)), ("bones", (128, 128)), ("oh", (32, 512)),
    ]:
        H[name] = din(name, shape)
    yp = nc.dram_tensor("yp", [2, 2048, D], F32, kind="ExternalOutput")
    ys = nc.dram_tensor("ys", [2048, D], F32, kind="ExternalOutput")
    ktx = nc.dram_tensor("ktx", [128, 4 * 2048], BF16)
    vx = nc.dram_tensor("vx", [128, 16 * 516], BF16)
    gscr = nc.dram_tensor("gscr", [4, 128, 512], F32)
    wscr = nc.dram_tensor("wscr", [128, 13 * 4096], BF16)

    def bc(h, n, off=0):
        return bass.AP(h, off, [[0, 128], [1, n]])

    with st:
        P = Prog(nc)
        arena = st.enter_context(nc.sbuf_tensor("arena", [128, ARENA_BYTES // 2], BF16))
        PS = [st.enter_context(nc.psum_tensor("ps%d" % i, [128, 1024], F32)) for i in range(4)]

        class Alloc:
            def __init__(self, base, limit):
                self.off = base
                self.limit = limit

            def __call__(self, dtype, shape):
                esz = 4 if dtype == F32 else 2
                n = 1
                for s in shape:
                    n *= s
                nbytes = (n * esz + 63) // 64 * 64
                off = self.off
                self.off += nbytes
                assert self.off <= self.limit, ("arena overflow", self.off, self.limit)
                a = arena[:, off // 2: off // 2 + n * esz // 2]
                if dtype == F32:
                    a = a.bitcast(F32)
                if len(shape) == 2:
                    a = a.rearrange("p (a b) -> p a b", a=shape[0])
                elif len(shape) == 3:
                    a = a.rearrange("p (a b c) -> p a b c", a=shape[0], b=shape[1])
                return a

        fx = Alloc(0, ARENA_BYTES)
        xres = fx(F32, [16, D])
        KTo = fx(BF16, [4, 2048])
        Vo = fx(BF16, [16, 4, 129])
        wp = [fx(BF16, [4096]) for _ in range(2)]
        idb = fx(BF16, [128])
        bones = fx(BF16, [128])
        gcol = fx(F32, [3, 8])
        gfin = fx(F32, [D])
        gsgu = fx(F32, [512])
        gsub = fx(F32, [128])
        gq = fx(F32, [1])
        gk = fx(F32, [1])
        gbias = fx(F32, [16])
        bsT = fx(F32, [8])
        wsT = fx(BF16, [8, 128])
        nlam = fx(F32, [1])
        cpos = fx(F32, [4])
        cneg = fx(F32, [4])
        coth = fx(F32, [4])
        epsc = fx(F32, [1])
        parc = fx(F32, [2])
        pss = fx(F32, [16])
        prs = fx(F32, [16])
        strips = {}
        for side in ("neg", "pos"):
            for part in ("hi", "lo"):
                strips[(side, part)] = fx(BF16, [4, 384])
        bstr = {}
        for nm in ("A", "B"):
            for part in ("hi", "lo"):
                bstr[(nm, part)] = fx(BF16, [4, 128])
        RBASE = fx.off
        RLIM = ARENA_BYTES

        def psbank(i):
            return PS[i // 2][:, (i % 2) * 512:(i % 2) * 512 + 512]

        def psbank_bf(i):
            return psbank(i).bitcast(BF16)

        tmp = Alloc(RBASE, RLIM)
        t_rb = tmp(F32, [4])
        t_ones = tmp(F32, [128])
        t_rbB = tmp(F32, [4, 128])
        t_oh = tmp(F32, [512])
        t_g = tmp(F32, [4, 512])
        t_T = tmp(F32, [4, 384])
        t_Ts = tmp(F32, [384])
        t_hi = tmp(BF16, [384])
        t_w = tmp(F32, [8, 128])
        t_wb = tmp(BF16, [8, 128])
        t_l = tmp(F32, [4, 64])
        t_lp = tmp(F32, [2, 64])
        t_ls = tmp(F32, [2])
        t_le = tmp(F32, [2])

        A = P.add
        A("pool", lambda e: e.dma_start(out=idb, in_=H["ident"].ap()), writes=["idb"], dma_ch="c0")
        A("pool", lambda e: e.dma_start(out=bones, in_=H["bones"].ap()), writes=["bones"], dma_ch="c1")
        for i, nm in enumerate(("ffn1_norm", "mix_norm", "ffn2_norm")):
            A("sp", lambda e, i=i, nm=nm: e.dma_start(
                out=gcol[:, i, :], in_=H[nm].ap().rearrange("o (k p) -> p (o k)", p=128),
                allow_slow_non_contiguous=True), writes=["gcol"], dma_ch="c2")
        A("sp", lambda e: e.dma_start(out=gbias, in_=H["gate_bias"].ap().rearrange("o (k p) -> p (o k)", p=128),
                                      allow_slow_non_contiguous=True), writes=["gbias"], dma_ch="c3")
        A("sp", lambda e: e.dma_start(out=bsT, in_=H["sgu_b"].ap().rearrange("g t -> t g"),
                                      allow_slow_non_contiguous=True), writes=["bsT"], dma_ch="c3")
        A("sp", lambda e: e.dma_start(out=gfin, in_=bc(H["final_norm"], D)), writes=["gfin"], dma_ch="c4")
        A("sp", lambda e: e.dma_start(out=gsgu, in_=bc(H["sgu_norm"], 512)), writes=["gsgu"], dma_ch="c4")
        A("sp", lambda e: e.dma_start(out=gsub, in_=bc(H["diff_subln"], 128)), writes=["gsub"], dma_ch="c4")
        A("sp", lambda e: e.dma_start(out=parc, in_=H["par"].ap()), writes=["parc"], dma_ch="c4")
        for half in range(2):
            A("sp", lambda e, half=half: e.dma_start(
                out=gq[half * 64:(half + 1) * 64, :], in_=H["q_norm"].ap().rearrange("o d -> d o")),
                writes=["gq"], dma_ch="c5")
            A("sp", lambda e, half=half: e.dma_start(
                out=gk[half * 64:(half + 1) * 64, :], in_=H["k_norm"].ap().rearrange("o d -> d o")),
                writes=["gk"], dma_ch="c5")
        A("dve", lambda e: e.tensor_scalar(out=gq, in0=gq, scalar1=0.125, scalar2=None, op0=ALU.mult),
          reads=["gq"], writes=["gq"])
        A("dve", lambda e: e.tensor_scalar(out=gsub, in0=gsub, scalar1=1.0 - LAMBDA_INIT, scalar2=None, op0=ALU.mult),
          reads=["gsub"], writes=["gsub"])
        A("dve", lambda e: e.memset(epsc, EPS), writes=["epsc"])
        A("dve", lambda e: e.memset(Vo[:, :, :, 128:129], 1.0), writes=[("V", b) for b in range(16)])
        A("sp", lambda e: e.dma_start(out=cpos, in_=bc(H["rel_bias"], 4, off=31 * 4)), writes=["cpos"], dma_ch="c6")
        A("sp", lambda e: e.dma_start(out=cneg, in_=bc(H["rel_bias"], 4, off=15 * 4)), writes=["cneg"], dma_ch="c6")
        A("dve", lambda e: e.tensor_scalar(out=coth, in0=cpos, scalar1=parc[:, 0:1], scalar2=None, op0=ALU.mult),
          reads=["cpos", "parc"], writes=["coth"])
        A("dve", lambda e: e.scalar_tensor_tensor(out=coth, in0=cneg, scalar=parc[:, 1:2], in1=coth,
                                                  op0=ALU.mult, op1=ALU.add),
          reads=["cneg", "parc", "coth"], writes=["coth"])
        for i, nm in enumerate(("lambda_q1", "lambda_k1", "lambda_q2", "lambda_k2")):
            A("sp", lambda e, i=i, nm=nm: e.dma_start(out=t_l[:, i, :], in_=bc(H[nm], 64)), writes=["t_l"], dma_ch="c7")
        A("dve", lambda e: e.tensor_tensor(out=t_lp[:, 0, :], in0=t_l[:, 0, :], in1=t_l[:, 1, :], op=ALU.mult),
          reads=["t_l"], writes=["t_lp"])
        A("dve", lambda e: e.tensor_tensor(out=t_lp[:, 1, :], in0=t_l[:, 2, :], in1=t_l[:, 3, :], op=ALU.mult),
          reads=["t_l", "t_lp"], writes=["t_lp"])
        A("dve", lambda e: e.reduce_sum(out=t_ls, in_=t_lp, axis=AX.X), reads=["t_lp"], writes=["t_ls"])
        A("act", lambda e: e.activation(out=t_le, in_=t_ls, func=AF.Exp), reads=["t_ls"], writes=["t_le"])
        A("dve", lambda e: e.scalar_tensor_tensor(out=nlam, in0=t_le[:, 1:2], scalar=-LAMBDA_INIT, in1=t_le[:, 0:1],
                                                  op0=ALU.add, op1=ALU.subtract),
          reads=["t_le"], writes=["nlam"])
        A("sp", lambda e: e.dma_start(out=t_w, in_=H["sgu_w"].ap().rearrange("g t s -> t g s")), writes=["t_w"], dma_ch="c8")
        A("dve", lambda e: e.tensor_copy(out=t_wb, in_=t_w), reads=["t_w"], writes=["t_wb"])
        for g in range(8):
            A("pe", lambda e, g=g: e.transpose(out=psbank_bf(0)[:, g * 128:(g + 1) * 128], in_=t_wb[:, g, :], identity=idb),
              reads=["t_wb", "idb"], writes=[("ps", 0)])
        A("dve", lambda e: e.tensor_copy(out=wsT, in_=psbank_bf(0).rearrange("p (g t) -> p g t", g=8)),
          reads=[("ps", 0)], writes=["wsT"])
        A("sp", lambda e: e.dma_start(out=t_rb[0:32, :], in_=H["rel_bias"].ap()), writes=["t_rb"], dma_ch="c9")
        A("sp", lambda e: e.dma_start(out=t_oh[0:32, :], in_=H["oh"].ap()), writes=["t_oh"], dma_ch="c9")
        A("dve", lambda e: e.memset(t_ones, 1.0), writes=["t_ones"])
        for h in range(4):
            A("dve", lambda e, h=h: e.tensor_scalar(out=t_rbB[0:32, h, :], in0=t_ones[0:32, :], scalar1=t_rb[0:32, h:h + 1],
                                                    scalar2=None, op0=ALU.mult),
              reads=["t_ones", "t_rb"], writes=[("t_rbB", h)])
            A("pe", lambda e, h=h: e.matmul(out=psbank(2 + (h % 2)), lhsT=t_rbB[0:32, h, :], rhs=t_oh[0:32, :],
                                            start=True, stop=True),
              reads=[("t_rbB", h), "t_oh"], writes=[("ps", 2 + (h % 2))])
            A("act", lambda e, h=h: e.copy(out=t_g[:, h, :], in_=psbank(2 + (h % 2))),
              reads=[("ps", 2 + (h % 2))], writes=[("t_g", h)])
            A("sp", lambda e, h=h: e.dma_start(out=gscr.ap()[h], in_=t_g[:, h, :]), reads=[("t_g", h)],
              writes=[("gscr", h)], dma_ch="c10")
            A("sp", lambda e, h=h: e.dma_start(out=t_T[:, h, :],
                                               in_=bass.AP(gscr, h * 128 * 512 + 127, [[511, 128], [1, 384]])),
              reads=[("gscr", h)], writes=[("t_T", h)], dma_ch="c11")

            def mk_strip(src, ccol, mask, dst_hi, dst_lo, h=h):
                n = src.shape[-1]
                A("dve", lambda e: e.tensor_scalar(out=t_Ts[:, 0:n], in0=src, scalar1=ccol, scalar2=None, op0=ALU.subtract),
                  reads=[("t_T", h), "cpos", "cneg"], writes=["t_Ts"])
                if mask is not None:
                    A("dve", lambda e: e.tensor_scalar(out=t_Ts[:, 0:n], in0=t_Ts[:, 0:n], scalar1=mask, scalar2=None,
                                                       op0=ALU.mult),
                      reads=["t_Ts", "parc"], writes=["t_Ts"])
                A("dve", lambda e: e.tensor_copy(out=dst_hi, in_=t_Ts[:, 0:n]), reads=["t_Ts"], writes=["strips", "t_hi"])
                A("dve", lambda e: e.tensor_tensor(out=dst_lo, in0=t_Ts[:, 0:n], in1=dst_hi, op=ALU.subtract),
                  reads=["t_Ts", "strips", "t_hi"], writes=["strips"])

            mk_strip(t_T[:, h, :], cneg[:, h:h + 1], None, strips[("neg", "hi")][:, h, :], strips[("neg", "lo")][:, h, :])
            mk_strip(t_T[:, h, :], cpos[:, h:h + 1], None, strips[("pos", "hi")][:, h, :], strips[("pos", "lo")][:, h, :])
            mk_strip(t_T[:, h, 0:128], cpos[:, h:h + 1], parc[:, 0:1], bstr[("A", "hi")][:, h, :], bstr[("A", "lo")][:, h, :])
            mk_strip(t_T[:, h, 256:384], cneg[:, h:h + 1], parc[:, 1:2], bstr[("B", "hi")][:, h, :], bstr[("B", "lo")][:, h, :])

        bar_t = fx(F32, [1]) if False else epsc

        def barrier():
            P.barrier("dve", lambda e: e.memset(epsc, EPS))

        if (debug or "full") == "full":
            v0 = H["xp"].ap()[0].rearrange("(b p) d -> p b d", p=128)
            for b4 in range(4):
                A("sp", lambda e, b4=b4: e.dma_start(out=xres[:, b4 * 4:(b4 + 1) * 4, :], in_=v0[:, b4 * 4:(b4 + 1) * 4, :]),
                  writes=[("xr", b) for b in range(b4 * 4, b4 * 4 + 4)], dma_ch=("x", b4))
        barrier()

        fv = Alloc(RBASE, RLIM)
        xnT = fv(BF16, [8, 2048])
        actT = fv(BF16, [4, 2048])
        wo = fv(BF16, [4, D])
        f_sg = [fv(F32, [512]) for _ in range(2)]
        f_xnb = [fv(BF16, [D]) for _ in range(2)]
        f_junk = fv(BF16, [D])
        f_ss = fv(F32, [16])
        f_rs = fv(F32, [16])
        print("ffn view end", fv.off, RLIM)
        _al = Alloc(RBASE + 32768, RBASE + 32768 + 16384)
        f_sq = [_al(BF16, [512]) for _ in range(2)]
        f_krs = [_al(F32, [512]) for _ in range(2)]

        tv = Alloc(RBASE, RLIM)
        hT = tv(BF16, [8, 512])
        QT = tv(BF16, [4, 512])
        PT2 = [tv(BF16, [1024]) for _ in range(2)]
        PT = [PT2[i // 2][:, (i % 2) * 512:(i % 2) * 512 + 512] for i in range(4)]
        kring = [tv(BF16, [1024]) for _ in range(2)]
        vring = [tv(BF16, [8, 129]) for _ in range(2)]
        t_gg = tv(F32, [2 * D])
        t_ga = t_gg[:, 0:D]
        t_gu = t_gg[:, D:2 * D]
        o_sb = t_gg[:, 0:1032].rearrange("p (c q d) -> p c q d", c=2, q=4)
        t_o0 = t_gg[:, 1032:1544].rearrange("p (q d) -> p q d", q=4)
        t_junk = tv(BF16, [D])
        t_vn2 = [tv(BF16, [512]) for _ in range(2)]
        t_s1 = t_ga[:, 0:512]
        t_A2pair = tv(BF16, [1024])
        t_A2 = [t_A2pair[:, 0:512], t_A2pair[:, 512:1024]]
        t_xnb_b = t_A2pair
        AT = tv(BF16, [4, 512])
        t_r8 = tv(F32, [8])
        obn = tv(BF16, [4, 512])
        boT = tv(BF16, [4, 512])
        t_sa = tv(F32, [512])
        t_sb = tv(F32, [512])
        t_m1 = t_sa
        t_m2 = t_sb
        mT = tv(BF16, [8, 512])
        t_sq2 = [tv(BF16, [512]), t_A2[1]]
        t_rs2 = [tv(F32, [512]), t_sb]
        q_sqres = [("qsq", 0), ("A", 1)]
        q_rsres = [("qrs", 0), "sb"]
        t_xnb = tv(BF16, [D])
        t_ss = tv(F32, [16])
        t_r = tv(F32, [16])

        print("tail view end", tv.off, RLIM)
        cnt = {"x": 0, "wp": 0, "ps": 0, "o": 0}

        def load_x(src_rows, part="both"):
            v = src_rows.rearrange("(b p) d -> p b d", p=128)
            if part != "sq":
                for b4 in range(4):
                    A("sp", lambda e, b4=b4: e.dma_start(out=xres[:, b4 * 4:(b4 + 1) * 4, :], in_=v[:, b4 * 4:(b4 + 1) * 4, :]),
                      writes=[("xr", b) for b in range(b4 * 4, b4 * 4 + 4)], dma_ch=("x", b4))
            if part != "dma":
                for b4 in range(4):
                    sq_blocks(list(range(b4 * 4, b4 * 4 + 4)), f_junk)

        def sq_blocks(blks, junk):
            c = blks[0] // 4
            A("dve", lambda e: e.memset(pss[:, blks[0]:blks[-1] + 1], 0.0), writes=[("pss", c)])
            for b in blks:
                A("act", lambda e, b=b: e.activation(out=junk, in_=xres[:, b, :], func=AF.Square, accum_out=pss[:, b:b + 1]),
                  reads=[("xr", b), ("pss", c)], writes=["njunk", ("pss", c)])

        def norm_to_T(blks, gi, dstT, xnb, tbank, col0=0, reuse_rs=False):
            c = blks[0] // 4
            lo, hi = blks[0], blks[-1] + 1
            if not reuse_rs:
                A("act", lambda e: e.activation(out=prs[:, lo:hi], in_=pss[:, lo:hi], func=AF.Ln, scale=1.0 / D, bias=epsc),
                  reads=[("pss", c), "epsc"], writes=[("prs", c)])
                A("act", lambda e: e.activation(out=prs[:, lo:hi], in_=prs[:, lo:hi], func=AF.Exp, scale=-0.5),
                  reads=[("prs", c)], writes=[("prs", c)])
            for i, b in enumerate(blks):
                xb = xnb[i % len(xnb)]
                rn = ("nxnb", i % len(xnb))
                A("act", lambda e, b=b, xb=xb: e.mul(out=xb, in_=xres[:, b, :], mul=prs[:, b:b + 1]),
                  reads=[("xr", b), ("prs", c)], writes=[rn])
                bank = tbank[i % len(tbank)]
                for k in range(8):
                    A("pe", lambda e, k=k, xb=xb, bank=bank: e.transpose(
                        out=psbank_bf(bank)[:, k * 128:(k + 1) * 128], in_=xb[:, k * 128:(k + 1) * 128], identity=idb),
                      reads=[rn, "idb"], writes=[("ps", bank)])
                cc = col0 + i * 128
                A("dve", lambda e, bank=bank, cc=cc: e.tensor_tensor(
                    out=dstT[:, :, cc:cc + 128], in0=psbank_bf(bank).rearrange("p (k t) -> p k t", k=8),
                    in1=gcol[:, gi, :].unsqueeze(2).to_broadcast([128, 8, 128]), op=ALU.mult),
                  reads=[("ps", bank), "gcol"], writes=[("xT", (col0 // 128) + i)])

        def wload(dst, src, res, chname):
            war = [("wpR", res[1])] if isinstance(res, tuple) else []
            A("pool", lambda e: e.dma_start(out=dst, in_=src), writes=[res], dma_ch=chname, war=war)

        def ffn(w_in_h, w_out_h, gi):
            win = w_in_h.ap().rearrange("(k p) c -> p k c", p=128)
            wout = w_out_h.ap().rearrange("(k p) c -> p k c", p=128)
            for ch in range(4):
                norm_to_T(list(range(4 * ch, 4 * ch + 4)), gi, xnT, f_xnb, [6, 7], col0=512 * ch)
            groups = [(0, 4), (4, 4), (8, 4), (12, 4), (16, 4), (20, 2)]
            for (c0, n) in groups:
                for pc in range(n // 2):
                    wi = cnt["wp"] % 2
                    cnt["wp"] += 1
                    w = wp[wi].rearrange("p (k c) -> p k c", k=8)
                    cc = (c0 + 2 * pc) * 128
                    wload(w[:, :, 0:256], win[:, :, cc:cc + 256], ("wpa", wi), ("wpa", wi))
                    wload(w[:, :, 256:512], win[:, :, DFF + cc:DFF + cc + 256], ("wpb", wi), ("wpb", wi))
                    if pc == n // 2 - 1:
                        wload(wo[:, 0:n, :], wout[:, c0:c0 + n, :], "wo", "wo")
                    for pr in range(2):
                        slot = 2 * pc + pr
                        for t in range(4):
                            pi = cnt["ps"] % 2
                            cnt["ps"] += 1
                            bg, bu = 2 * pi, 2 * pi + 1
                            for k in range(8):
                                A("pe", lambda e, k=k, w=w, pr=pr, t=t, bg=bg: e.matmul(
                                    out=psbank(bg), lhsT=w[:, k, pr * 128:(pr + 1) * 128],
                                    rhs=xnT[:, k, t * 512:(t + 1) * 512], start=(k == 0), stop=(k == 7)),
                                  reads=[("wpa", wi), ("wpR", wi)] + [("xT", 4 * t + j) for j in range(4)], writes=[("ps", bg)])
                            for k in range(8):
                                A("pe", lambda e, k=k, w=w, pr=pr, t=t, bu=bu: e.matmul(
                                    out=psbank(bu), lhsT=w[:, k, 256 + pr * 128:256 + (pr + 1) * 128],
                                    rhs=xnT[:, k, t * 512:(t + 1) * 512], start=(k == 0), stop=(k == 7)),
                                  reads=[("wpb", wi), ("wpR", wi)] + [("xT", 4 * t + j) for j in range(4)], writes=[("ps", bu)])
                            sg = f_sg[pi]
                            A("act", lambda e, sg=sg, bg=bg: e.activation(out=sg, in_=psbank(bg), func=AF.Silu),
                              reads=[("ps", bg)], writes=[("sg", pi)])
                            A("dve", lambda e, sg=sg, bu=bu, slot=slot, t=t: e.tensor_tensor(
                                out=actT[:, slot, t * 512:(t + 1) * 512], in0=psbank(bu), in1=sg, op=ALU.mult),
                              reads=[("ps", bu), ("sg", pi)], writes=[("act", slot, t)])
                for b in range(16):
                    yb = 2 + (cnt["o"] % 2)
                    cnt["o"] += 1
                    for half in range(2):
                        for s in range(n):
                            A("pe", lambda e, b=b, half=half, s=s, yb=yb, n=n: e.matmul(
                                out=PS[yb][:, half * 512:(half + 1) * 512], lhsT=actT[:, s, b * 128:(b + 1) * 128],
                                rhs=wo[:, s, half * 512:(half + 1) * 512], start=(s == 0), stop=(s == n - 1)),
                              reads=[("act", s, b // 4), "wo"], writes=[("ps", 2 * yb), ("ps", 2 * yb + 1)])
                    A("dve", lambda e, b=b, yb=yb: e.scalar_tensor_tensor(
                        out=xres[:, b, :], in0=PS[yb][:, :], scalar=0.5, in1=xres[:, b, :], op0=ALU.mult, op1=ALU.add),
                      reads=[("ps", 2 * yb), ("ps", 2 * yb + 1), ("xr", b)], writes=[("xr", b)])
                    if c0 == 20 and b % 4 == 3:
                        sq_blocks(list(range(b - 3, b + 1)), f_junk)

        wmix = H["w_in"].ap().rearrange("(k p) c -> p k c", p=128)

        def qk_pre(ps_i, sq, sq_res):
            A("act", lambda e: e.activation(out=sq, in_=psbank(ps_i), func=AF.Square), reads=[("ps", ps_i)], writes=[sq_res])

        def qk_post(ps_i, ss_i, gcolq, sq, rs, dst, dst_res, rd_extra, sq_res, rs_res):
            A("pe", lambda e: e.matmul(out=psbank(ss_i), lhsT=bones, rhs=sq, start=True, stop=True),
              reads=[sq_res, "bones"], writes=[("ps", ss_i)])
            A("act", lambda e: e.activation(out=rs, in_=psbank(ss_i), func=AF.Ln, bias=epsc), reads=[("ps", ss_i), "epsc"],
              writes=[rs_res])
            A("act", lambda e: e.activation(out=rs, in_=rs, func=AF.Exp, scale=-0.5), reads=[rs_res], writes=[rs_res])
            A("dve", lambda e: e.scalar_tensor_tensor(out=dst, in0=psbank(ps_i), scalar=gcolq, in1=rs, op0=ALU.mult,
                                                      op1=ALU.mult),
              reads=[("ps", ps_i), rs_res] + rd_extra, writes=[dst_res])

        def mix_kv():
            for ch in range(4):
                norm_to_T(list(range(4 * ch, 4 * ch + 4)), 1, xnT, f_xnb, [6, 7], col0=512 * ch)
            wk = wp[0].rearrange("p (k c) -> p k c", k=8)
            wv = wp[1].rearrange("p (k c) -> p k c", k=8)
            wload(wk, wmix[:, :, 1536:2048], ("wpa", 0), ("wpa", 0))
            wload(wv, wmix[:, :, 2048:2560], ("wpa", 1), ("wpa", 1))
            items = [(h, t) for h in range(4) for t in range(4)]

            def k_mm(i):
                h, t = items[i]
                pb = i % 4
                for k in range(8):
                    A("pe", lambda e, k=k, h=h, t=t, pb=pb: e.matmul(
                        out=psbank(pb), lhsT=wk[:, k, h * 128:(h + 1) * 128], rhs=xnT[:, k, t * 512:(t + 1) * 512],
                        start=(k == 0), stop=(k == 7)),
                      reads=[("wpa", 0), ("wpR", 0)] + [("xT", 4 * t + j) for j in range(4)], writes=[("ps", pb)])
                qk_pre(pb, f_sq[i % 2], ("qsq", i % 2))

            def k_post(i):
                h, t = items[i]
                qk_post(i % 4, 4 + (i % 2), gk[:, 0:1], f_sq[i % 2], f_krs[i % 2], KTo[:, h, t * 512:(t + 1) * 512],
                        ("KT", h, t), ["gk"], ("qsq", i % 2), ("qrs", i % 2))

            k_mm(0)
            for i in range(16):
                if i + 1 < 16:
                    k_mm(i + 1)
                k_post(i)
            for b in range(16):
                pb = 6 + (b % 2)
                for k in range(8):
                    A("pe", lambda e, k=k, b=b, pb=pb: e.matmul(
                        out=psbank(pb), lhsT=xnT[:, k, b * 128:(b + 1) * 128], rhs=wv[:, k, :], start=(k == 0), stop=(k == 7)),
                      reads=[("wpa", 1), ("wpR", 1), ("xT", b)], writes=[("ps", pb)])
                A("act", lambda e, b=b, pb=pb: e.copy(out=Vo[:, b, :, 0:128], in_=psbank(pb).rearrange("p (h d) -> p h d", h=4)),
                  reads=[("ps", pb)], writes=[("V", b)])

        def spill_kv():
            A("sp", lambda e: e.dma_start(out=ktx.ap(), in_=KTo.rearrange("p h t -> p (h t)")),
              reads=[("KT", h, t) for h in range(4) for t in range(4)], writes=["ktx"], dma_ch="sk")
            A("sp", lambda e: e.dma_start(out=vx.ap(), in_=Vo.rearrange("p b h d -> p (b h d)")),
              reads=[("V", b) for b in range(16)], writes=["vx"], dma_ch="sv")

        ktxv = ktx.ap().rearrange("p (h t) -> p h t", h=4)
        vxv = vx.ap().rearrange("p (b h d) -> p b h d", b=16, h=4)

        tail_no = [0]

        def tload(pid, wi, loads):
            subs = [("wpa", wi), ("wpb", wi), ("wpc", wi), ("wpd", wi)]
            scr = wscr.ap()[:, pid * 4096:(pid + 1) * 4096]
            if tail_no[0] == 0:
                for (dst, src, sub) in loads:
                    A("pool", lambda e, dst=dst, src=src: e.dma_start(out=dst, in_=src), writes=[(sub, wi)],
                      dma_ch=(sub, wi), war=[("wpR", wi)])
                A("sp", lambda e: e.dma_start(out=scr, in_=wp[wi]), reads=[(sub, wi) for (_, _, sub) in loads],
                  writes=[("wscr", pid)], dma_ch=("wsw", wi))
            else:
                A("sp", lambda e: e.dma_start(out=wp[wi], in_=scr), reads=[("wscr", pid)], writes=subs,
                  dma_ch=("wsr", wi), war=[("wpR", wi)])

        def tail(qt, has_oth):
            blks = [4 * qt + j for j in range(4)]
            norm_to_T(blks, 1, hT, [t_xnb, t_xnb_b], [7, 6], reuse_rs=True)
            hres = [("xT", j) for j in range(4)]
            wq = wp[cnt["wp"] % 2].rearrange("p (k c) -> p k c", k=8)
            wqi = cnt["wp"] % 2
            cnt["wp"] += 1
            tload(0, wqi, [(wq, wmix[:, :, 1024:1536], "wpa")])
            def q_mm(h):
                for k in range(8):
                    A("pe", lambda e, k=k, h=h: e.matmul(out=psbank(h), lhsT=wq[:, k, h * 128:(h + 1) * 128],
                                                          rhs=hT[:, k, :], start=(k == 0), stop=(k == 7)),
                      reads=[("wpa", wqi), ("wpR", wqi)] + hres, writes=[("ps", h)])
                qk_pre(h, t_sq2[h % 2], q_sqres[h % 2])

            def q_post(h):
                qk_post(h, 4 + (h % 2), gq[:, 0:1], t_sq2[h % 2], t_rs2[h % 2], QT[:, h, :], ("QT", h), ["gq"],
                        q_sqres[h % 2], q_rsres[h % 2])

            q_mm(0)
            for h in range(4):
                if h + 1 < 4:
                    q_mm(h + 1)
                q_post(h)
            wui = cnt["wp"] % 2
            cnt["wp"] += 1
            wu = wp[wui].rearrange("p (k c) -> p k c", k=8)
            tload(1, wui, [(wu, wmix[:, :, 0:512], "wpa")])
            wvi = cnt["wp"] % 2
            cnt["wp"] += 1
            wva = wp[wvi].rearrange("p (k c) -> p k c", k=8)
            tload(2, wvi, [(wva, wmix[:, :, 512:1024], "wpa")])
            def t3_proj(j):
                p = j % 2
                uvp = PS[p]
                ub0 = 2 * p
                gu = t_gg[:, p * D:(p + 1) * D]
                vn = t_vn2[p]
                Ab = t_A2[p]
                for k in range(8):
                    A("pe", lambda e, k=k, j=j, uvp=uvp: e.matmul(out=uvp[:, 0:512], lhsT=hT[:, k, j * 128:(j + 1) * 128],
                                                                   rhs=wu[:, k, :], start=(k == 0), stop=(k == 7)),
                      reads=[("wpa", wui), ("wpR", wui)] + hres, writes=[("ps", ub0)])
                for k in range(8):
                    A("pe", lambda e, k=k, j=j, uvp=uvp: e.matmul(out=uvp[:, 512:1024], lhsT=hT[:, k, j * 128:(j + 1) * 128],
                                                                   rhs=wva[:, k, :], start=(k == 0), stop=(k == 7)),
                      reads=[("wpa", wvi), ("wpR", wvi)] + hres, writes=[("ps", ub0 + 1)])
                uvr = [("ps", ub0), ("ps", ub0 + 1)]
                gr = "ga" if p == 0 else "gu"
                A("act", lambda e, uvp=uvp, gu=gu: e.activation(out=gu, in_=uvp[:, :], func=AF.Gelu_apprx_tanh), reads=uvr,
                  writes=[gr])
                A("dve", lambda e, p=p: e.memset(t_ss[:, 8 + p:9 + p], 0.0), writes=[("vss", p)])
                A("act", lambda e, p=p, gu=gu: e.activation(out=t_junk[:, 0:512], in_=gu[:, 512:1024], func=AF.Square,
                                                            accum_out=t_ss[:, 8 + p:9 + p]), reads=[gr, ("vss", p)],
                  writes=[("vss", p), "vjunk"])
                A("act", lambda e, p=p: e.activation(out=t_r[:, 10 + p:11 + p], in_=t_ss[:, 8 + p:9 + p], func=AF.Ln,
                                                     scale=1.0 / 512, bias=epsc), reads=[("vss", p), "epsc"], writes=[("vr", p)])
                A("act", lambda e, p=p: e.activation(out=t_r[:, 10 + p:11 + p], in_=t_r[:, 10 + p:11 + p], func=AF.Exp,
                                                     scale=-0.5), reads=[("vr", p)], writes=[("vr", p)])
                A("dve", lambda e, p=p, gu=gu, vn=vn: e.tensor_scalar(out=vn, in0=gu[:, 512:1024], scalar1=t_r[:, 10 + p:11 + p],
                                                                     scalar2=None, op0=ALU.mult), reads=[gr, ("vr", p)],
                  writes=[("vn", p)])

            def t3_post(j):
                p = j % 2
                uvp = PS[p]
                ub0 = 2 * p
                gu = t_gg[:, p * D:(p + 1) * D]
                vn = t_vn2[p]
                Ab = t_A2[p]
                gr = "ga" if p == 0 else "gu"
                for g in range(8):
                    A("pe", lambda e, g=g, vn=vn: e.matmul(out=psbank(6)[:, g * 64:(g + 1) * 64], lhsT=wsT[:, g, :],
                                                           rhs=vn[:, g * 64:(g + 1) * 64], start=True, stop=True),
                      reads=[("vn", p), "wsT"], writes=[("ps", 6)])
                s1 = gu[:, 512:1024]
                A("dve", lambda e, s1=s1: e.tensor_tensor(out=s1, in0=psbank(6), in1=gsgu, op=ALU.mult),
                  reads=[("ps", 6), "gsgu", gr], writes=[gr])
                A("dve", lambda e, s1=s1: e.tensor_tensor(out=s1.rearrange("p (g d) -> p g d", g=8),
                                                          in0=s1.rearrange("p (g d) -> p g d", g=8),
                                                          in1=bsT.unsqueeze(2).to_broadcast([128, 8, 64]), op=ALU.add),
                  reads=[gr, "bsT"], writes=[gr])
                A("dve", lambda e, s1=s1, gu=gu, Ab=Ab: e.tensor_tensor(out=Ab, in0=s1, in1=gu[:, 0:512], op=ALU.mult),
                  reads=[gr], writes=[("A", p)])

            def t3_post_b(j):
                p = j % 2
                Ab = t_A2[p]
                for c in range(4):
                    A("pe", lambda e, c=c, Ab=Ab: e.transpose(out=psbank_bf(7)[:, c * 128:(c + 1) * 128],
                                                              in_=Ab[:, c * 128:(c + 1) * 128], identity=idb),
                      reads=[("A", p), "idb"], writes=[("ps", 7)])
                A("act", lambda e, j=j: e.copy(out=AT[:, :, j * 128:(j + 1) * 128],
                                               in_=psbank_bf(7)[:, 0:512].rearrange("p (c t) -> p c t", c=4)),
                  reads=[("ps", 7)], writes=[("AT", j)])

            t3_proj(0)
            t3_proj(1)
            t3_post(0)
            t3_proj(2)
            t3_post_b(0)
            t3_post(1)
            t3_proj(3)
            t3_post_b(1)
            t3_post(2)
            t3_post_b(2)
            t3_post(3)
            t3_post_b(3)
            sctr = [0]
            pending = [None]

            def combine(h):
                o0 = o_sb[:, 0, :, 0:128]
                o1 = o_sb[:, 1, :, 0:128]
                gg = ["ga", "gu"]
                A("dve", lambda e: e.reciprocal(out=t_r8.rearrange("p (c q) -> p c q", c=2), in_=o_sb[:, :, :, 128]),
                  reads=gg, writes=["r8"])
                A("dve", lambda e: e.tensor_scalar(out=t_r8[:, 4:8], in0=t_r8[:, 4:8], scalar1=nlam[:, 0:1], scalar2=None,
                                                   op0=ALU.mult), reads=["r8", "nlam"], writes=["r8"])
                A("dve", lambda e: e.tensor_tensor(out=t_o0, in0=o0, in1=t_r8[:, 0:4].unsqueeze(2).to_broadcast([128, 4, 128]),
                                                   op=ALU.mult), reads=gg + ["r8"], writes=gg)
                A("dve", lambda e: e.tensor_tensor(out=o1, in0=o1, in1=t_r8[:, 4:8].unsqueeze(2).to_broadcast([128, 4, 128]),
                                                   op=ALU.mult), reads=gg + ["r8"], writes=gg)
                A("dve", lambda e: e.tensor_tensor(out=t_o0, in0=t_o0, in1=o1, op=ALU.add), reads=gg, writes=gg)
                A("dve", lambda e: e.tensor_tensor(out=o1, in0=t_o0, in1=t_o0, op=ALU.mult), reads=gg, writes=gg)
                A("dve", lambda e: e.reduce_sum(out=t_ss[:, 12:16], in_=o1, axis=AX.X), reads=gg, writes=["oss"])
                A("act", lambda e: e.activation(out=t_r[:, 4:8], in_=t_ss[:, 12:16], func=AF.Ln, scale=1.0 / 128, bias=epsc),
                  reads=["oss", "epsc"], writes=["orr"])
                A("act", lambda e: e.activation(out=t_r[:, 4:8], in_=t_r[:, 4:8], func=AF.Exp, scale=-0.5), reads=["orr"],
                  writes=["orr"])
                A("dve", lambda e: e.tensor_tensor(out=t_o0, in0=t_o0, in1=t_r[:, 4:8].unsqueeze(2).to_broadcast([128, 4, 128]),
                                                   op=ALU.mult), reads=gg + ["orr"], writes=gg)
                A("dve", lambda e, h=h: e.tensor_tensor(out=obn[:, :, h * 128:(h + 1) * 128], in0=t_o0,
                                                        in1=gsub.unsqueeze(1).to_broadcast([128, 4, 128]), op=ALU.mult),
                  reads=gg + ["gsub"], writes=[("obn", h)])

            def bo_transpose(h):
                for qb in range(4):
                    A("pe", lambda e, qb=qb, h=h: e.transpose(out=psbank_bf(7)[:, qb * 128:(qb + 1) * 128],
                                                              in_=obn[:, qb, h * 128:(h + 1) * 128], identity=idb),
                      reads=[("obn", h), "idb"], writes=[("ps", 7)])
                A("dve", lambda e, h=h: e.tensor_copy(out=boT[:, h, :], in_=psbank_bf(7)[:, 0:512]),
                  reads=[("ps", 7)], writes=[("boT", h)])

            for h in range(4):
                kbs = [("own", kb) for kb in range(16)]
                if has_oth:
                    kbs += [("oth", kb) for kb in range(16)]
                nk = len(kbs)
                st_info = {}

                def stage_qk(ki, h=h, kbs=kbs, st_info=st_info):
                    src, kb = kbs[ki]
                    if src == "oth" and kb % 8 == 0:
                        ri = (kb // 8) % 2
                        A("sp", lambda e, h=h, kb=kb, ri=ri: e.dma_start(out=kring[ri], in_=ktxv[:, h, kb * 128:(kb + 8) * 128]),
                          reads=["ktx"], writes=[("kr", ri)], dma_ch=("kr", ri))
                        A("sp", lambda e, h=h, kb=kb, ri=ri: e.dma_start(out=vring[ri], in_=vxv[:, kb:kb + 8, h, :]),
                          reads=["vx"], writes=[("vr", ri)], dma_ch=("vr", ri))
                    if src == "own":
                        d = kb - 4 * qt
                        side = "neg" if d <= 1 else "pos"
                        bcol = (cneg if d <= 1 else cpos)[:, h:h + 1]
                        bres = "cneg" if d <= 1 else "cpos"
                        i0, i1 = max(0, d - 1), min(3, d + 1)
                        band = None
                        if i0 <= i1:
                            sc0 = 128 * (1 - (d - i0))
                            nb = 128 * (i1 - i0 + 1)
                            band = (128 * i0, nb, strips[(side, "hi")][:, h, sc0:sc0 + nb], strips[(side, "lo")][:, h, sc0:sc0 + nb])
                        kres = [("KT", h, kb // 4)]
                        vres = [("V", kb)]

                        def kt_ap(c, h=h, kb=kb):
                            return KTo[c * 64:(c + 1) * 64, h, kb * 128:(kb + 1) * 128]

                        v_ap = Vo[:, kb, h, :]
                    else:
                        ri = (kb // 8) % 2
                        bcol = coth[:, h:h + 1]
                        bres = "coth"
                        band = None
                        if kb == 0 and qt == 3:
                            band = (384, 128, bstr[("A", "hi")][:, h, :], bstr[("A", "lo")][:, h, :])
                        if kb == 15 and qt == 0:
                            band = (0, 128, bstr[("B", "hi")][:, h, :], bstr[("B", "lo")][:, h, :])
                        kres = [("kr", ri)]
                        vres = [("vr", ri)]

                        def kt_ap(c, ri=ri, kb=kb):
                            return kring[ri][c * 64:(c + 1) * 64, (kb % 8) * 128:(kb % 8 + 1) * 128]

                        v_ap = vring[ri][:, kb % 8, :]
                    pts = []
                    pi2 = sctr[0] % 2
                    sctr[0] += 1
                    for c in range(2):
                        sb = 2 * pi2 + c
                        A("pe", lambda e, c=c, sb=sb, kt_ap=kt_ap, h=h, band=band: e.matmul(
                            out=psbank(sb), lhsT=kt_ap(c), rhs=QT[c * 64:(c + 1) * 64, h, :], start=True, stop=(band is None)),
                          reads=kres + [("QT", h)], writes=[("ps", sb)])
                        if band is not None:
                            c0, nb, shi, slo = band
                            A("pe", lambda e, sb=sb, c0=c0, nb=nb, shi=shi: e.matmul(
                                out=psbank(sb)[:, c0:c0 + nb], lhsT=idb, rhs=shi, start=False, stop=False),
                              reads=["idb", "strips"], writes=[("ps", sb)])
                            A("pe", lambda e, sb=sb, c0=c0, nb=nb, slo=slo: e.matmul(
                                out=psbank(sb)[:, c0:c0 + nb], lhsT=idb, rhs=slo, start=False, stop=True),
                              reads=["idb", "strips"], writes=[("ps", sb)])
                        pts.append(sb)
                    A("act", lambda e, pi2=pi2, bcol=bcol: e.activation(out=PT2[pi2], in_=PS[pi2][:, :], func=AF.Exp, bias=bcol),
                      reads=[("ps", 2 * pi2), ("ps", 2 * pi2 + 1), bres], writes=[("PT", 2 * pi2), ("PT", 2 * pi2 + 1)])
                    st_info[ki] = (pts, v_ap, vres)

                def stage_pv(ki, nk=nk, st_info=st_info):
                    pts, v_ap, vres = st_info[ki]
                    for c in range(2):
                        for qb in range(4):
                            idx = c * 4 + qb
                            ob_bank = 4 + idx // 3
                            oc = (idx % 3) * 129
                            A("pe", lambda e, qb=qb, ob_bank=ob_bank, oc=oc, v_ap=v_ap, sb=pts[c], ki=ki, nk=nk, idx=idx: e.matmul(
                                out=psbank(ob_bank)[:, oc:oc + 129], lhsT=PT[sb][:, qb * 128:(qb + 1) * 128], rhs=v_ap,
                                start=(ki == 0 and idx % 3 == 0), stop=(ki == nk - 1), skip_group_check=True),
                              reads=[("PT", pts[c])] + vres, writes=[("ps", ob_bank)])

                stage_qk(0)
                for ki in range(nk):
                    if ki + 1 < nk:
                        stage_qk(ki + 1)
                    stage_pv(ki)
                    if ki == 3 and pending[0] is not None:
                        combine(pending[0])
                    if ki == 10 and pending[0] is not None:
                        bo_transpose(pending[0])
                        pending[0] = None
                for bi, ncol in ((0, 387), (1, 387), (2, 258)):
                    A("dve", lambda e, bi=bi, ncol=ncol: e.tensor_copy(out=t_gg[:, bi * 387:bi * 387 + ncol],
                                                                      in_=psbank(4 + bi)[:, 0:ncol]),
                      reads=[("ps", 4 + bi)], writes=["ga", "gu"])
                pending[0] = h
            combine(pending[0])
            bo_transpose(pending[0])
            wpa_v = H["w_proj_a"].ap().rearrange("(k p) c -> p k c", p=128)
            wpb_v = H["w_proj_b"].ap().rearrange("(k p) c -> p k c", p=128)
            atres = [("AT", j) for j in range(4)]
            bores = [("boT", j) for j in range(4)]
            for oc in range(8):
                pb0 = 4 * (oc % 2)
                wi = cnt["wp"] % 2
                cnt["wp"] += 1
                wb = wp[wi]
                w_a = wb[:, 0:512].rearrange("p (k c) -> p k c", k=4)
                w_b = wb[:, 512:1024].rearrange("p (k c) -> p k c", k=4)
                w_ga = wb[:, 1024:2048].rearrange("p (k c) -> p k c", k=8)
                w_gb = wb[:, 2048:3072].rearrange("p (k c) -> p k c", k=8)
                tload(3 + oc, wi, [(w_a, wpa_v[:, :, oc * 128:(oc + 1) * 128], "wpa"),
                                   (w_b, wpb_v[:, :, oc * 128:(oc + 1) * 128], "wpb"),
                                   (w_ga, wmix[:, :, 2560 + oc * 128:2560 + (oc + 1) * 128], "wpc"),
                                   (w_gb, wmix[:, :, 3584 + oc * 128:3584 + (oc + 1) * 128], "wpd")])
                for k in range(4):
                    A("pe", lambda e, k=k, w_a=w_a, pb0=pb0: e.matmul(out=psbank(pb0 + 0), lhsT=w_a[:, k, :], rhs=AT[:, k, :], start=(k == 0),
                                                             stop=(k == 3)), reads=[("wpa", wi), ("wpR", wi)] + atres, writes=[("ps", pb0 + 0)])
                for k in range(4):
                    A("pe", lambda e, k=k, w_b=w_b, pb0=pb0: e.matmul(out=psbank(pb0 + 1), lhsT=w_b[:, k, :], rhs=boT[:, k, :], start=(k == 0),
                                                             stop=(k == 3)), reads=[("wpb", wi), ("wpR", wi)] + bores, writes=[("ps", pb0 + 1)])
                for k in range(8):
                    A("pe", lambda e, k=k, w_ga=w_ga, pb0=pb0: e.matmul(out=psbank(pb0 + 2), lhsT=w_ga[:, k, :], rhs=hT[:, k, :], start=(k == 0),
                                                               stop=(k == 7)), reads=[("wpc", wi), ("wpR", wi)] + hres, writes=[("ps", pb0 + 2)])
                for k in range(8):
                    A("pe", lambda e, k=k, w_gb=w_gb, pb0=pb0: e.matmul(out=psbank(pb0 + 3), lhsT=w_gb[:, k, :], rhs=hT[:, k, :], start=(k == 0),
                                                               stop=(k == 7)), reads=[("wpd", wi), ("wpR", wi)] + hres, writes=[("ps", pb0 + 3)])
                A("act", lambda e, oc=oc, pb0=pb0: e.activation(out=t_sa, in_=psbank(pb0 + 2), func=AF.Sigmoid, bias=gbias[:, oc:oc + 1]),
                  reads=[("ps", pb0 + 2), "gbias"], writes=["sa"])
                A("act", lambda e, oc=oc, pb0=pb0: e.activation(out=t_sb, in_=psbank(pb0 + 3), func=AF.Sigmoid, bias=gbias[:, 8 + oc:9 + oc]),
                  reads=[("ps", pb0 + 3), "gbias"], writes=["sb"])
                A("dve", lambda e, pb0=pb0: e.tensor_tensor(out=t_m1, in0=psbank(pb0 + 0), in1=t_sa, op=ALU.mult), reads=[("ps", pb0 + 0), "sa"],
                  writes=["sa"])
                A("dve", lambda e, pb0=pb0: e.tensor_tensor(out=t_m2, in0=psbank(pb0 + 1), in1=t_sb, op=ALU.mult), reads=[("ps", pb0 + 1), "sb"],
                  writes=["sb"])
                A("dve", lambda e, oc=oc: e.tensor_tensor(out=mT[:, oc, :], in0=t_m1, in1=t_m2, op=ALU.add), reads=["sa", "sb"],
                  writes=[("mT", oc)])
            wo_v = H["w_out"].ap().rearrange("(k p) c -> p k c", p=128)
            mres = [("mT", oc) for oc in range(8)]
            for half in range(2):
                wi = cnt["wp"] % 2
                cnt["wp"] += 1
                w = wp[wi].rearrange("p (k c) -> p k c", k=8)
                tload(11 + half, wi, [(w, wo_v[:, :, half * 512:(half + 1) * 512], "wpa")])
                for j in range(4):
                    pb = 4 + (j % 2)
                    for k in range(8):
                        A("pe", lambda e, k=k, j=j, w=w, pb=pb: e.matmul(out=psbank(pb), lhsT=mT[:, k, j * 128:(j + 1) * 128],
                                                                          rhs=w[:, k, :], start=(k == 0), stop=(k == 7)),
                          reads=[("wpa", wi), ("wpR", wi)] + mres, writes=[("ps", pb)])
                    b = blks[j]
                    A("dve", lambda e, b=b, pb=pb, half=half: e.tensor_tensor(
                        out=xres[:, b, half * 512:(half + 1) * 512], in0=psbank(pb), in1=xres[:, b, half * 512:(half + 1) * 512],
                        op=ALU.add), reads=[("ps", pb), ("xr", b)], writes=[("xr", b)])
            sq_blocks(blks, t_junk)

        def final_store(dst_rows):
            v = dst_rows.rearrange("(b p) d -> p b d", p=128)
            pr = [("pss", c) for c in range(4)]
            A("act", lambda e: e.activation(out=prs, in_=pss, func=AF.Ln, scale=1.0 / D, bias=epsc), reads=pr + ["epsc"],
              writes=[("prs", c) for c in range(4)])
            A("act", lambda e: e.activation(out=prs, in_=prs, func=AF.Exp, scale=-0.5), reads=[("prs", c) for c in range(4)],
              writes=[("prs", c) for c in range(4)])
            for b in range(16):
                A("dve", lambda e, b=b: e.scalar_tensor_tensor(out=xres[:, b, :], in0=xres[:, b, :], scalar=prs[:, b:b + 1],
                                                               in1=gfin, op0=ALU.mult, op1=ALU.mult),
                  reads=[("xr", b), ("prs", b // 4), "gfin"], writes=[("xr", b)])
            for b4 in range(4):
                A("sp", lambda e, b4=b4: e.dma_start(out=v[:, b4 * 4:(b4 + 1) * 4, :], in_=xres[:, b4 * 4:(b4 + 1) * 4, :]),
                  reads=[("xr", b) for b in range(b4 * 4, b4 * 4 + 4)], writes=[("OUT", cnt["x"], b4)], dma_ch=("o", b4))
                out_res.append(("OUT", cnt["x"], b4))
            cnt["x"] += 1

        out_res = []

        def unit(src, dst, has_oth, preloaded=False):
            load_x(src, part="sq" if preloaded else "both")
            ffn(H["ffn1_w_in"], H["ffn1_w_out"], 0)
            mix_kv()
            barrier()
            for qt in range(4):
                tail(qt, has_oth)
                tail_no[0] += 1
            barrier()
            ffn(H["ffn2_w_in"], H["ffn2_w_out"], 2)
            final_store(dst)

        mode = debug or "full"
        if mode == "full":
            unit(H["xp"].ap()[0], yp.ap()[0], False, preloaded=True)
            unit(H["xp"].ap()[1], yp.ap()[1], False)
            load_x(H["xsx"].ap())
            ffn(H["ffn1_w_in"], H["ffn1_w_out"], 0)
            mix_kv()
            spill_kv()
            unit(H["xso"].ap(), ys.ap(), True)
        elif mode == "p0":
            unit(H["xp"].ap()[0], yp.ap()[0], False)
        elif mode == "s":
            load_x(H["xsx"].ap())
            ffn(H["ffn1_w_in"], H["ffn1_w_out"], 0)
            mix_kv()
            spill_kv()
            unit(H["xso"].ap(), ys.ap(), True)
        A("sp", None, reads=out_res)
        P.emit(st)
    return nc


def _bucket_table():
    rel = np.arange(-255, 256, dtype=np.int32)
    half = 16
    max_exact = 8
    try:
        import jax
        import jax.numpy as jnp
        with jax.default_device(jax.devices("cpu")[0]):
            r = jnp.asarray(rel)
            bucket = jnp.where(r > 0, half, 0).astype(jnp.int32)
            n = jnp.abs(r)
            nf = jnp.maximum(n, 1).astype(jnp.float32)
            large = max_exact + (jnp.log(nf / max_exact) / math.log(128 / max_exact) * (half - max_exact)).astype(jnp.int32)
            large = jnp.minimum(large, half - 1)
            out = np.asarray(bucket + jnp.where(n < max_exact, n, large))
        return rel, out.astype(np.int64)
    except Exception:
        bucket = np.where(rel > 0, half, 0)
        n = np.abs(rel)
        nf = np.maximum(n, 1).astype(np.float32)
        large = max_exact + (np.log(nf / np.float32(max_exact)) / np.float32(math.log(128 / max_exact))
                             * np.float32(half - max_exact)).astype(np.int32)
        large = np.minimum(large, half - 1)
        return rel, (bucket + np.where(n < max_exact, n, large)).astype(np.int64)


_CACHE = {}


def kernel(**inputs):
    f = lambda k: np.ascontiguousarray(np.asarray(inputs[k], dtype=np.float32))
    xp_all = f("x_prompt")
    xs_all = f("x_sample")
    shared = {}
    for k in ("ffn1_norm", "mix_norm", "ffn2_norm", "final_norm", "gate_bias", "sgu_norm", "q_norm", "k_norm",
              "lambda_q1", "lambda_k1", "lambda_q2", "lambda_k2", "diff_subln"):
        shared[k] = f(k).reshape(1, -1)
    for k in ("ffn1_w_in", "ffn1_w_out", "w_in", "sgu_w", "sgu_b", "w_proj_a", "w_proj_b", "w_out", "ffn2_w_in", "ffn2_w_out"):
        shared[k] = np.ascontiguousarray(f(k)[0])
    shared["rel_bias"] = f("rel_bias")
    shared["ident"] = np.eye(128, dtype=np.float32)
    bo = np.zeros((128, 128), np.float32)
    bo[:64, :64] = 1.0 / 64
    bo[64:, 64:] = 1.0 / 64
    shared["bones"] = bo
    rel, bk = _bucket_table()
    oh = np.zeros((32, 512), np.float32)
    for i in range(511):
        r = 255 - i
        oh[bk[r + 255], i] = 1.0
    shared["oh"] = oh
    in_maps = []
    for c in range(8):
        m = dict(shared)
        m["xp"] = np.ascontiguousarray(xp_all[2 * c:2 * c + 2])
        s = c // 2
        p = c % 2
        m["xso"] = np.ascontiguousarray(xs_all[s, p * 2048:(p + 1) * 2048])
        m["xsx"] = np.ascontiguousarray(xs_all[s, (1 - p) * 2048:(2 - p) * 2048])
        par = np.zeros((128, 2), np.float32)
        par[:, 0] = 1.0 - p
        par[:, 1] = float(p)
        m["par"] = par
        in_maps.append(m)
    if "nc" not in _CACHE:
        _CACHE["nc"] = build_program()
    res = run_bass_kernel_spmd(_CACHE["nc"], in_maps, core_ids=list(range(8)))
    y_prompt = np.empty((16, 2048, D), np.float32)
    y_sample = np.empty((4, 4096, D), np.float32)
    for c in range(8):
        r = res.results[c]
        y_prompt[2 * c:2 * c + 2] = r["yp"]
        s = c // 2
        p = c % 2
        y_sample[s, p * 2048:(p + 1) * 2048] = r["ys"]
    return (y_prompt, y_sample)
```

```python
import math
import numpy as np
import concourse.bass as bass
import concourse.mybir as mybir
from concourse.bass_utils import run_bass_kernel_spmd
from contextlib import ExitStack

F32 = mybir.dt.float32
BF16 = mybir.dt.bfloat16
AF = mybir.ActivationFunctionType
ALU = mybir.AluOpType
AX = mybir.AxisListType

D = 1024
DFF = 2816
NCH = 22
EPS = 1e-6
SQRT_GC = math.sqrt(0.044715)
GELU_S = 2.0 * math.sqrt(2.0 / math.pi)
LAMBDA_INIT = 0.8 - 0.6 * math.exp(0.0)


class Op:
    __slots__ = ("eng", "fn", "deps", "is_dma", "ch", "ch_idx", "signaled", "tick")

    def __init__(self, eng, fn, is_dma=False, ch=None):
        self.eng = eng
        self.fn = fn
        self.deps = []
        self.is_dma = is_dma
        self.ch = ch
        self.ch_idx = 0
        self.signaled = False
        self.tick = 0


class Prog:
    ENGS = ["pe", "act", "dve", "pool", "sp"]

    def __init__(self, nc):
        self.nc = nc
        self.ops = {e: [] for e in self.ENGS}
        self.last_w = {}
        self.readers = {}
        self.ch_ops = {}
        self.cur_barrier = None

    def add(self, eng, fn, reads=(), writes=(), dma_ch=None, war=()):
        op = Op(eng, fn, is_dma=dma_ch is not None, ch=dma_ch)
        deps = set()
        if self.cur_barrier is not None:
            deps.add(self.cur_barrier)
        for r in war:
            rd = self.readers.get(r)
            if rd:
                deps.update(rd.values())
        for r in reads:
            w = self.last_w.get(r)
            if w is not None:
                deps.add(w)
        for r in writes:
            w = self.last_w.get(r)
            if w is not None:
                deps.add(w)
            rd = self.readers.get(r)
            if rd:
                deps.update(rd.values())
        if dma_ch is not None:
            lst = self.ch_ops.setdefault(dma_ch, [])
            if lst:
                deps.add(lst[-1])
            lst.append(op)
            op.ch_idx = len(lst)
        for d in deps:
            if d.is_dma:
                op.deps.append(d)
                continue
            if d.eng == eng and eng == "pe" and not op.is_dma:
                continue
            d.signaled = True
            op.deps.append(d)
        for r in reads:
            key = ("d", dma_ch) if op.is_dma else eng
            self.readers.setdefault(r, {})[key] = op
        for r in writes:
            self.last_w[r] = op
            self.readers[r] = {}
        self.ops[eng].append(op)
        return op

    def barrier(self, eng, fn):
        op = Op(eng, fn)
        for e in self.ENGS:
            for o in reversed(self.ops[e]):
                if not o.is_dma and o.fn is not None:
                    if not (e == eng and eng == "pe"):
                        o.signaled = True
                        op.deps.append(o)
                    break
        for ch, lst in self.ch_ops.items():
            if lst:
                op.deps.append(lst[-1])
        op.signaled = True
        self.ops[eng].append(op)
        self.cur_barrier = op
        self.last_w = {}
        self.readers = {}
        return op

    def emit(self, stack):
        nc = self.nc
        esem = {}
        for e in self.ENGS:
            if e == "sp":
                continue
            esem[e] = stack.enter_context(nc.semaphore("S_" + e))
        chsem = {}
        for i, ch in enumerate(self.ch_ops):
            chsem[ch] = stack.enter_context(nc.semaphore("D%d" % i))
        for e in self.ENGS:
            t = 0
            for op in self.ops[e]:
                if op.signaled and not op.is_dma:
                    t += 1
                    op.tick = t
        block = stack.enter_context(nc.Block())
        ops = self.ops

        def run(e, eng):
            waited = {}
            for op in ops[e]:
                need = {}
                for d in op.deps:
                    if d.is_dma:
                        key = ("d", d.ch)
                        val = 16 * d.ch_idx
                    else:
                        key = ("e", d.eng)
                        val = d.tick
                    if val > need.get(key, 0):
                        need[key] = val
                for key, val in need.items():
                    if val <= waited.get(key, 0):
                        continue
                    waited[key] = val
                    sem = chsem[key[1]] if key[0] == "d" else esem[key[1]]
                    eng.wait_ge(sem, val)
                if op.fn is None:
                    continue
                ins = op.fn(eng)
                if op.is_dma:
                    ins.then_inc(chsem[op.ch], 16)
                elif op.signaled:
                    ins.then_inc(esem[e], 1)

        @block.tensor
        def _(eng):
            run("pe", eng)

        @block.scalar
        def _(eng):
            run("act", eng)

        @block.vector
        def _(eng):
            run("dve", eng)

        @block.gpsimd
        def _(eng):
            run("pool", eng)

        @block.sync
        def _(eng):
            run("sp", eng)


ARENA_BYTES = 210944


def build_program(debug=None):
    nc = bass.Bass("TRN2", target_bir_lowering=False)
    st = ExitStack()

    def din(name, shape):
        return nc.dram_tensor(name, list(shape), F32, kind="ExternalInput")

    H = {}
    for name, shape in [
        ("xp", (2, 2048, D)), ("xso", (2048, D)), ("xsx", (2048, D)), ("par", (128, 2)),
        ("rel_bias", (32, 4)), ("ffn1_norm", (1, D)), ("ffn1_w_in", (D, 2 * DFF)), ("ffn1_w_out", (DFF, D)),
        ("mix_norm", (1, D)), ("w_in", (D, 4608)), ("gate_bias", (1, 2048)), ("sgu_norm", (1, 512)),
        ("sgu_w", (8, 128, 128)), ("sgu_b", (8, 128)), ("q_norm", (1, 64)), ("k_norm", (1, 64)),
        ("lambda_q1", (1, 64)), ("lambda_k1", (1, 64)), ("lambda_q2", (1, 64)), ("lambda_k2", (1, 64)),
        ("diff_subln", (1, 128)), ("w_proj_a", (512, D)), ("w_proj_b", (512, D)), ("w_out", (D, D)),
        ("ffn2_norm", (1, D)), ("ffn2_w_in", (D, 2 * DFF)), ("ffn2_w_out", (DFF, D)), ("final_norm", (1, D)),
        ("ident", (128, 128)), ("bones", (128, 128)), ("oh", (32, 512)),
    ]:
        H[name] = din(name, shape)
    yp = nc.dram_tensor("yp", [2, 2048, D], F32, kind="ExternalOutput")
    ys = nc.dram_tensor("ys", [2048, D], F32, kind="ExternalOutput")
    ktx = nc.dram_tensor("ktx", [128, 4 * 2048], BF16)
    vx = nc.dram_tensor("vx", [128, 16 * 516], BF16)
    gscr = nc.dram_tensor("gscr", [4, 128, 512], F32)
    wscr = nc.dram_tensor("wscr", [128, 13 * 4096], BF16)

    def bc(h, n, off=0):
        return bass.AP(h, off, [[0, 128], [1, n]])

    with st:
        P = Prog(nc)
        arena = st.enter_context(nc.sbuf_tensor("arena", [128, ARENA_BYTES // 2], BF16))
        PS = [st.enter_context(nc.psum_tensor("ps%d" % i, [128, 1024], F32)) for i in range(4)]

        class Alloc:
            def __init__(self, base, limit):
                self.off = base
                self.limit = limit

            def __call__(self, dtype, shape):
                esz = 4 if dtype == F32 else 2
                n = 1
                for s in shape:
                    n *= s
                nbytes = (n * esz + 63) // 64 * 64
                off = self.off
                self.off += nbytes
                assert self.off <= self.limit, ("arena overflow", self.off, self.limit)
                a = arena[:, off // 2: off // 2 + n * esz // 2]
                if dtype == F32:
                    a = a.bitcast(F32)
                if len(shape) == 2:
                    a = a.rearrange("p (a b) -> p a b", a=shape[0])
                elif len(shape) == 3:
                    a = a.rearrange("p (a b c) -> p a b c", a=shape[0], b=shape[1])
                return a

        fx = Alloc(0, ARENA_BYTES)
        xres = fx(F32, [16, D])
        KTo = fx(BF16, [4, 2048])
        Vo = fx(BF16, [16, 4, 129])
        wp = [fx(BF16, [4096]) for _ in range(2)]
        idb = fx(BF16, [128])
        bones = fx(BF16, [128])
        gcol = fx(F32, [3, 8])
        gfin = fx(F32, [D])
        gsgu = fx(F32, [512])
        gsub = fx(F32, [128])
        gq = fx(F32, [1])
        gk = fx(F32, [1])
        gbias = fx(F32, [16])
        bsT = fx(F32, [8])
        wsT = fx(BF16, [8, 128])
        nlam = fx(F32, [1])
        cpos = fx(F32, [4])
        cneg = fx(F32, [4])
        coth = fx(F32, [4])
        epsc = fx(F32, [1])
        parc = fx(F32, [2])
        pss = fx(F32, [16])
        prs = fx(F32, [16])
        strips = {}
        for side in ("neg", "pos"):
            for part in ("hi", "lo"):
                strips[(side, part)] = fx(BF16, [4, 384])
        bstr = {}
        for nm in ("A", "B"):
            for part in ("hi", "lo"):
                bstr[(nm, part)] = fx(BF16, [4, 128])
        RBASE = fx.off
        RLIM = ARENA_BYTES

        def psbank(i):
            return PS[i // 2][:, (i % 2) * 512:(i % 2) * 512 + 512]

        def psbank_bf(i):
            return psbank(i).bitcast(BF16)

        tmp = Alloc(RBASE, RLIM)
        t_rb = tmp(F32, [4])
        t_ones = tmp(F32, [128])
        t_rbB = tmp(F32, [4, 128])
        t_oh = tmp(F32, [512])
        t_g = tmp(F32, [4, 512])
        t_T = tmp(F32, [4, 384])
        t_Ts = tmp(F32, [384])
        t_hi = tmp(BF16, [384])
        t_w = tmp(F32, [8, 128])
        t_wb = tmp(BF16, [8, 128])
        t_l = tmp(F32, [4, 64])
        t_lp = tmp(F32, [2, 64])
        t_ls = tmp(F32, [2])
        t_le = tmp(F32, [2])

        A = P.add
        A("pool", lambda e: e.dma_start(out=idb, in_=H["ident"].ap()), writes=["idb"], dma_ch="c0")
        A("pool", lambda e: e.dma_start(out=bones, in_=H["bones"].ap()), writes=["bones"], dma_ch="c1")
        for i, nm in enumerate(("ffn1_norm", "mix_norm", "ffn2_norm")):
            A("sp", lambda e, i=i, nm=nm: e.dma_start(
                out=gcol[:, i, :], in_=H[nm].ap().rearrange("o (k p) -> p (o k)", p=128),
                allow_slow_non_contiguous=True), writes=["gcol"], dma_ch="c2")
        A("sp", lambda e: e.dma_start(out=gbias, in_=H["gate_bias"].ap().rearrange("o (k p) -> p (o k)", p=128),
                                      allow_slow_non_contiguous=True), writes=["gbias"], dma_ch="c3")
        A("sp", lambda e: e.dma_start(out=bsT, in_=H["sgu_b"].ap().rearrange("g t -> t g"),
                                      allow_slow_non_contiguous=True), writes=["bsT"], dma_ch="c3")
        A("sp", lambda e: e.dma_start(out=gfin, in_=bc(H["final_norm"], D)), writes=["gfin"], dma_ch="c4")
        A("sp", lambda e: e.dma_start(out=gsgu, in_=bc(H["sgu_norm"], 512)), writes=["gsgu"], dma_ch="c4")
        A("sp", lambda e: e.dma_start(out=gsub, in_=bc(H["diff_subln"], 128)), writes=["gsub"], dma_ch="c4")
        A("sp", lambda e: e.dma_start(out=parc, in_=H["par"].ap()), writes=["parc"], dma_ch="c4")
        for half in range(2):
            A("sp", lambda e, half=half: e.dma_start(
                out=gq[half * 64:(half + 1) * 64, :], in_=H["q_norm"].ap().rearrange("o d -> d o")),
                writes=["gq"], dma_ch="c5")
            A("sp", lambda e, half=half: e.dma_start(
                out=gk[half * 64:(half + 1) * 64, :], in_=H["k_norm"].ap().rearrange("o d -> d o")),
                writes=["gk"], dma_ch="c5")
        A("dve", lambda e: e.tensor_scalar(out=gq, in0=gq, scalar1=0.125, scalar2=None, op0=ALU.mult),
          reads=["gq"], writes=["gq"])
        A("dve", lambda e: e.tensor_scalar(out=gsub, in0=gsub, scalar1=1.0 - LAMBDA_INIT, scalar2=None, op0=ALU.mult),
          reads=["gsub"], writes=["gsub"])
        A("dve", lambda e: e.memset(epsc, EPS), writes=["epsc"])
        A("dve", lambda e: e.memset(Vo[:, :, :, 128:129], 1.0), writes=[("V", b) for b in range(16)])
        A("sp", lambda e: e.dma_start(out=cpos, in_=bc(H["rel_bias"], 4, off=31 * 4)), writes=["cpos"], dma_ch="c6")
        A("sp", lambda e: e.dma_start(out=cneg, in_=bc(H["rel_bias"], 4, off=15 * 4)), writes=["cneg"], dma_ch="c6")
        A("dve", lambda e: e.tensor_scalar(out=coth, in0=cpos, scalar1=parc[:, 0:1], scalar2=None, op0=ALU.mult),
          reads=["cpos", "parc"], writes=["coth"])
        A("dve", lambda e: e.scalar_tensor_tensor(out=coth, in0=cneg, scalar=parc[:, 1:2], in1=coth,
                                                  op0=ALU.mult, op1=ALU.add),
          reads=["cneg", "parc", "coth"], writes=["coth"])
        for i, nm in enumerate(("lambda_q1", "lambda_k1", "lambda_q2", "lambda_k2")):
            A("sp", lambda e, i=i, nm=nm: e.dma_start(out=t_l[:, i, :], in_=bc(H[nm], 64)), writes=["t_l"], dma_ch="c7")
        A("dve", lambda e: e.tensor_tensor(out=t_lp[:, 0, :], in0=t_l[:, 0, :], in1=t_l[:, 1, :], op=ALU.mult),
          reads=["t_l"], writes=["t_lp"])
        A("dve", lambda e: e.tensor_tensor(out=t_lp[:, 1, :], in0=t_l[:, 2, :], in1=t_l[:, 3, :], op=ALU.mult),
          reads=["t_l", "t_lp"], writes=["t_lp"])
        A("dve", lambda e: e.reduce_sum(out=t_ls, in_=t_lp, axis=AX.X), reads=["t_lp"], writes=["t_ls"])
        A("act", lambda e: e.activation(out=t_le, in_=t_ls, func=AF.Exp), reads=["t_ls"], writes=["t_le"])
        A("dve", lambda e: e.scalar_tensor_tensor(out=nlam, in0=t_le[:, 1:2], scalar=-LAMBDA_INIT, in1=t_le[:, 0:1],
                                                  op0=ALU.add, op1=ALU.subtract),
          reads=["t_le"], writes=["nlam"])
        A("sp", lambda e: e.dma_start(out=t_w, in_=H["sgu_w"].ap().rearrange("g t s -> t g s")), writes=["t_w"], dma_ch="c8")
        A("dve", lambda e: e.tensor_copy(out=t_wb, in_=t_w), reads=["t_w"], writes=["t_wb"])
        for g in range(8):
            A("pe", lambda e, g=g: e.transpose(out=psbank_bf(0)[:, g * 128:(g + 1) * 128], in_=t_wb[:, g, :], identity=idb),
              reads=["t_wb", "idb"], writes=[("ps", 0)])
        A("dve", lambda e: e.tensor_copy(out=wsT, in_=psbank_bf(0).rearrange("p (g t) -> p g t", g=8)),
          reads=[("ps", 0)], writes=["wsT"])
        A("sp", lambda e: e.dma_start(out=t_rb[0:32, :], in_=H["rel_bias"].ap()), writes=["t_rb"], dma_ch="c9")
        A("sp", lambda e: e.dma_start(out=t_oh[0:32, :], in_=H["oh"].ap()), writes=["t_oh"], dma_ch="c9")
        A("dve", lambda e: e.memset(t_ones, 1.0), writes=["t_ones"])
        for h in range(4):
            A("dve", lambda e, h=h: e.tensor_scalar(out=t_rbB[0:32, h, :], in0=t_ones[0:32, :], scalar1=t_rb[0:32, h:h + 1],
                                                    scalar2=None, op0=ALU.mult),
              reads=["t_ones", "t_rb"], writes=[("t_rbB", h)])
            A("pe", lambda e, h=h: e.matmul(out=psbank(2 + (h % 2)), lhsT=t_rbB[0:32, h, :], rhs=t_oh[0:32, :],
                                            start=True, stop=True),
              reads=[("t_rbB", h), "t_oh"], writes=[("ps", 2 + (h % 2))])
            A("act", lambda e, h=h: e.copy(out=t_g[:, h, :], in_=psbank(2 + (h % 2))),
              reads=[("ps", 2 + (h % 2))], writes=[("t_g", h)])
            A("sp", lambda e, h=h: e.dma_start(out=gscr.ap()[h], in_=t_g[:, h, :]), reads=[("t_g", h)],
              writes=[("gscr", h)], dma_ch="c10")
            A("sp", lambda e, h=h: e.dma_start(out=t_T[:, h, :],
                                               in_=bass.AP(gscr, h * 128 * 512 + 127, [[511, 128], [1, 384]])),
              reads=[("gscr", h)], writes=[("t_T", h)], dma_ch="c11")

            def mk_strip(src, ccol, mask, dst_hi, dst_lo, h=h):
                n = src.shape[-1]
                A("dve", lambda e: e.tensor_scalar(out=t_Ts[:, 0:n], in0=src, scalar1=ccol, scalar2=None, op0=ALU.subtract),
                  reads=[("t_T", h), "cpos", "cneg"], writes=["t_Ts"])
                if mask is not None:
                    A("dve", lambda e: e.tensor_scalar(out=t_Ts[:, 0:n], in0=t_Ts[:, 0:n], scalar1=mask, scalar2=None,
                                                       op0=ALU.mult),
                      reads=["t_Ts", "parc"], writes=["t_Ts"])
                A("dve", lambda e: e.tensor_copy(out=dst_hi, in_=t_Ts[:, 0:n]), reads=["t_Ts"], writes=["strips", "t_hi"])
                A("dve", lambda e: e.tensor_tensor(out=dst_lo, in0=t_Ts[:, 0:n], in1=dst_hi, op=ALU.subtract),
                  reads=["t_Ts", "strips", "t_hi"], writes=["strips"])

            mk_strip(t_T[:, h, :], cneg[:, h:h + 1], None, strips[("neg", "hi")][:, h, :], strips[("neg", "lo")][:, h, :])
            mk_strip(t_T[:, h, :], cpos[:, h:h + 1], None, strips[("pos", "hi")][:, h, :], strips[("pos", "lo")][:, h, :])
            mk_strip(t_T[:, h, 0:128], cpos[:, h:h + 1], parc[:, 0:1], bstr[("A", "hi")][:, h, :], bstr[("A", "lo")][:, h, :])
            mk_strip(t_T[:, h, 256:384], cneg[:, h:h + 1], parc[:, 1:2], bstr[("B", "hi")][:, h, :], bstr[("B", "lo")][:, h, :])

        bar_t = fx(F32, [1]) if False else epsc

        def barrier():
            P.barrier("dve", lambda e: e.memset(epsc, EPS))

        if (debug or "full") == "full":
            v0 = H["xp"].ap()[0].rearrange("(b p) d -> p b d", p=128)
            for b4 in range(4):
                A("sp", lambda e, b4=b4: e.dma_start(out=xres[:, b4 * 4:(b4 + 1) * 4, :], in_=v0[:, b4 * 4:(b4 + 1) * 4, :]),
                  writes=[("xr", b) for b in range(b4 * 4, b4 * 4 + 4)], dma_ch=("x", b4))
        barrier()

        fv = Alloc(RBASE, RLIM)
        xnT = fv(BF16, [8, 2048])
        actT = fv(BF16, [4, 2048])
        wo = fv(BF16, [4, D])
        f_sg = [fv(F32, [512]) for _ in range(2)]
        f_xnb = [fv(BF16, [D]) for _ in range(2)]
        f_junk = fv(BF16, [D])
        f_ss = fv(F32, [16])
        f_rs = fv(F32, [16])
        print("ffn view end", fv.off, RLIM)
        _al = Alloc(RBASE + 32768, RBASE + 32768 + 16384)
        f_sq = [_al(BF16, [512]) for _ in range(2)]
        f_krs = [_al(F32, [512]) for _ in range(2)]

        tv = Alloc(RBASE, RLIM)
        hT = tv(BF16, [8, 512])
        QT = tv(BF16, [4, 512])
        PT2 = [tv(BF16, [1024]) for _ in range(2)]
        PT = [PT2[i // 2][:, (i % 2) * 512:(i % 2) * 512 + 512] for i in range(4)]
        kring = [tv(BF16, [1024]) for _ in range(2)]
        vring = [tv(BF16, [8, 129]) for _ in range(2)]
        t_gg = tv(F32, [2 * D])
        t_ga = t_gg[:, 0:D]
        t_gu = t_gg[:, D:2 * D]
        o_sb = t_gg[:, 0:1032].rearrange("p (c q d) -> p c q d", c=2, q=4)
        t_o0 = t_gg[:, 1032:1544].rearrange("p (q d) -> p q d", q=4)
        t_junk = tv(BF16, [D])
        t_vn2 = [tv(BF16, [512]) for _ in range(2)]
        t_s1 = t_ga[:, 0:512]
        t_A2pair = tv(BF16, [1024])
        t_A2 = [t_A2pair[:, 0:512], t_A2pair[:, 512:1024]]
        t_xnb_b = t_A2pair
        AT = tv(BF16, [4, 512])
        t_r8 = tv(F32, [8])
        obn = tv(BF16, [4, 512])
        boT = tv(BF16, [4, 512])
        t_sa = tv(F32, [512])
        t_sb = tv(F32, [512])
        t_m1 = t_sa
        t_m2 = t_sb
        mT = tv(BF16, [8, 512])
        t_sq2 = [tv(BF16, [512]), t_A2[1]]
        t_rs2 = [tv(F32, [512]), t_sb]
        q_sqres = [("qsq", 0), ("A", 1)]
        q_rsres = [("qrs", 0), "sb"]
        t_xnb = tv(BF16, [D])
        t_ss = tv(F32, [16])
        t_r = tv(F32, [16])

        print("tail view end", tv.off, RLIM)
        cnt = {"x": 0, "wp": 0, "ps": 0, "o": 0}

        def load_x(src_rows, part="both"):
            v = src_rows.rearrange("(b p) d -> p b d", p=128)
            if part != "sq":
                for b4 in range(4):
                    A("sp", lambda e, b4=b4: e.dma_start(out=xres[:, b4 * 4:(b4 + 1) * 4, :], in_=v[:, b4 * 4:(b4 + 1) * 4, :]),
                      writes=[("xr", b) for b in range(b4 * 4, b4 * 4 + 4)], dma_ch=("x", b4))
            if part != "dma":
                for b4 in range(4):
                    sq_blocks(list(range(b4 * 4, b4 * 4 + 4)), f_junk)

        def sq_blocks(blks, junk):
            c = blks[0] // 4
            A("dve", lambda e: e.memset(pss[:, blks[0]:blks[-1] + 1], 0.0), writes=[("pss", c)])
            for b in blks:
                A("act", lambda e, b=b: e.activation(out=junk, in_=xres[:, b, :], func=AF.Square, accum_out=pss[:, b:b + 1]),
                  reads=[("xr", b), ("pss", c)], writes=["njunk", ("pss", c)])

        def norm_to_T(blks, gi, dstT, xnb, tbank, col0=0, reuse_rs=False):
            c = blks[0] // 4
            lo, hi = blks[0], blks[-1] + 1
            if not reuse_rs:
                A("act", lambda e: e.activation(out=prs[:, lo:hi], in_=pss[:, lo:hi], func=AF.Ln, scale=1.0 / D, bias=epsc),
                  reads=[("pss", c), "epsc"], writes=[("prs", c)])
                A("act", lambda e: e.activation(out=prs[:, lo:hi], in_=prs[:, lo:hi], func=AF.Exp, scale=-0.5),
                  reads=[("prs", c)], writes=[("prs", c)])
            for i, b in enumerate(blks):
                xb = xnb[i % len(xnb)]
                rn = ("nxnb", i % len(xnb))
                A("act", lambda e, b=b, xb=xb: e.mul(out=xb, in_=xres[:, b, :], mul=prs[:, b:b + 1]),
                  reads=[("xr", b), ("prs", c)], writes=[rn])
                bank = tbank[i % len(tbank)]
                for k in range(8):
                    A("pe", lambda e, k=k, xb=xb, bank=bank: e.transpose(
                        out=psbank_bf(bank)[:, k * 128:(k + 1) * 128], in_=xb[:, k * 128:(k + 1) * 128], identity=idb),
                      reads=[rn, "idb"], writes=[("ps", bank)])
                cc = col0 + i * 128
                A("dve", lambda e, bank=bank, cc=cc: e.tensor_tensor(
                    out=dstT[:, :, cc:cc + 128], in0=psbank_bf(bank).rearrange("p (k t) -> p k t", k=8),
                    in1=gcol[:, gi, :].unsqueeze(2).to_broadcast([128, 8, 128]), op=ALU.mult),
                  reads=[("ps", bank), "gcol"], writes=[("xT", (col0 // 128) + i)])

        def wload(dst, src, res, chname):
            war = [("wpR", res[1])] if isinstance(res, tuple) else []
            A("pool", lambda e: e.dma_start(out=dst, in_=src), writes=[res], dma_ch=chname, war=war)

        def ffn(w_in_h, w_out_h, gi):
            win = w_in_h.ap().rearrange("(k p) c -> p k c", p=128)
            wout = w_out_h.ap().rearrange("(k p) c -> p k c", p=128)
            for ch in range(4):
                norm_to_T(list(range(4 * ch, 4 * ch + 4)), gi, xnT, f_xnb, [6, 7], col0=512 * ch)
            groups = [(0, 4), (4, 4), (8, 4), (12, 4), (16, 4), (20, 2)]
            for (c0, n) in groups:
                for pc in range(n // 2):
                    wi = cnt["wp"] % 2
                    cnt["wp"] += 1
                    w = wp[wi].rearrange("p (k c) -> p k c", k=8)
                    cc = (c0 + 2 * pc) * 128
                    wload(w[:, :, 0:256], win[:, :, cc:cc + 256], ("wpa", wi), ("wpa", wi))
                    wload(w[:, :, 256:512], win[:, :, DFF + cc:DFF + cc + 256], ("wpb", wi), ("wpb", wi))
                    if pc == n // 2 - 1:
                        wload(wo[:, 0:n, :], wout[:, c0:c0 + n, :], "wo", "wo")
                    for pr in range(2):
                        slot = 2 * pc + pr
                        for t in range(4):
                            pi = cnt["ps"] % 2
                            cnt["ps"] += 1
                            bg, bu = 2 * pi, 2 * pi + 1
                            for k in range(8):
                                A("pe", lambda e, k=k, w=w, pr=pr, t=t, bg=bg: e.matmul(
                                    out=psbank(bg), lhsT=w[:, k, pr * 128:(pr + 1) * 128],
                                    rhs=xnT[:, k, t * 512:(t + 1) * 512], start=(k == 0), stop=(k == 7)),
                                  reads=[("wpa", wi), ("wpR", wi)] + [("xT", 4 * t + j) for j in range(4)], writes=[("ps", bg)])
                            for k in range(8):
                                A("pe", lambda e, k=k, w=w, pr=pr, t=t, bu=bu: e.matmul(
                                    out=psbank(bu), lhsT=w[:, k, 256 + pr * 128:256 + (pr + 1) * 128],
                                    rhs=xnT[:, k, t * 512:(t + 1) * 512], start=(k == 0), stop=(k == 7)),
                                  reads=[("wpb", wi), ("wpR", wi)] + [("xT", 4 * t + j) for j in range(4)], writes=[("ps", bu)])
                            sg = f_sg[pi]
                            A("act", lambda e, sg=sg, bg=bg: e.activation(out=sg, in_=psbank(bg), func=AF.Silu),
                              reads=[("ps", bg)], writes=[("sg", pi)])
                            A("dve", lambda e, sg=sg, bu=bu, slot=slot, t=t: e.tensor_tensor(
                                out=actT[:, slot, t * 512:(t + 1) * 512], in0=psbank(bu), in1=sg, op=ALU.mult),
                              reads=[("ps", bu), ("sg", pi)], writes=[("act", slot, t)])
                for b in range(16):
                    yb = 2 + (cnt["o"] % 2)
                    cnt["o"] += 1
                    for half in range(2):
                        for s in range(n):
                            A("pe", lambda e, b=b, half=half, s=s, yb=yb, n=n: e.matmul(
                                out=PS[yb][:, half * 512:(half + 1) * 512], lhsT=actT[:, s, b * 128:(b + 1) * 128],
                                rhs=wo[:, s, half * 512:(half + 1) * 512], start=(s == 0), stop=(s == n - 1)),
                              reads=[("act", s, b // 4), "wo"], writes=[("ps", 2 * yb), ("ps", 2 * yb + 1)])
                    A("dve", lambda e, b=b, yb=yb: e.scalar_tensor_tensor(
                        out=xres[:, b, :], in0=PS[yb][:, :], scalar=0.5, in1=xres[:, b, :], op0=ALU.mult, op1=ALU.add),
                      reads=[("ps", 2 * yb), ("ps", 2 * yb + 1), ("xr", b)], writes=[("xr", b)])
                    if c0 == 20 and b % 4 == 3:
                        sq_blocks(list(range(b - 3, b + 1)), f_junk)

        wmix = H["w_in"].ap().rearrange("(k p) c -> p k c", p=128)

        def qk_pre(ps_i, sq, sq_res):
            A("act", lambda e: e.activation(out=sq, in_=psbank(ps_i), func=AF.Square), reads=[("ps", ps_i)], writes=[sq_res])

        def qk_post(ps_i, ss_i, gcolq, sq, rs, dst, dst_res, rd_extra, sq_res, rs_res):
            A("pe", lambda e: e.matmul(out=psbank(ss_i), lhsT=bones, rhs=sq, start=True, stop=True),
              reads=[sq_res, "bones"], writes=[("ps", ss_i)])
            A("act", lambda e: e.activation(out=rs, in_=psbank(ss_i), func=AF.Ln, bias=epsc), reads=[("ps", ss_i), "epsc"],
              writes=[rs_res])
            A("act", lambda e: e.activation(out=rs, in_=rs, func=AF.Exp, scale=-0.5), reads=[rs_res], writes=[rs_res])
            A("dve", lambda e: e.scalar_tensor_tensor(out=dst, in0=psbank(ps_i), scalar=gcolq, in1=rs, op0=ALU.mult,
                                                      op1=ALU.mult),
              reads=[("ps", ps_i), rs_res] + rd_extra, writes=[dst_res])

        def mix_kv():
            for ch in range(4):
                norm_to_T(list(range(4 * ch, 4 * ch + 4)), 1, xnT, f_xnb, [6, 7], col0=512 * ch)
            wk = wp[0].rearrange("p (k c) -> p k c", k=8)
            wv = wp[1].rearrange("p (k c) -> p k c", k=8)
            wload(wk, wmix[:, :, 1536:2048], ("wpa", 0), ("wpa", 0))
            wload(wv, wmix[:, :, 2048:2560], ("wpa", 1), ("wpa", 1))
            items = [(h, t) for h in range(4) for t in range(4)]

            def k_mm(i):
                h, t = items[i]
                pb = i % 4
                for k in range(8):
                    A("pe", lambda e, k=k, h=h, t=t, pb=pb: e.matmul(
                        out=psbank(pb), lhsT=wk[:, k, h * 128:(h + 1) * 128], rhs=xnT[:, k, t * 512:(t + 1) * 512],
                        start=(k == 0), stop=(k == 7)),
                      reads=[("wpa", 0), ("wpR", 0)] + [("xT", 4 * t + j) for j in range(4)], writes=[("ps", pb)])
                qk_pre(pb, f_sq[i % 2], ("qsq", i % 2))

            def k_post(i):
                h, t = items[i]
                qk_post(i % 4, 4 + (i % 2), gk[:, 0:1], f_sq[i % 2], f_krs[i % 2], KTo[:, h, t * 512:(t + 1) * 512],
                        ("KT", h, t), ["gk"], ("qsq", i % 2), ("qrs", i % 2))

            k_mm(0)
            for i in range(16):
                if i + 1 < 16:
                    k_mm(i + 1)
                k_post(i)
            for b in range(16):
                pb = 6 + (b % 2)
                for k in range(8):
                    A("pe", lambda e, k=k, b=b, pb=pb: e.matmul(
                        out=psbank(pb), lhsT=xnT[:, k, b * 128:(b + 1) * 128], rhs=wv[:, k, :], start=(k == 0), stop=(k == 7)),
                      reads=[("wpa", 1), ("wpR", 1), ("xT", b)], writes=[("ps", pb)])
                A("act", lambda e, b=b, pb=pb: e.copy(out=Vo[:, b, :, 0:128], in_=psbank(pb).rearrange("p (h d) -> p h d", h=4)),
                  reads=[("ps", pb)], writes=[("V", b)])

        def spill_kv():
            A("sp", lambda e: e.dma_start(out=ktx.ap(), in_=KTo.rearrange("p h t -> p (h t)")),
              reads=[("KT", h, t) for h in range(4) for t in range(4)], writes=["ktx"], dma_ch="sk")
            A("sp", lambda e: e.dma_start(out=vx.ap(), in_=Vo.rearrange("p b h d -> p (b h d)")),
              reads=[("V", b) for b in range(16)], writes=["vx"], dma_ch="sv")

        ktxv = ktx.ap().rearrange("p (h t) -> p h t", h=4)
        vxv = vx.ap().rearrange("p (b h d) -> p b h d", b=16, h=4)

        tail_no = [0]

        def tload(pid, wi, loads):
            subs = [("wpa", wi), ("wpb", wi), ("wpc", wi), ("wpd", wi)]
            scr = wscr.ap()[:, pid * 4096:(pid + 1) * 4096]
            if tail_no[0] == 0:
                for (dst, src, sub) in loads:
                    A("pool", lambda e, dst=dst, src=src: e.dma_start(out=dst, in_=src), writes=[(sub, wi)],
                      dma_ch=(sub, wi), war=[("wpR", wi)])
                A("sp", lambda e: e.dma_start(out=scr, in_=wp[wi]), reads=[(sub, wi) for (_, _, sub) in loads],
                  writes=[("wscr", pid)], dma_ch=("wsw", wi))
            else:
                A("sp", lambda e: e.dma_start(out=wp[wi], in_=scr), reads=[("wscr", pid)], writes=subs,
                  dma_ch=("wsr", wi), war=[("wpR", wi)])

        def tail(qt, has_oth):
            blks = [4 * qt + j for j in range(4)]
            norm_to_T(blks, 1, hT, [t_xnb, t_xnb_b], [7, 6], reuse_rs=True)
            hres = [("xT", j) for j in range(4)]
            wq = wp[cnt["wp"] % 2].rearrange("p (k c) -> p k c", k=8)
            wqi = cnt["wp"] % 2
            cnt["wp"] += 1
            tload(0, wqi, [(wq, wmix[:, :, 1024:1536], "wpa")])
            def q_mm(h):
                for k in range(8):
                    A("pe", lambda e, k=k, h=h: e.matmul(out=psbank(h), lhsT=wq[:, k, h * 128:(h + 1) * 128],
                                                          rhs=hT[:, k, :], start=(k == 0), stop=(k == 7)),
                      reads=[("wpa", wqi), ("wpR", wqi)] + hres, writes=[("ps", h)])
                qk_pre(h, t_sq2[h % 2], q_sqres[h % 2])

            def q_post(h):
                qk_post(h, 4 + (h % 2), gq[:, 0:1], t_sq2[h % 2], t_rs2[h % 2], QT[:, h, :], ("QT", h), ["gq"],
                        q_sqres[h % 2], q_rsres[h % 2])

            q_mm(0)
            for h in range(4):
                if h + 1 < 4:
                    q_mm(h + 1)
                q_post(h)
            wui = cnt["wp"] % 2
            cnt["wp"] += 1
            wu = wp[wui].rearrange("p (k c) -> p k c", k=8)
            tload(1, wui, [(wu, wmix[:, :, 0:512], "wpa")])
            wvi = cnt["wp"] % 2
            cnt["wp"] += 1
            wva = wp[wvi].rearrange("p (k c) -> p k c", k=8)
            tload(2, wvi, [(wva, wmix[:, :, 512:1024], "wpa")])
            def t3_proj(j):
                p = j % 2
                uvp = PS[p]
                ub0 = 2 * p
                gu = t_gg[:, p * D:(p + 1) * D]
                vn = t_vn2[p]
                Ab = t_A2[p]
                for k in range(8):
                    A("pe", lambda e, k=k, j=j, uvp=uvp: e.matmul(out=uvp[:, 0:512], lhsT=hT[:, k, j * 128:(j + 1) * 128],
                                                                   rhs=wu[:, k, :], start=(k == 0), stop=(k == 7)),
                      reads=[("wpa", wui), ("wpR", wui)] + hres, writes=[("ps", ub0)])
                for k in range(8):
                    A("pe", lambda e, k=k, j=j, uvp=uvp: e.matmul(out=uvp[:, 512:1024], lhsT=hT[:, k, j * 128:(j + 1) * 128],
                                                                   rhs=wva[:, k, :], start=(k == 0), stop=(k == 7)),
                      reads=[("wpa", wvi), ("wpR", wvi)] + hres, writes=[("ps", ub0 + 1)])
                uvr = [("ps", ub0), ("ps", ub0 + 1)]
                gr = "ga" if p == 0 else "gu"
                A("act", lambda e, uvp=uvp, gu=gu: e.activation(out=gu, in_=uvp[:, :], func=AF.Gelu_apprx_tanh), reads=uvr,
                  writes=[gr])
                A("dve", lambda e, p=p: e.memset(t_ss[:, 8 + p:9 + p], 0.0), writes=[("vss", p)])
                A("act", lambda e, p=p, gu=gu: e.activation(out=t_junk[:, 0:512], in_=gu[:, 512:1024], func=AF.Square,
                                                            accum_out=t_ss[:, 8 + p:9 + p]), reads=[gr, ("vss", p)],
                  writes=[("vss", p), "vjunk"])
                A("act", lambda e, p=p: e.activation(out=t_r[:, 10 + p:11 + p], in_=t_ss[:, 8 + p:9 + p], func=AF.Ln,
                                                     scale=1.0 / 512, bias=epsc), reads=[("vss", p), "epsc"], writes=[("vr", p)])
                A("act", lambda e, p=p: e.activation(out=t_r[:, 10 + p:11 + p], in_=t_r[:, 10 + p:11 + p], func=AF.Exp,
                                                     scale=-0.5), reads=[("vr", p)], writes=[("vr", p)])
                A("dve", lambda e, p=p, gu=gu, vn=vn: e.tensor_scalar(out=vn, in0=gu[:, 512:1024], scalar1=t_r[:, 10 + p:11 + p],
                                                                     scalar2=None, op0=ALU.mult), reads=[gr, ("vr", p)],
                  writes=[("vn", p)])

            def t3_post(j):
                p = j % 2
                uvp = PS[p]
                ub0 = 2 * p
                gu = t_gg[:, p * D:(p + 1) * D]
                vn = t_vn2[p]
                Ab = t_A2[p]
                gr = "ga" if p == 0 else "gu"
                for g in range(8):
                    A("pe", lambda e, g=g, vn=vn: e.matmul(out=psbank(6)[:, g * 64:(g + 1) * 64], lhsT=wsT[:, g, :],
                                                           rhs=vn[:, g * 64:(g + 1) * 64], start=True, stop=True),
                      reads=[("vn", p), "wsT"], writes=[("ps", 6)])
                s1 = gu[:, 512:1024]
                A("dve", lambda e, s1=s1: e.tensor_tensor(out=s1, in0=psbank(6), in1=gsgu, op=ALU.mult),
                  reads=[("ps", 6), "gsgu", gr], writes=[gr])
                A("dve", lambda e, s1=s1: e.tensor_tensor(out=s1.rearrange("p (g d) -> p g d", g=8),
                                                          in0=s1.rearrange("p (g d) -> p g d", g=8),
                                                          in1=bsT.unsqueeze(2).to_broadcast([128, 8, 64]), op=ALU.add),
                  reads=[gr, "bsT"], writes=[gr])
                A("dve", lambda e, s1=s1, gu=gu, Ab=Ab: e.tensor_tensor(out=Ab, in0=s1, in1=gu[:, 0:512], op=ALU.mult),
                  reads=[gr], writes=[("A", p)])

            def t3_post_b(j):
                p = j % 2
                Ab = t_A2[p]
                for c in range(4):
                    A("pe", lambda e, c=c, Ab=Ab: e.transpose(out=psbank_bf(7)[:, c * 128:(c + 1) * 128],
                                                              in_=Ab[:, c * 128:(c + 1) * 128], identity=idb),
                      reads=[("A", p), "idb"], writes=[("ps", 7)])
                A("act", lambda e, j=j: e.copy(out=AT[:, :, j * 128:(j + 1) * 128],
                                               in_=psbank_bf(7)[:, 0:512].rearrange("p (c t) -> p c t", c=4)),
                  reads=[("ps", 7)], writes=[("AT", j)])

            t3_proj(0)
            t3_proj(1)
            t3_post(0)
            t3_proj(2)
            t3_post_b(0)
            t3_post(1)
            t3_proj(3)
            t3_post_b(1)
            t3_post(2)
            t3_post_b(2)
            t3_post(3)
            t3_post_b(3)
            sctr = [0]
            pending = [None]

            def combine(h):
                o0 = o_sb[:, 0, :, 0:128]
                o1 = o_sb[:, 1, :, 0:128]
                gg = ["ga", "gu"]
                A("dve", lambda e: e.reciprocal(out=t_r8.rearrange("p (c q) -> p c q", c=2), in_=o_sb[:, :, :, 128]),
                  reads=gg, writes=["r8"])
                A("dve", lambda e: e.tensor_scalar(out=t_r8[:, 4:8], in0=t_r8[:, 4:8], scalar1=nlam[:, 0:1], scalar2=None,
                                                   op0=ALU.mult), reads=["r8", "nlam"], writes=["r8"])
                A("dve", lambda e: e.tensor_tensor(out=t_o0, in0=o0, in1=t_r8[:, 0:4].unsqueeze(2).to_broadcast([128, 4, 128]),
                                                   op=ALU.mult), reads=gg + ["r8"], writes=gg)
                A("dve", lambda e: e.tensor_tensor(out=o1, in0=o1, in1=t_r8[:, 4:8].unsqueeze(2).to_broadcast([128, 4, 128]),
                                                   op=ALU.mult), reads=gg + ["r8"], writes=gg)
                A("dve", lambda e: e.tensor_tensor(out=t_o0, in0=t_o0, in1=o1, op=ALU.add), reads=gg, writes=gg)
                A("dve", lambda e: e.tensor_tensor(out=o1, in0=t_o0, in1=t_o0, op=ALU.mult), reads=gg, writes=gg)
                A("dve", lambda e: e.reduce_sum(out=t_ss[:, 12:16], in_=o1, axis=AX.X), reads=gg, writes=["oss"])
                A("act", lambda e: e.activation(out=t_r[:, 4:8], in_=t_ss[:, 12:16], func=AF.Ln, scale=1.0 / 128, bias=epsc),
                  reads=["oss", "epsc"], writes=["orr"])
                A("act", lambda e: e.activation(out=t_r[:, 4:8], in_=t_r[:, 4:8], func=AF.Exp, scale=-0.5), reads=["orr"],
                  writes=["orr"])
                A("dve", lambda e: e.tensor_tensor(out=t_o0, in0=t_o0, in1=t_r[:, 4:8].unsqueeze(2).to_broadcast([128, 4, 128]),
                                                   op=ALU.mult), reads=gg + ["orr"], writes=gg)
                A("dve", lambda e, h=h: e.tensor_tensor(out=obn[:, :, h * 128:(h + 1) * 128], in0=t_o0,
                                                        in1=gsub.unsqueeze(1).to_broadcast([128, 4, 128]), op=ALU.mult),
                  reads=gg + ["gsub"], writes=[("obn", qb) for qb in range(4)])

            for h in range(4):
                kbs = [("own", kb) for kb in range(16)]
                if has_oth:
                    kbs += [("oth", kb) for kb in range(16)]
                nk = len(kbs)
                st_info = {}

                def stage_qk(ki, h=h, kbs=kbs, st_info=st_info):
                    src, kb = kbs[ki]
                    if src == "oth" and kb % 8 == 0:
                        ri = (kb // 8) % 2
                        A("sp", lambda e, h=h, kb=kb, ri=ri: e.dma_start(out=kring[ri], in_=ktxv[:, h, kb * 128:(kb + 8) * 128]),
                          reads=["ktx"], writes=[("kr", ri)], dma_ch=("kr", ri))
                        A("sp", lambda e, h=h, kb=kb, ri=ri: e.dma_start(out=vring[ri], in_=vxv[:, kb:kb + 8, h, :]),
                          reads=["vx"], writes=[("vr", ri)], dma_ch=("vr", ri))
                    if src == "own":
                        d = kb - 4 * qt
                        side = "neg" if d <= 1 else "pos"
                        bcol = (cneg if d <= 1 else cpos)[:, h:h + 1]
                        bres = "cneg" if d <= 1 else "cpos"
                        i0, i1 = max(0, d - 1), min(3, d + 1)
                        band = None
                        if i0 <= i1:
                            sc0 = 128 * (1 - (d - i0))
                            nb = 128 * (i1 - i0 + 1)
                            band = (128 * i0, nb, strips[(side, "hi")][:, h, sc0:sc0 + nb], strips[(side, "lo")][:, h, sc0:sc0 + nb])
                        kres = [("KT", h, kb // 4)]
                        vres = [("V", kb)]

                        def kt_ap(c, h=h, kb=kb):
                            return KTo[c * 64:(c + 1) * 64, h, kb * 128:(kb + 1) * 128]

                        v_ap = Vo[:, kb, h, :]
                    else:
                        ri = (kb // 8) % 2
                        bcol = coth[:, h:h + 1]
                        bres = "coth"
                        band = None
                        if kb == 0 and qt == 3:
                            band = (384, 128, bstr[("A", "hi")][:, h, :], bstr[("A", "lo")][:, h, :])
                        if kb == 15 and qt == 0:
                            band = (0, 128, bstr[("B", "hi")][:, h, :], bstr[("B", "lo")][:, h, :])
                        kres = [("kr", ri)]
                        vres = [("vr", ri)]

                        def kt_ap(c, ri=ri, kb=kb):
                            return kring[ri][c * 64:(c + 1) * 64, (kb % 8) * 128:(kb % 8 + 1) * 128]

                        v_ap = vring[ri][:, kb % 8, :]
                    pts = []
                    pi2 = sctr[0] % 2
                    sctr[0] += 1
                    for c in range(2):
                        sb = 2 * pi2 + c
                        A("pe", lambda e, c=c, sb=sb, kt_ap=kt_ap, h=h, band=band: e.matmul(
                            out=psbank(sb), lhsT=kt_ap(c), rhs=QT[c * 64:(c + 1) * 64, h, :], start=True, stop=(band is None)),
                          reads=kres + [("QT", h)], writes=[("ps", sb)])
                        if band is not None:
                            c0, nb, shi, slo = band
                            A("pe", lambda e, sb=sb, c0=c0, nb=nb, shi=shi: e.matmul(
                                out=psbank(sb)[:, c0:c0 + nb], lhsT=idb, rhs=shi, start=False, stop=False),
                              reads=["idb", "strips"], writes=[("ps", sb)])
                            A("pe", lambda e, sb=sb, c0=c0, nb=nb, slo=slo: e.matmul(
                                out=psbank(sb)[:, c0:c0 + nb], lhsT=idb, rhs=slo, start=False, stop=True),
                              reads=["idb", "strips"], writes=[("ps", sb)])
                        pts.append(sb)
                    A("act", lambda e, pi2=pi2, bcol=bcol: e.activation(out=PT2[pi2], in_=PS[pi2][:, :], func=AF.Exp, bias=bcol),
                      reads=[("ps", 2 * pi2), ("ps", 2 * pi2 + 1), bres], writes=[("PT", 2 * pi2), ("PT", 2 * pi2 + 1)])
                    st_info[ki] = (pts, v_ap, vres)

                def stage_pv(ki, nk=nk, st_info=st_info):
                    pts, v_ap, vres = st_info[ki]
                    for c in range(2):
                        for qb in range(4):
                            idx = c * 4 + qb
                            ob_bank = 4 + idx // 3
                            oc = (idx % 3) * 129
                            A("pe", lambda e, qb=qb, ob_bank=ob_bank, oc=oc, v_ap=v_ap, sb=pts[c], ki=ki, nk=nk, idx=idx: e.matmul(
                                out=psbank(ob_bank)[:, oc:oc + 129], lhsT=PT[sb][:, qb * 128:(qb + 1) * 128], rhs=v_ap,
                                start=(ki == 0 and idx % 3 == 0), stop=(ki == nk - 1), skip_group_check=True),
                              reads=[("PT", pts[c])] + vres, writes=[("ps", ob_bank)])

                stage_qk(0)
                for ki in range(nk):
                    if ki + 1 < nk:
                        stage_qk(ki + 1)
                    stage_pv(ki)
                    if ki == 3 and pending[0] is not None:
                        combine(pending[0])
                        pending[0] = None
                for bi, ncol in ((0, 387), (1, 387), (2, 258)):
                    A("dve", lambda e, bi=bi, ncol=ncol: e.tensor_copy(out=t_gg[:, bi * 387:bi * 387 + ncol],
                                                                      in_=psbank(4 + bi)[:, 0:ncol]),
                      reads=[("ps", 4 + bi)], writes=["ga", "gu"])
                pending[0] = h
            combine(pending[0])
            for qb in range(4):
                for c in range(4):
                    A("pe", lambda e, qb=qb, c=c: e.transpose(out=psbank_bf(7)[:, c * 128:(c + 1) * 128],
                                                              in_=obn[:, qb, c * 128:(c + 1) * 128], identity=idb),
                      reads=[("obn", qb), "idb"], writes=[("ps", 7)])
                A("act", lambda e, qb=qb: e.copy(out=boT[:, :, qb * 128:(qb + 1) * 128],
                                                 in_=psbank_bf(7)[:, 0:512].rearrange("p (c t) -> p c t", c=4)),
                  reads=[("ps", 7)], writes=[("boT", qb)])
            wpa_v = H["w_proj_a"].ap().rearrange("(k p) c -> p k c", p=128)
            wpb_v = H["w_proj_b"].ap().rearrange("(k p) c -> p k c", p=128)
            atres = [("AT", j) for j in range(4)]
            bores = [("boT", j) for j in range(4)]
            for oc in range(8):
                pb0 = 4 * (oc % 2)
                wi = cnt["wp"] % 2
                cnt["wp"] += 1
                wb = wp[wi]
                w_a = wb[:, 0:512].rearrange("p (k c) -> p k c", k=4)
                w_b = wb[:, 512:1024].rearrange("p (k c) -> p k c", k=4)
                w_ga = wb[:, 1024:2048].rearrange("p (k c) -> p k c", k=8)
                w_gb = wb[:, 2048:3072].rearrange("p (k c) -> p k c", k=8)
                tload(3 + oc, wi, [(w_a, wpa_v[:, :, oc * 128:(oc + 1) * 128], "wpa"),
                                   (w_b, wpb_v[:, :, oc * 128:(oc + 1) * 128], "wpb"),
                                   (w_ga, wmix[:, :, 2560 + oc * 128:2560 + (oc + 1) * 128], "wpc"),
                                   (w_gb, wmix[:, :, 3584 + oc * 128:3584 + (oc + 1) * 128], "wpd")])
                for k in range(4):
                    A("pe", lambda e, k=k, w_a=w_a, pb0=pb0: e.matmul(out=psbank(pb0 + 0), lhsT=w_a[:, k, :], rhs=AT[:, k, :], start=(k == 0),
                                                             stop=(k == 3)), reads=[("wpa", wi), ("wpR", wi)] + atres, writes=[("ps", pb0 + 0)])
                for k in range(4):
                    A("pe", lambda e, k=k, w_b=w_b, pb0=pb0: e.matmul(out=psbank(pb0 + 1), lhsT=w_b[:, k, :], rhs=boT[:, k, :], start=(k == 0),
                                                             stop=(k == 3)), reads=[("wpb", wi), ("wpR", wi)] + bores, writes=[("ps", pb0 + 1)])
                for k in range(8):
                    A("pe", lambda e, k=k, w_ga=w_ga, pb0=pb0: e.matmul(out=psbank(pb0 + 2), lhsT=w_ga[:, k, :], rhs=hT[:, k, :], start=(k == 0),
                                                               stop=(k == 7)), reads=[("wpc", wi), ("wpR", wi)] + hres, writes=[("ps", pb0 + 2)])
                for k in range(8):
                    A("pe", lambda e, k=k, w_gb=w_gb, pb0=pb0: e.matmul(out=psbank(pb0 + 3), lhsT=w_gb[:, k, :], rhs=hT[:, k, :], start=(k == 0),
                                                               stop=(k == 7)), reads=[("wpd", wi), ("wpR", wi)] + hres, writes=[("ps", pb0 + 3)])
                A("act", lambda e, oc=oc, pb0=pb0: e.activation(out=t_sa, in_=psbank(pb0 + 2), func=AF.Sigmoid, bias=gbias[:, oc:oc + 1]),
                  reads=[("ps", pb0 + 2), "gbias"], writes=["sa"])
                A("act", lambda e, oc=oc, pb0=pb0: e.activation(out=t_sb, in_=psbank(pb0 + 3), func=AF.Sigmoid, bias=gbias[:, 8 + oc:9 + oc]),
                  reads=[("ps", pb0 + 3), "gbias"], writes=["sb"])
                A("dve", lambda e, pb0=pb0: e.tensor_tensor(out=t_m1, in0=psbank(pb0 + 0), in1=t_sa, op=ALU.mult), reads=[("ps", pb0 + 0), "sa"],
                  writes=["sa"])
                A("dve", lambda e, pb0=pb0: e.tensor_tensor(out=t_m2, in0=psbank(pb0 + 1), in1=t_sb, op=ALU.mult), reads=[("ps", pb0 + 1), "sb"],
                  writes=["sb"])
                A("dve", lambda e, oc=oc: e.tensor_tensor(out=mT[:, oc, :], in0=t_m1, in1=t_m2, op=ALU.add), reads=["sa", "sb"],
                  writes=[("mT", oc)])
            wo_v = H["w_out"].ap().rearrange("(k p) c -> p k c", p=128)
            mres = [("mT", oc) for oc in range(8)]
            for half in range(2):
                wi = cnt["wp"] % 2
                cnt["wp"] += 1
                w = wp[wi].rearrange("p (k c) -> p k c", k=8)
                tload(11 + half, wi, [(w, wo_v[:, :, half * 512:(half + 1) * 512], "wpa")])
                for j in range(4):
                    pb = 4 + (j % 2)
                    for k in range(8):
                        A("pe", lambda e, k=k, j=j, w=w, pb=pb: e.matmul(out=psbank(pb), lhsT=mT[:, k, j * 128:(j + 1) * 128],
                                                                          rhs=w[:, k, :], start=(k == 0), stop=(k == 7)),
                          reads=[("wpa", wi), ("wpR", wi)] + mres, writes=[("ps", pb)])
                    b = blks[j]
                    A("dve", lambda e, b=b, pb=pb, half=half: e.tensor_tensor(
                        out=xres[:, b, half * 512:(half + 1) * 512], in0=psbank(pb), in1=xres[:, b, half * 512:(half + 1) * 512],
                        op=ALU.add), reads=[("ps", pb), ("xr", b)], writes=[("xr", b)])
            sq_blocks(blks, t_junk)

        def final_store(dst_rows):
            v = dst_rows.rearrange("(b p) d -> p b d", p=128)
            pr = [("pss", c) for c in range(4)]
            A("act", lambda e: e.activation(out=prs, in_=pss, func=AF.Ln, scale=1.0 / D, bias=epsc), reads=pr + ["epsc"],
              writes=[("prs", c) for c in range(4)])
            A("act", lambda e: e.activation(out=prs, in_=prs, func=AF.Exp, scale=-0.5), reads=[("prs", c) for c in range(4)],
              writes=[("prs", c) for c in range(4)])
            for b in range(16):
                A("dve", lambda e, b=b: e.scalar_tensor_tensor(out=xres[:, b, :], in0=xres[:, b, :], scalar=prs[:, b:b + 1],
                                                               in1=gfin, op0=ALU.mult, op1=ALU.mult),
                  reads=[("xr", b), ("prs", b // 4), "gfin"], writes=[("xr", b)])
            for b4 in range(4):
                A("sp", lambda e, b4=b4: e.dma_start(out=v[:, b4 * 4:(b4 + 1) * 4, :], in_=xres[:, b4 * 4:(b4 + 1) * 4, :]),
                  reads=[("xr", b) for b in range(b4 * 4, b4 * 4 + 4)], writes=[("OUT", cnt["x"], b4)], dma_ch=("o", b4))
                out_res.append(("OUT", cnt["x"], b4))
            cnt["x"] += 1

        out_res = []

        def unit(src, dst, has_oth, preloaded=False):
            load_x(src, part="sq" if preloaded else "both")
            ffn(H["ffn1_w_in"], H["ffn1_w_out"], 0)
            mix_kv()
            barrier()
            for qt in range(4):
                tail(qt, has_oth)
                tail_no[0] += 1
            barrier()
            ffn(H["ffn2_w_in"], H["ffn2_w_out"], 2)
            final_store(dst)

        mode = debug or "full"
        if mode == "full":
            unit(H["xp"].ap()[0], yp.ap()[0], False, preloaded=True)
            unit(H["xp"].ap()[1], yp.ap()[1], False)
            load_x(H["xsx"].ap())
            ffn(H["ffn1_w_in"], H["ffn1_w_out"], 0)
            mix_kv()
            spill_kv()
            unit(H["xso"].ap(), ys.ap(), True)
        elif mode == "p0":
            unit(H["xp"].ap()[0], yp.ap()[0], False)
        elif mode == "s":
            load_x(H["xsx"].ap())
            ffn(H["ffn1_w_in"], H["ffn1_w_out"], 0)
            mix_kv()
            spill_kv()
            unit(H["xso"].ap(), ys.ap(), True)
        A("sp", None, reads=out_res)
        P.emit(st)
    return nc


def _bucket_table():
    rel = np.arange(-255, 256, dtype=np.int32)
    half = 16
    max_exact = 8
    try:
        import jax
        import jax.numpy as jnp
        with jax.default_device(jax.devices("cpu")[0]):
            r = jnp.asarray(rel)
            bucket = jnp.where(r > 0, half, 0).astype(jnp.int32)
            n = jnp.abs(r)
            nf = jnp.maximum(n, 1).astype(jnp.float32)
            large = max_exact + (jnp.log(nf / max_exact) / math.log(128 / max_exact) * (half - max_exact)).astype(jnp.int32)
            large = jnp.minimum(large, half - 1)
            out = np.asarray(bucket + jnp.where(n < max_exact, n, large))
        return rel, out.astype(np.int64)
    except Exception:
        bucket = np.where(rel > 0, half, 0)
        n = np.abs(rel)
        nf = np.maximum(n, 1).astype(np.float32)
        large = max_exact + (np.log(nf / np.float32(max_exact)) / np.float32(math.log(128 / max_exact))
                             * np.float32(half - max_exact)).astype(np.int32)
        large = np.minimum(large, half - 1)
        return rel, (bucket + np.where(n < max_exact, n, large)).astype(np.int64)


_CACHE = {}


def kernel(**inputs):
    f = lambda k: np.ascontiguousarray(np.asarray(inputs[k], dtype=np.float32))
    xp_all = f("x_prompt")
    xs_all = f("x_sample")
    shared = {}
    for k in ("ffn1_norm", "mix_norm", "ffn2_norm", "final_norm", "gate_bias", "sgu_norm", "q_norm", "k_norm",
              "lambda_q1", "lambda_k1", "lambda_q2", "lambda_k2", "diff_subln"):
        shared[k] = f(k).reshape(1, -1)
    for k in ("ffn1_w_in", "ffn1_w_out", "w_in", "sgu_w", "sgu_b", "w_proj_a", "w_proj_b", "w_out", "ffn2_w_in", "ffn2_w_out"):
        shared[k] = np.ascontiguousarray(f(k)[0])
    shared["rel_bias"] = f("rel_bias")
    shared["ident"] = np.eye(128, dtype=np.float32)
    bo = np.zeros((128, 128), np.float32)
    bo[:64, :64] = 1.0 / 64
    bo[64:, 64:] = 1.0 / 64
    shared["bones"] = bo
    rel, bk = _bucket_table()
    oh = np.zeros((32, 512), np.float32)
    for i in range(511):
        r = 255 - i
        oh[bk[r + 255], i] = 1.0
    shared["oh"] = oh
    in_maps = []
    for c in range(8):
        m = dict(shared)
        m["xp"] = np.ascontiguousarray(xp_all[2 * c:2 * c + 2])
        s = c // 2
        p = c % 2
        m["xso"] = np.ascontiguousarray(xs_all[s, p * 2048:(p + 1) * 2048])
        m["xsx"] = np.ascontiguousarray(xs_all[s, (1 - p) * 2048:(2 - p) * 2048])
        par = np.zeros((128, 2), np.float32)
        par[:, 0] = 1.0 - p
        par[:, 1] = float(p)
        m["par"] = par
        in_maps.append(m)
    if "nc" not in _CACHE:
        _CACHE["nc"] = build_program()
    res = run_bass_kernel_spmd(_CACHE["nc"], in_maps, core_ids=list(range(8)))
    y_prompt = np.empty((16, 2048, D), np.float32)
    y_sample = np.empty((4, 4096, D), np.float32)
    for c in range(8):
        r = res.results[c]
        y_prompt[2 * c:2 * c + 2] = r["yp"]
        s = c // 2
        p = c % 2
        y_sample[s, p * 2048:(p + 1) * 2048] = r["ys"]
    return (y_prompt, y_sample)
```

```python
import math
import numpy as np
import concourse.bass as bass
import concourse.mybir as mybir
from concourse.bass_utils import run_bass_kernel_spmd
from contextlib import ExitStack

F32 = mybir.dt.float32
BF16 = mybir.dt.bfloat16
AF = mybir.ActivationFunctionType
ALU = mybir.AluOpType
AX = mybir.AxisListType

D = 1024
DFF = 2816
NCH = 22
EPS = 1e-6
SQRT_GC = math.sqrt(0.044715)
GELU_S = 2.0 * math.sqrt(2.0 / math.pi)
LAMBDA_INIT = 0.8 - 0.6 * math.exp(0.0)


class Op:
    __slots__ = ("eng", "fn", "deps", "is_dma", "ch", "ch_idx", "signaled", "tick")

    def __init__(self, eng, fn, is_dma=False, ch=None):
        self.eng = eng
        self.fn = fn
        self.deps = []
        self.is_dma = is_dma
        self.ch = ch
        self.ch_idx = 0
        self.signaled = False
        self.tick = 0


class Prog:
    ENGS = ["pe", "act", "dve", "pool", "sp"]

    def __init__(self, nc):
        self.nc = nc
        self.ops = {e: [] for e in self.ENGS}
        self.last_w = {}
        self.readers = {}
        self.ch_ops = {}
        self.cur_barrier = None

    def add(self, eng, fn, reads=(), writes=(), dma_ch=None, war=()):
        op = Op(eng, fn, is_dma=dma_ch is not None, ch=dma_ch)
        deps = set()
        if self.cur_barrier is not None:
            deps.add(self.cur_barrier)
        for r in war:
            rd = self.readers.get(r)
            if rd:
                deps.update(rd.values())
        for r in reads:
            w = self.last_w.get(r)
            if w is not None:
                deps.add(w)
        for r in writes:
            w = self.last_w.get(r)
            if w is not None:
                deps.add(w)
            rd = self.readers.get(r)
            if rd:
                deps.update(rd.values())
        if dma_ch is not None:
            lst = self.ch_ops.setdefault(dma_ch, [])
            if lst:
                deps.add(lst[-1])
            lst.append(op)
            op.ch_idx = len(lst)
        for d in deps:
            if d.is_dma:
                op.deps.append(d)
                continue
            if d.eng == eng and eng == "pe" and not op.is_dma:
                continue
            d.signaled = True
            op.deps.append(d)
        for r in reads:
            key = ("d", dma_ch) if op.is_dma else eng
            self.readers.setdefault(r, {})[key] = op
        for r in writes:
            self.last_w[r] = op
            self.readers[r] = {}
        self.ops[eng].append(op)
        return op

    def barrier(self, eng, fn):
        op = Op(eng, fn)
        for e in self.ENGS:
            for o in reversed(self.ops[e]):
                if not o.is_dma and o.fn is not None:
                    if not (e == eng and eng == "pe"):
                        o.signaled = True
                        op.deps.append(o)
                    break
        for ch, lst in self.ch_ops.items():
            if lst:
                op.deps.append(lst[-1])
        op.signaled = True
        self.ops[eng].append(op)
        self.cur_barrier = op
        self.last_w = {}
        self.readers = {}
        return op

    def emit(self, stack):
        nc = self.nc
        esem = {}
        for e in self.ENGS:
            if e == "sp":
                continue
            esem[e] = stack.enter_context(nc.semaphore("S_" + e))
        chsem = {}
        for i, ch in enumerate(self.ch_ops):
            chsem[ch] = stack.enter_context(nc.semaphore("D%d" % i))
        for e in self.ENGS:
            t = 0
            for op in self.ops[e]:
                if op.signaled and not op.is_dma:
                    t += 1
                    op.tick = t
        block = stack.enter_context(nc.Block())
        ops = self.ops

        def run(e, eng):
            waited = {}
            for op in ops[e]:
                need = {}
                for d in op.deps:
                    if d.is_dma:
                        key = ("d", d.ch)
                        val = 16 * d.ch_idx
                    else:
                        key = ("e", d.eng)
                        val = d.tick
                    if val > need.get(key, 0):
                        need[key] = val
                for key, val in need.items():
                    if val <= waited.get(key, 0):
                        continue
                    waited[key] = val
                    sem = chsem[key[1]] if key[0] == "d" else esem[key[1]]
                    eng.wait_ge(sem, val)
                if op.fn is None:
                    continue
                ins = op.fn(eng)
                if op.is_dma:
                    ins.then_inc(chsem[op.ch], 16)
                elif op.signaled:
                    ins.then_inc(esem[e], 1)

        @block.tensor
        def _(eng):
            run("pe", eng)

        @block.scalar
        def _(eng):
            run("act", eng)

        @block.vector
        def _(eng):
            run("dve", eng)

        @block.gpsimd
        def _(eng):
            run("pool", eng)

        @block.sync
        def _(eng):
            run("sp", eng)


ARENA_BYTES = 210944


def build_program(debug=None):
    nc = bass.Bass("TRN2", target_bir_lowering=False)
    st = ExitStack()

    def din(name, shape):
        return nc.dram_tensor(name, list(shape), F32, kind="ExternalInput")

    H = {}
    for name, shape in [
        ("xp", (2, 2048, D)), ("xso", (2048, D)), ("xsx", (2048, D)), ("par", (128, 2)),
        ("rel_bias", (32, 4)), ("ffn1_norm", (1, D)), ("ffn1_w_in", (D, 2 * DFF)), ("ffn1_w_out", (DFF, D)),
        ("mix_norm", (1, D)), ("w_in", (D, 4608)), ("gate_bias", (1, 2048)), ("sgu_norm", (1, 512)),
        ("sgu_w", (8, 128, 128)), ("sgu_b", (8, 128)), ("q_norm", (1, 64)), ("k_norm", (1, 64)),
        ("lambda_q1", (1, 64)), ("lambda_k1", (1, 64)), ("lambda_q2", (1, 64)), ("lambda_k2", (1, 64)),
        ("diff_subln", (1, 128)), ("w_proj_a", (512, D)), ("w_proj_b", (512, D)), ("w_out", (D, D)),
        ("ffn2_norm", (1, D)), ("ffn2_w_in", (D, 2 * DFF)), ("ffn2_w_out", (DFF, D)), ("final_norm", (1, D)),
        ("ident", (128, 128)), ("bones", (128, 128)), ("oh", (32, 512)),
    ]:
        H[name] = din(name, shape)
    yp = nc.dram_tensor("yp", [2, 2048, D], F32, kind="ExternalOutput")
    ys = nc.dram_tensor("ys", [2048, D], F32, kind="ExternalOutput")
    ktx = nc.dram_tensor("ktx", [128, 4 * 2048], BF16)
    vx = nc.dram_tensor("vx", [128, 16 * 516], BF16)
    gscr = nc.dram_tensor("gscr", [4, 128, 512], F32)
    wscr = nc.dram_tensor("wscr", [128, 13 * 4096], BF16)

    def bc(h, n, off=0):
        return bass.AP(h, off, [[0, 128], [1, n]])

    with st:
        P = Prog(nc)
        arena = st.enter_context(nc.sbuf_tensor("arena", [128, ARENA_BYTES // 2], BF16))
        PS = [st.enter_context(nc.psum_tensor("ps%d" % i, [128, 1024], F32)) for i in range(4)]

        class Alloc:
            def __init__(self, base, limit):
                self.off = base
                self.limit = limit

            def __call__(self, dtype, shape):
                esz = 4 if dtype == F32 else 2
                n = 1
                for s in shape:
                    n *= s
                nbytes = (n * esz + 63) // 64 * 64
                off = self.off
                self.off += nbytes
                assert self.off <= self.limit, ("arena overflow", self.off, self.limit)
                a = arena[:, off // 2: off // 2 + n * esz // 2]
                if dtype == F32:
                    a = a.bitcast(F32)
                if len(shape) == 2:
                    a = a.rearrange("p (a b) -> p a b", a=shape[0])
                elif len(shape) == 3:
                    a = a.rearrange("p (a b c) -> p a b c", a=shape[0], b=shape[1])
                return a

        fx = Alloc(0, ARENA_BYTES)
        xres = fx(F32, [16, D])
        KTo = fx(BF16, [4, 2048])
        Vo = fx(BF16, [16, 4, 129])
        wp = [fx(BF16, [4096]) for _ in range(2)]
        idb = fx(BF16, [128])
        bones = fx(BF16, [128])
        gcol = fx(F32, [3, 8])
        gfin = fx(F32, [D])
        gsgu = fx(F32, [512])
        gsub = fx(F32, [128])
        gq = fx(F32, [1])
        gk = fx(F32, [1])
        gbias = fx(F32, [16])
        bsT = fx(F32, [8])
        wsT = fx(BF16, [8, 128])
        nlam = fx(F32, [1])
        cpos = fx(F32, [4])
        cneg = fx(F32, [4])
        coth = fx(F32, [4])
        epsc = fx(F32, [1])
        parc = fx(F32, [2])
        pss = fx(F32, [16])
        prs = fx(F32, [16])
        strips = {}
        for side in ("neg", "pos"):
            for part in ("hi", "lo"):
                strips[(side, part)] = fx(BF16, [4, 384])
        bstr = {}
        for nm in ("A", "B"):
            for part in ("hi", "lo"):
                bstr[(nm, part)] = fx(BF16, [4, 128])
        RBASE = fx.off
        RLIM = ARENA_BYTES

        def psbank(i):
            return PS[i // 2][:, (i % 2) * 512:(i % 2) * 512 + 512]

        def psbank_bf(i):
            return psbank(i).bitcast(BF16)

        tmp = Alloc(RBASE, RLIM)
        t_rb = tmp(F32, [4])
        t_ones = tmp(F32, [128])
        t_rbB = tmp(F32, [4, 128])
        t_oh = tmp(F32, [512])
        t_g = tmp(F32, [4, 512])
        t_T = tmp(F32, [4, 384])
        t_Ts = tmp(F32, [384])
        t_hi = tmp(BF16, [384])
        t_w = tmp(F32, [8, 128])
        t_wb = tmp(BF16, [8, 128])
        t_l = tmp(F32, [4, 64])
        t_lp = tmp(F32, [2, 64])
        t_ls = tmp(F32, [2])
        t_le = tmp(F32, [2])

        A = P.add
        A("pool", lambda e: e.dma_start(out=idb, in_=H["ident"].ap()), writes=["idb"], dma_ch="c0")
        A("pool", lambda e: e.dma_start(out=bones, in_=H["bones"].ap()), writes=["bones"], dma_ch="c1")
        for i, nm in enumerate(("ffn1_norm", "mix_norm", "ffn2_norm")):
            A("sp", lambda e, i=i, nm=nm: e.dma_start(
                out=gcol[:, i, :], in_=H[nm].ap().rearrange("o (k p) -> p (o k)", p=128),
                allow_slow_non_contiguous=True), writes=["gcol"], dma_ch="c2")
        A("sp", lambda e: e.dma_start(out=gbias, in_=H["gate_bias"].ap().rearrange("o (k p) -> p (o k)", p=128),
                                      allow_slow_non_contiguous=True), writes=["gbias"], dma_ch="c3")
        A("sp", lambda e: e.dma_start(out=bsT, in_=H["sgu_b"].ap().rearrange("g t -> t g"),
                                      allow_slow_non_contiguous=True), writes=["bsT"], dma_ch="c3")
        A("sp", lambda e: e.dma_start(out=gfin, in_=bc(H["final_norm"], D)), writes=["gfin"], dma_ch="c4")
        A("sp", lambda e: e.dma_start(out=gsgu, in_=bc(H["sgu_norm"], 512)), writes=["gsgu"], dma_ch="c4")
        A("sp", lambda e: e.dma_start(out=gsub, in_=bc(H["diff_subln"], 128)), writes=["gsub"], dma_ch="c4")
        A("sp", lambda e: e.dma_start(out=parc, in_=H["par"].ap()), writes=["parc"], dma_ch="c4")
        for half in range(2):
            A("sp", lambda e, half=half: e.dma_start(
                out=gq[half * 64:(half + 1) * 64, :], in_=H["q_norm"].ap().rearrange("o d -> d o")),
                writes=["gq"], dma_ch="c5")
            A("sp", lambda e, half=half: e.dma_start(
                out=gk[half * 64:(half + 1) * 64, :], in_=H["k_norm"].ap().rearrange("o d -> d o")),
                writes=["gk"], dma_ch="c5")
        A("dve", lambda e: e.tensor_scalar(out=gq, in0=gq, scalar1=0.125, scalar2=None, op0=ALU.mult),
          reads=["gq"], writes=["gq"])
        A("dve", lambda e: e.tensor_scalar(out=gsub, in0=gsub, scalar1=1.0 - LAMBDA_INIT, scalar2=None, op0=ALU.mult),
          reads=["gsub"], writes=["gsub"])
        A("dve", lambda e: e.memset(epsc, EPS), writes=["epsc"])
        A("dve", lambda e: e.memset(Vo[:, :, :, 128:129], 1.0), writes=[("V", b) for b in range(16)])
        A("sp", lambda e: e.dma_start(out=cpos, in_=bc(H["rel_bias"], 4, off=31 * 4)), writes=["cpos"], dma_ch="c6")
        A("sp", lambda e: e.dma_start(out=cneg, in_=bc(H["rel_bias"], 4, off=15 * 4)), writes=["cneg"], dma_ch="c6")
        A("dve", lambda e: e.tensor_scalar(out=coth, in0=cpos, scalar1=parc[:, 0:1], scalar2=None, op0=ALU.mult),
          reads=["cpos", "parc"], writes=["coth"])
        A("dve", lambda e: e.scalar_tensor_tensor(out=coth, in0=cneg, scalar=parc[:, 1:2], in1=coth,
                                                  op0=ALU.mult, op1=ALU.add),
          reads=["cneg", "parc", "coth"], writes=["coth"])
        for i, nm in enumerate(("lambda_q1", "lambda_k1", "lambda_q2", "lambda_k2")):
            A("sp", lambda e, i=i, nm=nm: e.dma_start(out=t_l[:, i, :], in_=bc(H[nm], 64)), writes=["t_l"], dma_ch="c7")
        A("dve", lambda e: e.tensor_tensor(out=t_lp[:, 0, :], in0=t_l[:, 0, :], in1=t_l[:, 1, :], op=ALU.mult),
          reads=["t_l"], writes=["t_lp"])
        A("dve", lambda e: e.tensor_tensor(out=t_lp[:, 1, :], in0=t_l[:, 2, :], in1=t_l[:, 3, :], op=ALU.mult),
          reads=["t_l", "t_lp"], writes=["t_lp"])
        A("dve", lambda e: e.reduce_sum(out=t_ls, in_=t_lp, axis=AX.X), reads=["t_lp"], writes=["t_ls"])
        A("act", lambda e: e.activation(out=t_le, in_=t_ls, func=AF.Exp), reads=["t_ls"], writes=["t_le"])
        A("dve", lambda e: e.scalar_tensor_tensor(out=nlam, in0=t_le[:, 1:2], scalar=-LAMBDA_INIT, in1=t_le[:, 0:1],
                                                  op0=ALU.add, op1=ALU.subtract),
          reads=["t_le"], writes=["nlam"])
        A("sp", lambda e: e.dma_start(out=t_w, in_=H["sgu_w"].ap().rearrange("g t s -> t g s")), writes=["t_w"], dma_ch="c8")
        A("dve", lambda e: e.tensor_copy(out=t_wb, in_=t_w), reads=["t_w"], writes=["t_wb"])
        for g in range(8):
            A("pe", lambda e, g=g: e.transpose(out=psbank_bf(0)[:, g * 128:(g + 1) * 128], in_=t_wb[:, g, :], identity=idb),
              reads=["t_wb", "idb"], writes=[("ps", 0)])
        A("dve", lambda e: e.tensor_copy(out=wsT, in_=psbank_bf(0).rearrange("p (g t) -> p g t", g=8)),
          reads=[("ps", 0)], writes=["wsT"])
        A("sp", lambda e: e.dma_start(out=t_rb[0:32, :], in_=H["rel_bias"].ap()), writes=["t_rb"], dma_ch="c9")
        A("sp", lambda e: e.dma_start(out=t_oh[0:32, :], in_=H["oh"].ap()), writes=["t_oh"], dma_ch="c9")
        A("dve", lambda e: e.memset(t_ones, 1.0), writes=["t_ones"])
        for h in range(4):
            A("dve", lambda e, h=h: e.tensor_scalar(out=t_rbB[0:32, h, :], in0=t_ones[0:32, :], scalar1=t_rb[0:32, h:h + 1],
                                                    scalar2=None, op0=ALU.mult),
              reads=["t_ones", "t_rb"], writes=[("t_rbB", h)])
            A("pe", lambda e, h=h: e.matmul(out=psbank(2 + (h % 2)), lhsT=t_rbB[0:32, h, :], rhs=t_oh[0:32, :],
                                            start=True, stop=True),
              reads=[("t_rbB", h), "t_oh"], writes=[("ps", 2 + (h % 2))])
            A("act", lambda e, h=h: e.copy(out=t_g[:, h, :], in_=psbank(2 + (h % 2))),
              reads=[("ps", 2 + (h % 2))], writes=[("t_g", h)])
            A("sp", lambda e, h=h: e.dma_start(out=gscr.ap()[h], in_=t_g[:, h, :]), reads=[("t_g", h)],
              writes=[("gscr", h)], dma_ch="c10")
            A("sp", lambda e, h=h: e.dma_start(out=t_T[:, h, :],
                                               in_=bass.AP(gscr, h * 128 * 512 + 127, [[511, 128], [1, 384]])),
              reads=[("gscr", h)], writes=[("t_T", h)], dma_ch="c11")

            def mk_strip(src, ccol, mask, dst_hi, dst_lo, h=h):
                n = src.shape[-1]
                A("dve", lambda e: e.tensor_scalar(out=t_Ts[:, 0:n], in0=src, scalar1=ccol, scalar2=None, op0=ALU.subtract),
                  reads=[("t_T", h), "cpos", "cneg"], writes=["t_Ts"])
                if mask is not None:
                    A("dve", lambda e: e.tensor_scalar(out=t_Ts[:, 0:n], in0=t_Ts[:, 0:n], scalar1=mask, scalar2=None,
                                                       op0=ALU.mult),
                      reads=["t_Ts", "parc"], writes=["t_Ts"])
                A("dve", lambda e: e.tensor_copy(out=dst_hi, in_=t_Ts[:, 0:n]), reads=["t_Ts"], writes=["strips", "t_hi"])
                A("dve", lambda e: e.tensor_tensor(out=dst_lo, in0=t_Ts[:, 0:n], in1=dst_hi, op=ALU.subtract),
                  reads=["t_Ts", "strips", "t_hi"], writes=["strips"])

            mk_strip(t_T[:, h, :], cneg[:, h:h + 1], None, strips[("neg", "hi")][:, h, :], strips[("neg", "lo")][:, h, :])
            mk_strip(t_T[:, h, :], cpos[:, h:h + 1], None, strips[("pos", "hi")][:, h, :], strips[("pos", "lo")][:, h, :])
            mk_strip(t_T[:, h, 0:128], cpos[:, h:h + 1], parc[:, 0:1], bstr[("A", "hi")][:, h, :], bstr[("A", "lo")][:, h, :])
            mk_strip(t_T[:, h, 256:384], cneg[:, h:h + 1], parc[:, 1:2], bstr[("B", "hi")][:, h, :], bstr[("B", "lo")][:, h, :])

        bar_t = fx(F32, [1]) if False else epsc

        def barrier():
            P.barrier("dve", lambda e: e.memset(epsc, EPS))

        if (debug or "full") == "full":
            v0 = H["xp"].ap()[0].rearrange("(b p) d -> p b d", p=128)
            for b4 in range(4):
                A("sp", lambda e, b4=b4: e.dma_start(out=xres[:, b4 * 4:(b4 + 1) * 4, :], in_=v0[:, b4 * 4:(b4 + 1) * 4, :]),
                  writes=[("xr", b) for b in range(b4 * 4, b4 * 4 + 4)], dma_ch=("x", b4))
        barrier()

        fv = Alloc(RBASE, RLIM)
        xnT = fv(BF16, [8, 2048])
        actT = fv(BF16, [4, 2048])
        wo = fv(BF16, [4, D])
        f_sg = [fv(F32, [512]) for _ in range(2)]
        f_xnb = [fv(BF16, [D]) for _ in range(2)]
        f_junk = fv(BF16, [D])
        f_ss = fv(F32, [16])
        f_rs = fv(F32, [16])
        print("ffn view end", fv.off, RLIM)
        _al = Alloc(RBASE + 32768, RBASE + 32768 + 16384)
        f_sq = [_al(BF16, [512]) for _ in range(2)]
        f_krs = [_al(F32, [512]) for _ in range(2)]

        tv = Alloc(RBASE, RLIM)
        hT = tv(BF16, [8, 512])
        QT = tv(BF16, [4, 512])
        PT2 = [tv(BF16, [1024]) for _ in range(2)]
        PT = [PT2[i // 2][:, (i % 2) * 512:(i % 2) * 512 + 512] for i in range(4)]
        kring = [tv(BF16, [1024]) for _ in range(2)]
        vring = [tv(BF16, [8, 129]) for _ in range(2)]
        t_gg = tv(F32, [2 * D])
        t_ga = t_gg[:, 0:D]
        t_gu = t_gg[:, D:2 * D]
        o_sb = t_gg[:, 0:1032].rearrange("p (c q d) -> p c q d", c=2, q=4)
        t_o0 = t_gg[:, 1032:1544].rearrange("p (q d) -> p q d", q=4)
        t_junk = tv(BF16, [D])
        t_vn2 = [tv(BF16, [512]) for _ in range(2)]
        t_s1 = t_ga[:, 0:512]
        t_A2 = [tv(BF16, [512]) for _ in range(2)]
        AT = tv(BF16, [4, 512])
        t_r8 = tv(F32, [8])
        obn = tv(BF16, [4, 512])
        boT = tv(BF16, [4, 512])
        t_sa = tv(F32, [512])
        t_sb = tv(F32, [512])
        t_m1 = t_sa
        t_m2 = t_sb
        mT = tv(BF16, [8, 512])
        t_sq2 = [tv(BF16, [512]), t_A2[1]]
        t_rs2 = [tv(F32, [512]), t_sb]
        q_sqres = [("qsq", 0), ("A", 1)]
        q_rsres = [("qrs", 0), "sb"]
        t_xnb = tv(BF16, [D])
        t_ss = tv(F32, [16])
        t_r = tv(F32, [16])

        print("tail view end", tv.off, RLIM)
        cnt = {"x": 0, "wp": 0, "ps": 0, "o": 0}

        def load_x(src_rows, part="both"):
            v = src_rows.rearrange("(b p) d -> p b d", p=128)
            if part != "sq":
                for b4 in range(4):
                    A("sp", lambda e, b4=b4: e.dma_start(out=xres[:, b4 * 4:(b4 + 1) * 4, :], in_=v[:, b4 * 4:(b4 + 1) * 4, :]),
                      writes=[("xr", b) for b in range(b4 * 4, b4 * 4 + 4)], dma_ch=("x", b4))
            if part != "dma":
                for b4 in range(4):
                    sq_blocks(list(range(b4 * 4, b4 * 4 + 4)), f_junk)

        def sq_blocks(blks, junk):
            c = blks[0] // 4
            A("dve", lambda e: e.memset(pss[:, blks[0]:blks[-1] + 1], 0.0), writes=[("pss", c)])
            for b in blks:
                A("act", lambda e, b=b: e.activation(out=junk, in_=xres[:, b, :], func=AF.Square, accum_out=pss[:, b:b + 1]),
                  reads=[("xr", b), ("pss", c)], writes=["njunk", ("pss", c)])

        def norm_to_T(blks, gi, dstT, xnb, tbank, col0=0, reuse_rs=False):
            c = blks[0] // 4
            lo, hi = blks[0], blks[-1] + 1
            if not reuse_rs:
                A("act", lambda e: e.activation(out=prs[:, lo:hi], in_=pss[:, lo:hi], func=AF.Ln, scale=1.0 / D, bias=epsc),
                  reads=[("pss", c), "epsc"], writes=[("prs", c)])
                A("act", lambda e: e.activation(out=prs[:, lo:hi], in_=prs[:, lo:hi], func=AF.Exp, scale=-0.5),
                  reads=[("prs", c)], writes=[("prs", c)])
            for i, b in enumerate(blks):
                xb = xnb[i % len(xnb)]
                rn = ("nxnb", i % len(xnb))
                A("act", lambda e, b=b, xb=xb: e.mul(out=xb, in_=xres[:, b, :], mul=prs[:, b:b + 1]),
                  reads=[("xr", b), ("prs", c)], writes=[rn])
                bank = tbank[i % len(tbank)]
                for k in range(8):
                    A("pe", lambda e, k=k, xb=xb, bank=bank: e.transpose(
                        out=psbank_bf(bank)[:, k * 128:(k + 1) * 128], in_=xb[:, k * 128:(k + 1) * 128], identity=idb),
                      reads=[rn, "idb"], writes=[("ps", bank)])
                cc = col0 + i * 128
                A("dve", lambda e, bank=bank, cc=cc: e.tensor_tensor(
                    out=dstT[:, :, cc:cc + 128], in0=psbank_bf(bank).rearrange("p (k t) -> p k t", k=8),
                    in1=gcol[:, gi, :].unsqueeze(2).to_broadcast([128, 8, 128]), op=ALU.mult),
                  reads=[("ps", bank), "gcol"], writes=[("xT", (col0 // 128) + i)])

        def wload(dst, src, res, chname):
            war = [("wpR", res[1])] if isinstance(res, tuple) else []
            A("pool", lambda e: e.dma_start(out=dst, in_=src), writes=[res], dma_ch=chname, war=war)

        def ffn(w_in_h, w_out_h, gi):
            win = w_in_h.ap().rearrange("(k p) c -> p k c", p=128)
            wout = w_out_h.ap().rearrange("(k p) c -> p k c", p=128)
            for ch in range(4):
                norm_to_T(list(range(4 * ch, 4 * ch + 4)), gi, xnT, f_xnb, [6, 7], col0=512 * ch)
            groups = [(0, 4), (4, 4), (8, 4), (12, 4), (16, 4), (20, 2)]
            for (c0, n) in groups:
                for pc in range(n // 2):
                    wi = cnt["wp"] % 2
                    cnt["wp"] += 1
                    w = wp[wi].rearrange("p (k c) -> p k c", k=8)
                    cc = (c0 + 2 * pc) * 128
                    wload(w[:, :, 0:256], win[:, :, cc:cc + 256], ("wpa", wi), ("wpa", wi))
                    wload(w[:, :, 256:512], win[:, :, DFF + cc:DFF + cc + 256], ("wpb", wi), ("wpb", wi))
                    if pc == n // 2 - 1:
                        wload(wo[:, 0:n, :], wout[:, c0:c0 + n, :], "wo", "wo")
                    for pr in range(2):
                        slot = 2 * pc + pr
                        for t in range(4):
                            pi = cnt["ps"] % 2
                            cnt["ps"] += 1
                            bg, bu = 2 * pi, 2 * pi + 1
                            for k in range(8):
                                A("pe", lambda e, k=k, w=w, pr=pr, t=t, bg=bg: e.matmul(
                                    out=psbank(bg), lhsT=w[:, k, pr * 128:(pr + 1) * 128],
                                    rhs=xnT[:, k, t * 512:(t + 1) * 512], start=(k == 0), stop=(k == 7)),
                                  reads=[("wpa", wi), ("wpR", wi)] + [("xT", 4 * t + j) for j in range(4)], writes=[("ps", bg)])
                            for k in range(8):
                                A("pe", lambda e, k=k, w=w, pr=pr, t=t, bu=bu: e.matmul(
                                    out=psbank(bu), lhsT=w[:, k, 256 + pr * 128:256 + (pr + 1) * 128],
                                    rhs=xnT[:, k, t * 512:(t + 1) * 512], start=(k == 0), stop=(k == 7)),
                                  reads=[("wpb", wi), ("wpR", wi)] + [("xT", 4 * t + j) for j in range(4)], writes=[("ps", bu)])
                            sg = f_sg[pi]
                            A("act", lambda e, sg=sg, bg=bg: e.activation(out=sg, in_=psbank(bg), func=AF.Silu),
                              reads=[("ps", bg)], writes=[("sg", pi)])
                            A("dve", lambda e, sg=sg, bu=bu, slot=slot, t=t: e.tensor_tensor(
                                out=actT[:, slot, t * 512:(t + 1) * 512], in0=psbank(bu), in1=sg, op=ALU.mult),
                              reads=[("ps", bu), ("sg", pi)], writes=[("act", slot, t)])
                for b in range(16):
                    yb = 2 + (cnt["o"] % 2)
                    cnt["o"] += 1
                    for half in range(2):
                        for s in range(n):
                            A("pe", lambda e, b=b, half=half, s=s, yb=yb, n=n: e.matmul(
                                out=PS[yb][:, half * 512:(half + 1) * 512], lhsT=actT[:, s, b * 128:(b + 1) * 128],
                                rhs=wo[:, s, half * 512:(half + 1) * 512], start=(s == 0), stop=(s == n - 1)),
                              reads=[("act", s, b // 4), "wo"], writes=[("ps", 2 * yb), ("ps", 2 * yb + 1)])
                    A("dve", lambda e, b=b, yb=yb: e.scalar_tensor_tensor(
                        out=xres[:, b, :], in0=PS[yb][:, :], scalar=0.5, in1=xres[:, b, :], op0=ALU.mult, op1=ALU.add),
                      reads=[("ps", 2 * yb), ("ps", 2 * yb + 1), ("xr", b)], writes=[("xr", b)])
                    if c0 == 20 and b % 4 == 3:
                        sq_blocks(list(range(b - 3, b + 1)), f_junk)

        wmix = H["w_in"].ap().rearrange("(k p) c -> p k c", p=128)

        def qk_pre(ps_i, sq, sq_res):
            A("act", lambda e: e.activation(out=sq, in_=psbank(ps_i), func=AF.Square), reads=[("ps", ps_i)], writes=[sq_res])

        def qk_post(ps_i, ss_i, gcolq, sq, rs, dst, dst_res, rd_extra, sq_res, rs_res):
            A("pe", lambda e: e.matmul(out=psbank(ss_i), lhsT=bones, rhs=sq, start=True, stop=True),
              reads=[sq_res, "bones"], writes=[("ps", ss_i)])
            A("act", lambda e: e.activation(out=rs, in_=psbank(ss_i), func=AF.Ln, bias=epsc), reads=[("ps", ss_i), "epsc"],
              writes=[rs_res])
            A("act", lambda e: e.activation(out=rs, in_=rs, func=AF.Exp, scale=-0.5), reads=[rs_res], writes=[rs_res])
            A("dve", lambda e: e.scalar_tensor_tensor(out=dst, in0=psbank(ps_i), scalar=gcolq, in1=rs, op0=ALU.mult,
                                                      op1=ALU.mult),
              reads=[("ps", ps_i), rs_res] + rd_extra, writes=[dst_res])

        def mix_kv():
            for ch in range(4):
                norm_to_T(list(range(4 * ch, 4 * ch + 4)), 1, xnT, f_xnb, [6, 7], col0=512 * ch)
            wk = wp[0].rearrange("p (k c) -> p k c", k=8)
            wv = wp[1].rearrange("p (k c) -> p k c", k=8)
            wload(wk, wmix[:, :, 1536:2048], ("wpa", 0), ("wpa", 0))
            wload(wv, wmix[:, :, 2048:2560], ("wpa", 1), ("wpa", 1))
            items = [(h, t) for h in range(4) for t in range(4)]

            def k_mm(i):
                h, t = items[i]
                pb = i % 4
                for k in range(8):
                    A("pe", lambda e, k=k, h=h, t=t, pb=pb: e.matmul(
                        out=psbank(pb), lhsT=wk[:, k, h * 128:(h + 1) * 128], rhs=xnT[:, k, t * 512:(t + 1) * 512],
                        start=(k == 0), stop=(k == 7)),
                      reads=[("wpa", 0), ("wpR", 0)] + [("xT", 4 * t + j) for j in range(4)], writes=[("ps", pb)])
                qk_pre(pb, f_sq[i % 2], ("qsq", i % 2))

            def k_post(i):
                h, t = items[i]
                qk_post(i % 4, 4 + (i % 2), gk[:, 0:1], f_sq[i % 2], f_krs[i % 2], KTo[:, h, t * 512:(t + 1) * 512],
                        ("KT", h, t), ["gk"], ("qsq", i % 2), ("qrs", i % 2))

            k_mm(0)
            for i in range(16):
                if i + 1 < 16:
                    k_mm(i + 1)
                k_post(i)
            for b in range(16):
                pb = 6 + (b % 2)
                for k in range(8):
                    A("pe", lambda e, k=k, b=b, pb=pb: e.matmul(
                        out=psbank(pb), lhsT=xnT[:, k, b * 128:(b + 1) * 128], rhs=wv[:, k, :], start=(k == 0), stop=(k == 7)),
                      reads=[("wpa", 1), ("wpR", 1), ("xT", b)], writes=[("ps", pb)])
                A("act", lambda e, b=b, pb=pb: e.copy(out=Vo[:, b, :, 0:128], in_=psbank(pb).rearrange("p (h d) -> p h d", h=4)),
                  reads=[("ps", pb)], writes=[("V", b)])

        def spill_kv():
            A("sp", lambda e: e.dma_start(out=ktx.ap(), in_=KTo.rearrange("p h t -> p (h t)")),
              reads=[("KT", h, t) for h in range(4) for t in range(4)], writes=["ktx"], dma_ch="sk")
            A("sp", lambda e: e.dma_start(out=vx.ap(), in_=Vo.rearrange("p b h d -> p (b h d)")),
              reads=[("V", b) for b in range(16)], writes=["vx"], dma_ch="sv")

        ktxv = ktx.ap().rearrange("p (h t) -> p h t", h=4)
        vxv = vx.ap().rearrange("p (b h d) -> p b h d", b=16, h=4)

        tail_no = [0]

        def tload(pid, wi, loads):
            subs = [("wpa", wi), ("wpb", wi), ("wpc", wi), ("wpd", wi)]
            scr = wscr.ap()[:, pid * 4096:(pid + 1) * 4096]
            if tail_no[0] == 0:
                for (dst, src, sub) in loads:
                    A("pool", lambda e, dst=dst, src=src: e.dma_start(out=dst, in_=src), writes=[(sub, wi)],
                      dma_ch=(sub, wi), war=[("wpR", wi)])
                A("sp", lambda e: e.dma_start(out=scr, in_=wp[wi]), reads=[(sub, wi) for (_, _, sub) in loads],
                  writes=[("wscr", pid)], dma_ch=("wsw", wi))
            else:
                A("sp", lambda e: e.dma_start(out=wp[wi], in_=scr), reads=[("wscr", pid)], writes=subs,
                  dma_ch=("wsr", wi), war=[("wpR", wi)])

        def tail(qt, has_oth):
            blks = [4 * qt + j for j in range(4)]
            norm_to_T(blks, 1, hT, [t_xnb], [7], reuse_rs=True)
            hres = [("xT", j) for j in range(4)]
            wq = wp[cnt["wp"] % 2].rearrange("p (k c) -> p k c", k=8)
            wqi = cnt["wp"] % 2
            cnt["wp"] += 1
            tload(0, wqi, [(wq, wmix[:, :, 1024:1536], "wpa")])
            def q_mm(h):
                for k in range(8):
                    A("pe", lambda e, k=k, h=h: e.matmul(out=psbank(h), lhsT=wq[:, k, h * 128:(h + 1) * 128],
                                                          rhs=hT[:, k, :], start=(k == 0), stop=(k == 7)),
                      reads=[("wpa", wqi), ("wpR", wqi)] + hres, writes=[("ps", h)])
                qk_pre(h, t_sq2[h % 2], q_sqres[h % 2])

            def q_post(h):
                qk_post(h, 4 + (h % 2), gq[:, 0:1], t_sq2[h % 2], t_rs2[h % 2], QT[:, h, :], ("QT", h), ["gq"],
                        q_sqres[h % 2], q_rsres[h % 2])

            q_mm(0)
            for h in range(4):
                if h + 1 < 4:
                    q_mm(h + 1)
                q_post(h)
            wui = cnt["wp"] % 2
            cnt["wp"] += 1
            wu = wp[wui].rearrange("p (k c) -> p k c", k=8)
            tload(1, wui, [(wu, wmix[:, :, 0:512], "wpa")])
            wvi = cnt["wp"] % 2
            cnt["wp"] += 1
            wva = wp[wvi].rearrange("p (k c) -> p k c", k=8)
            tload(2, wvi, [(wva, wmix[:, :, 512:1024], "wpa")])
            def t3_proj(j):
                p = j % 2
                uvp = PS[p]
                ub0 = 2 * p
                gu = t_gg[:, p * D:(p + 1) * D]
                vn = t_vn2[p]
                Ab = t_A2[p]
                for k in range(8):
                    A("pe", lambda e, k=k, j=j, uvp=uvp: e.matmul(out=uvp[:, 0:512], lhsT=hT[:, k, j * 128:(j + 1) * 128],
                                                                   rhs=wu[:, k, :], start=(k == 0), stop=(k == 7)),
                      reads=[("wpa", wui), ("wpR", wui)] + hres, writes=[("ps", ub0)])
                for k in range(8):
                    A("pe", lambda e, k=k, j=j, uvp=uvp: e.matmul(out=uvp[:, 512:1024], lhsT=hT[:, k, j * 128:(j + 1) * 128],
                                                                   rhs=wva[:, k, :], start=(k == 0), stop=(k == 7)),
                      reads=[("wpa", wvi), ("wpR", wvi)] + hres, writes=[("ps", ub0 + 1)])
                uvr = [("ps", ub0), ("ps", ub0 + 1)]
                gr = "ga" if p == 0 else "gu"
                A("act", lambda e, uvp=uvp, gu=gu: e.activation(out=gu, in_=uvp[:, :], func=AF.Gelu_apprx_tanh), reads=uvr,
                  writes=[gr])
                A("dve", lambda e, p=p: e.memset(t_ss[:, 8 + p:9 + p], 0.0), writes=[("vss", p)])
                A("act", lambda e, p=p, gu=gu: e.activation(out=t_junk[:, 0:512], in_=gu[:, 512:1024], func=AF.Square,
                                                            accum_out=t_ss[:, 8 + p:9 + p]), reads=[gr, ("vss", p)],
                  writes=[("vss", p), "vjunk"])
                A("act", lambda e, p=p: e.activation(out=t_r[:, 10 + p:11 + p], in_=t_ss[:, 8 + p:9 + p], func=AF.Ln,
                                                     scale=1.0 / 512, bias=epsc), reads=[("vss", p), "epsc"], writes=[("vr", p)])
                A("act", lambda e, p=p: e.activation(out=t_r[:, 10 + p:11 + p], in_=t_r[:, 10 + p:11 + p], func=AF.Exp,
                                                     scale=-0.5), reads=[("vr", p)], writes=[("vr", p)])
                A("dve", lambda e, p=p, gu=gu, vn=vn: e.tensor_scalar(out=vn, in0=gu[:, 512:1024], scalar1=t_r[:, 10 + p:11 + p],
                                                                     scalar2=None, op0=ALU.mult), reads=[gr, ("vr", p)],
                  writes=[("vn", p)])

            def t3_post(j):
                p = j % 2
                uvp = PS[p]
                ub0 = 2 * p
                gu = t_gg[:, p * D:(p + 1) * D]
                vn = t_vn2[p]
                Ab = t_A2[p]
                gr = "ga" if p == 0 else "gu"
                for g in range(8):
                    A("pe", lambda e, g=g, vn=vn: e.matmul(out=psbank(6)[:, g * 64:(g + 1) * 64], lhsT=wsT[:, g, :],
                                                           rhs=vn[:, g * 64:(g + 1) * 64], start=True, stop=True),
                      reads=[("vn", p), "wsT"], writes=[("ps", 6)])
                s1 = gu[:, 512:1024]
                A("dve", lambda e, s1=s1: e.tensor_tensor(out=s1, in0=psbank(6), in1=gsgu, op=ALU.mult),
                  reads=[("ps", 6), "gsgu", gr], writes=[gr])
                A("dve", lambda e, s1=s1: e.tensor_tensor(out=s1.rearrange("p (g d) -> p g d", g=8),
                                                          in0=s1.rearrange("p (g d) -> p g d", g=8),
                                                          in1=bsT.unsqueeze(2).to_broadcast([128, 8, 64]), op=ALU.add),
                  reads=[gr, "bsT"], writes=[gr])
                A("dve", lambda e, s1=s1, gu=gu, Ab=Ab: e.tensor_tensor(out=Ab, in0=s1, in1=gu[:, 0:512], op=ALU.mult),
                  reads=[gr], writes=[("A", p)])

            def t3_post_b(j):
                p = j % 2
                Ab = t_A2[p]
                for c in range(4):
                    A("pe", lambda e, c=c, Ab=Ab: e.transpose(out=psbank_bf(7)[:, c * 128:(c + 1) * 128],
                                                              in_=Ab[:, c * 128:(c + 1) * 128], identity=idb),
                      reads=[("A", p), "idb"], writes=[("ps", 7)])
                A("act", lambda e, j=j: e.copy(out=AT[:, :, j * 128:(j + 1) * 128],
                                               in_=psbank_bf(7)[:, 0:512].rearrange("p (c t) -> p c t", c=4)),
                  reads=[("ps", 7)], writes=[("AT", j)])

            t3_proj(0)
            t3_proj(1)
            t3_post(0)
            t3_proj(2)
            t3_post_b(0)
            t3_post(1)
            t3_proj(3)
            t3_post_b(1)
            t3_post(2)
            t3_post_b(2)
            t3_post(3)
            t3_post_b(3)
            sctr = [0]
            pending = [None]

            def combine(h):
                o0 = o_sb[:, 0, :, 0:128]
                o1 = o_sb[:, 1, :, 0:128]
                gg = ["ga", "gu"]
                A("dve", lambda e: e.reciprocal(out=t_r8.rearrange("p (c q) -> p c q", c=2), in_=o_sb[:, :, :, 128]),
                  reads=gg, writes=["r8"])
                A("dve", lambda e: e.tensor_scalar(out=t_r8[:, 4:8], in0=t_r8[:, 4:8], scalar1=nlam[:, 0:1], scalar2=None,
                                                   op0=ALU.mult), reads=["r8", "nlam"], writes=["r8"])
                A("dve", lambda e: e.tensor_tensor(out=t_o0, in0=o0, in1=t_r8[:, 0:4].unsqueeze(2).to_broadcast([128, 4, 128]),
                                                   op=ALU.mult), reads=gg + ["r8"], writes=gg)
                A("dve", lambda e: e.tensor_tensor(out=o1, in0=o1, in1=t_r8[:, 4:8].unsqueeze(2).to_broadcast([128, 4, 128]),
                                                   op=ALU.mult), reads=gg + ["r8"], writes=gg)
                A("dve", lambda e: e.tensor_tensor(out=t_o0, in0=t_o0, in1=o1, op=ALU.add), reads=gg, writes=gg)
                A("dve", lambda e: e.tensor_tensor(out=o1, in0=t_o0, in1=t_o0, op=ALU.mult), reads=gg, writes=gg)
                A("dve", lambda e: e.reduce_sum(out=t_ss[:, 12:16], in_=o1, axis=AX.X), reads=gg, writes=["oss"])
                A("act", lambda e: e.activation(out=t_r[:, 4:8], in_=t_ss[:, 12:16], func=AF.Ln, scale=1.0 / 128, bias=epsc),
                  reads=["oss", "epsc"], writes=["orr"])
                A("act", lambda e: e.activation(out=t_r[:, 4:8], in_=t_r[:, 4:8], func=AF.Exp, scale=-0.5), reads=["orr"],
                  writes=["orr"])
                A("dve", lambda e: e.tensor_tensor(out=t_o0, in0=t_o0, in1=t_r[:, 4:8].unsqueeze(2).to_broadcast([128, 4, 128]),
                                                   op=ALU.mult), reads=gg + ["orr"], writes=gg)
                A("dve", lambda e, h=h: e.tensor_tensor(out=obn[:, :, h * 128:(h + 1) * 128], in0=t_o0,
                                                        in1=gsub.unsqueeze(1).to_broadcast([128, 4, 128]), op=ALU.mult),
                  reads=gg + ["gsub"], writes=[("obn", qb) for qb in range(4)])

            for h in range(4):
                kbs = [("own", kb) for kb in range(16)]
                if has_oth:
                    kbs += [("oth", kb) for kb in range(16)]
                nk = len(kbs)
                st_info = {}

                def stage_qk(ki, h=h, kbs=kbs, st_info=st_info):
                    src, kb = kbs[ki]
                    if src == "oth" and kb % 8 == 0:
                        ri = (kb // 8) % 2
                        A("sp", lambda e, h=h, kb=kb, ri=ri: e.dma_start(out=kring[ri], in_=ktxv[:, h, kb * 128:(kb + 8) * 128]),
                          reads=["ktx"], writes=[("kr", ri)], dma_ch=("kr", ri))
                        A("sp", lambda e, h=h, kb=kb, ri=ri: e.dma_start(out=vring[ri], in_=vxv[:, kb:kb + 8, h, :]),
                          reads=["vx"], writes=[("vr", ri)], dma_ch=("vr", ri))
                    if src == "own":
                        d = kb - 4 * qt
                        side = "neg" if d <= 1 else "pos"
                        bcol = (cneg if d <= 1 else cpos)[:, h:h + 1]
                        bres = "cneg" if d <= 1 else "cpos"
                        i0, i1 = max(0, d - 1), min(3, d + 1)
                        band = None
                        if i0 <= i1:
                            sc0 = 128 * (1 - (d - i0))
                            nb = 128 * (i1 - i0 + 1)
                            band = (128 * i0, nb, strips[(side, "hi")][:, h, sc0:sc0 + nb], strips[(side, "lo")][:, h, sc0:sc0 + nb])
                        kres = [("KT", h, kb // 4)]
                        vres = [("V", kb)]

                        def kt_ap(c, h=h, kb=kb):
                            return KTo[c * 64:(c + 1) * 64, h, kb * 128:(kb + 1) * 128]

                        v_ap = Vo[:, kb, h, :]
                    else:
                        ri = (kb // 8) % 2
                        bcol = coth[:, h:h + 1]
                        bres = "coth"
                        band = None
                        if kb == 0 and qt == 3:
                            band = (384, 128, bstr[("A", "hi")][:, h, :], bstr[("A", "lo")][:, h, :])
                        if kb == 15 and qt == 0:
                            band = (0, 128, bstr[("B", "hi")][:, h, :], bstr[("B", "lo")][:, h, :])
                        kres = [("kr", ri)]
                        vres = [("vr", ri)]

                        def kt_ap(c, ri=ri, kb=kb):
                            return kring[ri][c * 64:(c + 1) * 64, (kb % 8) * 128:(kb % 8 + 1) * 128]

                        v_ap = vring[ri][:, kb % 8, :]
                    pts = []
                    pi2 = sctr[0] % 2
                    sctr[0] += 1
                    for c in range(2):
                        sb = 2 * pi2 + c
                        A("pe", lambda e, c=c, sb=sb, kt_ap=kt_ap, h=h, band=band: e.matmul(
                            out=psbank(sb), lhsT=kt_ap(c), rhs=QT[c * 64:(c + 1) * 64, h, :], start=True, stop=(band is None)),
                          reads=kres + [("QT", h)], writes=[("ps", sb)])
                        if band is not None:
                            c0, nb, shi, slo = band
                            A("pe", lambda e, sb=sb, c0=c0, nb=nb, shi=shi: e.matmul(
                                out=psbank(sb)[:, c0:c0 + nb], lhsT=idb, rhs=shi, start=False, stop=False),
                              reads=["idb", "strips"], writes=[("ps", sb)])
                            A("pe", lambda e, sb=sb, c0=c0, nb=nb, slo=slo: e.matmul(
                                out=psbank(sb)[:, c0:c0 + nb], lhsT=idb, rhs=slo, start=False, stop=True),
                              reads=["idb", "strips"], writes=[("ps", sb)])
                        pts.append(sb)
                    A("act", lambda e, pi2=pi2, bcol=bcol: e.activation(out=PT2[pi2], in_=PS[pi2][:, :], func=AF.Exp, bias=bcol),
                      reads=[("ps", 2 * pi2), ("ps", 2 * pi2 + 1), bres], writes=[("PT", 2 * pi2), ("PT", 2 * pi2 + 1)])
                    st_info[ki] = (pts, v_ap, vres)

                def stage_pv(ki, nk=nk, st_info=st_info):
                    pts, v_ap, vres = st_info[ki]
                    for c in range(2):
                        for qb in range(4):
                            idx = c * 4 + qb
                            ob_bank = 4 + idx // 3
                            oc = (idx % 3) * 129
                            A("pe", lambda e, qb=qb, ob_bank=ob_bank, oc=oc, v_ap=v_ap, sb=pts[c], ki=ki, nk=nk, idx=idx: e.matmul(
                                out=psbank(ob_bank)[:, oc:oc + 129], lhsT=PT[sb][:, qb * 128:(qb + 1) * 128], rhs=v_ap,
                                start=(ki == 0 and idx % 3 == 0), stop=(ki == nk - 1), skip_group_check=True),
                              reads=[("PT", pts[c])] + vres, writes=[("ps", ob_bank)])

                stage_qk(0)
                for ki in range(nk):
                    if ki + 1 < nk:
                        stage_qk(ki + 1)
                    stage_pv(ki)
                    if ki == 7 and pending[0] is not None:
                        combine(pending[0])
                        pending[0] = None
                for bi, ncol in ((0, 387), (1, 387), (2, 258)):
                    A("dve", lambda e, bi=bi, ncol=ncol: e.tensor_copy(out=t_gg[:, bi * 387:bi * 387 + ncol],
                                                                      in_=psbank(4 + bi)[:, 0:ncol]),
                      reads=[("ps", 4 + bi)], writes=["ga", "gu"])
                pending[0] = h
            combine(pending[0])
            for qb in range(4):
                for c in range(4):
                    A("pe", lambda e, qb=qb, c=c: e.transpose(out=psbank_bf(7)[:, c * 128:(c + 1) * 128],
                                                              in_=obn[:, qb, c * 128:(c + 1) * 128], identity=idb),
                      reads=[("obn", qb), "idb"], writes=[("ps", 7)])
                A("act", lambda e, qb=qb: e.copy(out=boT[:, :, qb * 128:(qb + 1) * 128],
                                                 in_=psbank_bf(7)[:, 0:512].rearrange("p (c t) -> p c t", c=4)),
                  reads=[("ps", 7)], writes=[("boT", qb)])
            wpa_v = H["w_proj_a"].ap().rearrange("(k p) c -> p k c", p=128)
            wpb_v = H["w_proj_b"].ap().rearrange("(k p) c -> p k c", p=128)
            atres = [("AT", j) for j in range(4)]
            bores = [("boT", j) for j in range(4)]
            for oc in range(8):
                pb0 = 4 * (oc % 2)
                wi = cnt["wp"] % 2
                cnt["wp"] += 1
                wb = wp[wi]
                w_a = wb[:, 0:512].rearrange("p (k c) -> p k c", k=4)
                w_b = wb[:, 512:1024].rearrange("p (k c) -> p k c", k=4)
                w_ga = wb[:, 1024:2048].rearrange("p (k c) -> p k c", k=8)
                w_gb = wb[:, 2048:3072].rearrange("p (k c) -> p k c", k=8)
                tload(3 + oc, wi, [(w_a, wpa_v[:, :, oc * 128:(oc + 1) * 128], "wpa"),
                                   (w_b, wpb_v[:, :, oc * 128:(oc + 1) * 128], "wpb"),
                                   (w_ga, wmix[:, :, 2560 + oc * 128:2560 + (oc + 1) * 128], "wpc"),
                                   (w_gb, wmix[:, :, 3584 + oc * 128:3584 + (oc + 1) * 128], "wpd")])
                for k in range(4):
                    A("pe", lambda e, k=k, w_a=w_a, pb0=pb0: e.matmul(out=psbank(pb0 + 0), lhsT=w_a[:, k, :], rhs=AT[:, k, :], start=(k == 0),
                                                             stop=(k == 3)), reads=[("wpa", wi), ("wpR", wi)] + atres, writes=[("ps", pb0 + 0)])
                for k in range(4):
                    A("pe", lambda e, k=k, w_b=w_b, pb0=pb0: e.matmul(out=psbank(pb0 + 1), lhsT=w_b[:, k, :], rhs=boT[:, k, :], start=(k == 0),
                                                             stop=(k == 3)), reads=[("wpb", wi), ("wpR", wi)] + bores, writes=[("ps", pb0 + 1)])
                for k in range(8):
                    A("pe", lambda e, k=k, w_ga=w_ga, pb0=pb0: e.matmul(out=psbank(pb0 + 2), lhsT=w_ga[:, k, :], rhs=hT[:, k, :], start=(k == 0),
                                                               stop=(k == 7)), reads=[("wpc", wi), ("wpR", wi)] + hres, writes=[("ps", pb0 + 2)])
                for k in range(8):
                    A("pe", lambda e, k=k, w_gb=w_gb, pb0=pb0: e.matmul(out=psbank(pb0 + 3), lhsT=w_gb[:, k, :], rhs=hT[:, k, :], start=(k == 0),
                                                               stop=(k == 7)), reads=[("wpd", wi), ("wpR", wi)] + hres, writes=[("ps", pb0 + 3)])
                A("act", lambda e, oc=oc, pb0=pb0: e.activation(out=t_sa, in_=psbank(pb0 + 2), func=AF.Sigmoid, bias=gbias[:, oc:oc + 1]),
                  reads=[("ps", pb0 + 2), "gbias"], writes=["sa"])
                A("act", lambda e, oc=oc, pb0=pb0: e.activation(out=t_sb, in_=psbank(pb0 + 3), func=AF.Sigmoid, bias=gbias[:, 8 + oc:9 + oc]),
                  reads=[("ps", pb0 + 3), "gbias"], writes=["sb"])
                A("dve", lambda e, pb0=pb0: e.tensor_tensor(out=t_m1, in0=psbank(pb0 + 0), in1=t_sa, op=ALU.mult), reads=[("ps", pb0 + 0), "sa"],
                  writes=["sa"])
                A("dve", lambda e, pb0=pb0: e.tensor_tensor(out=t_m2, in0=psbank(pb0 + 1), in1=t_sb, op=ALU.mult), reads=[("ps", pb0 + 1), "sb"],
                  writes=["sb"])
                A("dve", lambda e, oc=oc: e.tensor_tensor(out=mT[:, oc, :], in0=t_m1, in1=t_m2, op=ALU.add), reads=["sa", "sb"],
                  writes=[("mT", oc)])
            wo_v = H["w_out"].ap().rearrange("(k p) c -> p k c", p=128)
            mres = [("mT", oc) for oc in range(8)]
            for half in range(2):
                wi = cnt["wp"] % 2
                cnt["wp"] += 1
                w = wp[wi].rearrange("p (k c) -> p k c", k=8)
                tload(11 + half, wi, [(w, wo_v[:, :, half * 512:(half + 1) * 512], "wpa")])
                for j in range(4):
                    pb = 4 + (j % 2)
                    for k in range(8):
                        A("pe", lambda e, k=k, j=j, w=w, pb=pb: e.matmul(out=psbank(pb), lhsT=mT[:, k, j * 128:(j + 1) * 128],
                                                                          rhs=w[:, k, :], start=(k == 0), stop=(k == 7)),
                          reads=[("wpa", wi), ("wpR", wi)] + mres, writes=[("ps", pb)])
                    b = blks[j]
                    A("dve", lambda e, b=b, pb=pb, half=half: e.tensor_tensor(
                        out=xres[:, b, half * 512:(half + 1) * 512], in0=psbank(pb), in1=xres[:, b, half * 512:(half + 1) * 512],
                        op=ALU.add), reads=[("ps", pb), ("xr", b)], writes=[("xr", b)])
            sq_blocks(blks, t_junk)

        def final_store(dst_rows):
            v = dst_rows.rearrange("(b p) d -> p b d", p=128)
            pr = [("pss", c) for c in range(4)]
            A("act", lambda e: e.activation(out=prs, in_=pss, func=AF.Ln, scale=1.0 / D, bias=epsc), reads=pr + ["epsc"],
              writes=[("prs", c) for c in range(4)])
            A("act", lambda e: e.activation(out=prs, in_=prs, func=AF.Exp, scale=-0.5), reads=[("prs", c) for c in range(4)],
              writes=[("prs", c) for c in range(4)])
            for b in range(16):
                A("dve", lambda e, b=b: e.scalar_tensor_tensor(out=xres[:, b, :], in0=xres[:, b, :], scalar=prs[:, b:b + 1],
                                                               in1=gfin, op0=ALU.mult, op1=ALU.mult),
                  reads=[("xr", b), ("prs", b // 4), "gfin"], writes=[("xr", b)])
            for b4 in range(4):
                A("sp", lambda e, b4=b4: e.dma_start(out=v[:, b4 * 4:(b4 + 1) * 4, :], in_=xres[:, b4 * 4:(b4 + 1) * 4, :]),
                  reads=[("xr", b) for b in range(b4 * 4, b4 * 4 + 4)], writes=[("OUT", cnt["x"], b4)], dma_ch=("o", b4))
                out_res.append(("OUT", cnt["x"], b4))
            cnt["x"] += 1

        out_res = []

        def unit(src, dst, has_oth, preloaded=False):
            load_x(src, part="sq" if preloaded else "both")
            ffn(H["ffn1_w_in"], H["ffn1_w_out"], 0)
            mix_kv()
            barrier()
            for qt in range(4):
                tail(qt, has_oth)
                tail_no[0] += 1
            barrier()
            ffn(H["ffn2_w_in"], H["ffn2_w_out"], 2)
            final_store(dst)

        mode = debug or "full"
        if mode == "full":
            unit(H["xp"].ap()[0], yp.ap()[0], False, preloaded=True)
            unit(H["xp"].ap()[1], yp.ap()[1], False)
            load_x(H["xsx"].ap())
            ffn(H["ffn1_w_in"], H["ffn1_w_out"], 0)
            mix_kv()
            spill_kv()
            unit(H["xso"].ap(), ys.ap(), True)
        elif mode == "p0":
            unit(H["xp"].ap()[0], yp.ap()[0], False)
        elif mode == "s":
            load_x(H["xsx"].ap())
            ffn(H["ffn1_w_in"], H["ffn1_w_out"], 0)
            mix_kv()
            spill_kv()
            unit(H["xso"].ap(), ys.ap(), True)
        A("sp", None, reads=out_res)
        P.emit(st)
    return nc


def _bucket_table():
    rel = np.arange(-255, 256, dtype=np.int32)
    half = 16
    max_exact = 8
    try:
        import jax
        import jax.numpy as jnp
        with jax.default_device(jax.devices("cpu")[0]):
            r = jnp.asarray(rel)
            bucket = jnp.where(r > 0, half, 0).astype(jnp.int32)
            n = jnp.abs(r)
            nf = jnp.maximum(n, 1).astype(jnp.float32)
            large = max_exact + (jnp.log(nf / max_exact) / math.log(128 / max_exact) * (half - max_exact)).astype(jnp.int32)
            large = jnp.minimum(large, half - 1)
            out = np.asarray(bucket + jnp.where(n < max_exact, n, large))
        return rel, out.astype(np.int64)
    except Exception:
        bucket = np.where(rel > 0, half, 0)
        n = np.abs(rel)
        nf = np.maximum(n, 1).astype(np.float32)
        large = max_exact + (np.log(nf / np.float32(max_exact)) / np.float32(math.log(128 / max_exact))
                             * np.float32(half - max_exact)).astype(np.int32)
        large = np.minimum(large, half - 1)
        return rel, (bucket + np.where(n < max_exact, n, large)).astype(np.int64)


_CACHE = {}


def kernel(**inputs):
    f = lambda k: np.ascontiguousarray(np.asarray(inputs[k], dtype=np.float32))
    xp_all = f("x_prompt")
    xs_all = f("x_sample")
    shared = {}
    for k in ("ffn1_norm", "mix_norm", "ffn2_norm", "final_norm", "gate_bias", "sgu_norm", "q_norm", "k_norm",
              "lambda_q1", "lambda_k1", "lambda_q2", "lambda_k2", "diff_subln"):
        shared[k] = f(k).reshape(1, -1)
    for k in ("ffn1_w_in", "ffn1_w_out", "w_in", "sgu_w", "sgu_b", "w_proj_a", "w_proj_b", "w_out", "ffn2_w_in", "ffn2_w_out"):
        shared[k] = np.ascontiguousarray(f(k)[0])
    shared["rel_bias"] = f("rel_bias")
    shared["ident"] = np.eye(128, dtype=np.float32)
    bo = np.zeros((128, 128), np.float32)
    bo[:64, :64] = 1.0 / 64
    bo[64:, 64:] = 1.0 / 64
    shared["bones"] = bo
    rel, bk = _bucket_table()
    oh = np.zeros((32, 512), np.float32)
    for i in range(511):
        r = 255 - i
        oh[bk[r + 255], i] = 1.0
    shared["oh"] = oh
    in_maps = []
    for c in range(8):
        m = dict(shared)
        m["xp"] = np.ascontiguousarray(xp_all[2 * c:2 * c + 2])
        s = c // 2
        p = c % 2
        m["xso"] = np.ascontiguousarray(xs_all[s, p * 2048:(p + 1) * 2048])
        m["xsx"] = np.ascontiguousarray(xs_all[s, (1 - p) * 2048:(2 - p) * 2048])
        par = np.zeros((128, 2), np.float32)
        par[:, 0] = 1.0 - p
        par[:, 1] = float(p)
        m["par"] = par
        in_maps.append(m)
    if "nc" not in _CACHE:
        _CACHE["nc"] = build_program()
    res = run_bass_kernel_spmd(_CACHE["nc"], in_maps, core_ids=list(range(8)))
    y_prompt = np.empty((16, 2048, D), np.float32)
    y_sample = np.empty((4, 4096, D), np.float32)
    for c in range(8):
        r = res.results[c]
        y_prompt[2 * c:2 * c + 2] = r["yp"]
        s = c // 2
        p = c % 2
        y_sample[s, p * 2048:(p + 1) * 2048] = r["ys"]
    return (y_prompt, y_sample)
```

```python
import math
import numpy as np
import concourse.bass as bass
import concourse.mybir as mybir
from concourse.bass_utils import run_bass_kernel_spmd
from contextlib import ExitStack

F32 = mybir.dt.float32
BF16 = mybir.dt.bfloat16
AF = mybir.ActivationFunctionType
ALU = mybir.AluOpType
AX = mybir.AxisListType

D = 1024
DFF = 2816
NCH = 22
EPS = 1e-6
SQRT_GC = math.sqrt(0.044715)
GELU_S = 2.0 * math.sqrt(2.0 / math.pi)
LAMBDA_INIT = 0.8 - 0.6 * math.exp(0.0)


class Op:
    __slots__ = ("eng", "fn", "deps", "is_dma", "ch", "ch_idx", "signaled", "tick")

    def __init__(self, eng, fn, is_dma=False, ch=None):
        self.eng = eng
        self.fn = fn
        self.deps = []
        self.is_dma = is_dma
        self.ch = ch
        self.ch_idx = 0
        self.signaled = False
        self.tick = 0


class Prog:
    ENGS = ["pe", "act", "dve", "pool", "sp"]

    def __init__(self, nc):
        self.nc = nc
        self.ops = {e: [] for e in self.ENGS}
        self.last_w = {}
        self.readers = {}
        self.ch_ops = {}
        self.cur_barrier = None

    def add(self, eng, fn, reads=(), writes=(), dma_ch=None, war=()):
        op = Op(eng, fn, is_dma=dma_ch is not None, ch=dma_ch)
        deps = set()
        if self.cur_barrier is not None:
            deps.add(self.cur_barrier)
        for r in war:
            rd = self.readers.get(r)
            if rd:
                deps.update(rd.values())
        for r in reads:
            w = self.last_w.get(r)
            if w is not None:
                deps.add(w)
        for r in writes:
            w = self.last_w.get(r)
            if w is not None:
                deps.add(w)
            rd = self.readers.get(r)
            if rd:
                deps.update(rd.values())
        if dma_ch is not None:
            lst = self.ch_ops.setdefault(dma_ch, [])
            if lst:
                deps.add(lst[-1])
            lst.append(op)
            op.ch_idx = len(lst)
        for d in deps:
            if d.is_dma:
                op.deps.append(d)
                continue
            if d.eng == eng and eng == "pe" and not op.is_dma:
                continue
            d.signaled = True
            op.deps.append(d)
        for r in reads:
            key = ("d", dma_ch) if op.is_dma else eng
            self.readers.setdefault(r, {})[key] = op
        for r in writes:
            self.last_w[r] = op
            self.readers[r] = {}
        self.ops[eng].append(op)
        return op

    def barrier(self, eng, fn):
        op = Op(eng, fn)
        for e in self.ENGS:
            for o in reversed(self.ops[e]):
                if not o.is_dma and o.fn is not None:
                    if not (e == eng and eng == "pe"):
                        o.signaled = True
                        op.deps.append(o)
                    break
        for ch, lst in self.ch_ops.items():
            if lst:
                op.deps.append(lst[-1])
        op.signaled = True
        self.ops[eng].append(op)
        self.cur_barrier = op
        self.last_w = {}
        self.readers = {}
        return op

    def emit(self, stack):
        nc = self.nc
        esem = {}
        for e in self.ENGS:
            if e == "sp":
                continue
            esem[e] = stack.enter_context(nc.semaphore("S_" + e))
        chsem = {}
        for i, ch in enumerate(self.ch_ops):
            chsem[ch] = stack.enter_context(nc.semaphore("D%d" % i))
        for e in self.ENGS:
            t = 0
            for op in self.ops[e]:
                if op.signaled and not op.is_dma:
                    t += 1
                    op.tick = t
        block = stack.enter_context(nc.Block())
        ops = self.ops

        def run(e, eng):
            waited = {}
            for op in ops[e]:
                need = {}
                for d in op.deps:
                    if d.is_dma:
                        key = ("d", d.ch)
                        val = 16 * d.ch_idx
                    else:
                        key = ("e", d.eng)
                        val = d.tick
                    if val > need.get(key, 0):
                        need[key] = val
                for key, val in need.items():
                    if val <= waited.get(key, 0):
                        continue
                    waited[key] = val
                    sem = chsem[key[1]] if key[0] == "d" else esem[key[1]]
                    eng.wait_ge(sem, val)
                if op.fn is None:
                    continue
                ins = op.fn(eng)
                if op.is_dma:
                    ins.then_inc(chsem[op.ch], 16)
                elif op.signaled:
                    ins.then_inc(esem[e], 1)

        @block.tensor
        def _(eng):
            run("pe", eng)

        @block.scalar
        def _(eng):
            run("act", eng)

        @block.vector
        def _(eng):
            run("dve", eng)

        @block.gpsimd
        def _(eng):
            run("pool", eng)

        @block.sync
        def _(eng):
            run("sp", eng)


ARENA_BYTES = 210944


def build_program(debug=None):
    nc = bass.Bass("TRN2", target_bir_lowering=False)
    st = ExitStack()

    def din(name, shape):
        return nc.dram_tensor(name, list(shape), F32, kind="ExternalInput")

    H = {}
    for name, shape in [
        ("xp", (2, 2048, D)), ("xso", (2048, D)), ("xsx", (2048, D)), ("par", (128, 2)),
        ("rel_bias", (32, 4)), ("ffn1_norm", (1, D)), ("ffn1_w_in", (D, 2 * DFF)), ("ffn1_w_out", (DFF, D)),
        ("mix_norm", (1, D)), ("w_in", (D, 4608)), ("gate_bias", (1, 2048)), ("sgu_norm", (1, 512)),
        ("sgu_w", (8, 128, 128)), ("sgu_b", (8, 128)), ("q_norm", (1, 64)), ("k_norm", (1, 64)),
        ("lambda_q1", (1, 64)), ("lambda_k1", (1, 64)), ("lambda_q2", (1, 64)), ("lambda_k2", (1, 64)),
        ("diff_subln", (1, 128)), ("w_proj_a", (512, D)), ("w_proj_b", (512, D)), ("w_out", (D, D)),
        ("ffn2_norm", (1, D)), ("ffn2_w_in", (D, 2 * DFF)), ("ffn2_w_out", (DFF, D)), ("final_norm", (1, D)),
        ("ident", (128, 128)), ("bones", (128, 128)), ("oh", (32, 512)),
    ]:
        H[name] = din(name, shape)
    yp = nc.dram_tensor("yp", [2, 2048, D], F32, kind="ExternalOutput")
    ys = nc.dram_tensor("ys", [2048, D], F32, kind="ExternalOutput")
    ktx = nc.dram_tensor("ktx", [128, 4 * 2048], BF16)
    vx = nc.dram_tensor("vx", [128, 16 * 516], BF16)
    gscr = nc.dram_tensor("gscr", [4, 128, 512], F32)
    wscr = nc.dram_tensor("wscr", [128, 13 * 4096], BF16)

    def bc(h, n, off=0):
        return bass.AP(h, off, [[0, 128], [1, n]])

    with st:
        P = Prog(nc)
        arena = st.enter_context(nc.sbuf_tensor("arena", [128, ARENA_BYTES // 2], BF16))
        PS = [st.enter_context(nc.psum_tensor("ps%d" % i, [128, 1024], F32)) for i in range(4)]

        class Alloc:
            def __init__(self, base, limit):
                self.off = base
                self.limit = limit

            def __call__(self, dtype, shape):
                esz = 4 if dtype == F32 else 2
                n = 1
                for s in shape:
                    n *= s
                nbytes = (n * esz + 63) // 64 * 64
                off = self.off
                self.off += nbytes
                assert self.off <= self.limit, ("arena overflow", self.off, self.limit)
                a = arena[:, off // 2: off // 2 + n * esz // 2]
                if dtype == F32:
                    a = a.bitcast(F32)
                if len(shape) == 2:
                    a = a.rearrange("p (a b) -> p a b", a=shape[0])
                elif len(shape) == 3:
                    a = a.rearrange("p (a b c) -> p a b c", a=shape[0], b=shape[1])
                return a

        fx = Alloc(0, ARENA_BYTES)
        xres = fx(F32, [16, D])
        KTo = fx(BF16, [4, 2048])
        Vo = fx(BF16, [16, 4, 129])
        wp = [fx(BF16, [4096]) for _ in range(2)]
        idb = fx(BF16, [128])
        bones = fx(BF16, [128])
        gcol = fx(F32, [3, 8])
        gfin = fx(F32, [D])
        gsgu = fx(F32, [512])
        gsub = fx(F32, [128])
        gq = fx(F32, [1])
        gk = fx(F32, [1])
        gbias = fx(F32, [16])
        bsT = fx(F32, [8])
        wsT = fx(BF16, [8, 128])
        nlam = fx(F32, [1])
        cpos = fx(F32, [4])
        cneg = fx(F32, [4])
        coth = fx(F32, [4])
        epsc = fx(F32, [1])
        parc = fx(F32, [2])
        pss = fx(F32, [16])
        prs = fx(F32, [16])
        strips = {}
        for side in ("neg", "pos"):
            for part in ("hi", "lo"):
                strips[(side, part)] = fx(BF16, [4, 384])
        bstr = {}
        for nm in ("A", "B"):
            for part in ("hi", "lo"):
                bstr[(nm, part)] = fx(BF16, [4, 128])
        RBASE = fx.off
        RLIM = ARENA_BYTES

        def psbank(i):
            return PS[i // 2][:, (i % 2) * 512:(i % 2) * 512 + 512]

        def psbank_bf(i):
            return psbank(i).bitcast(BF16)

        tmp = Alloc(RBASE, RLIM)
        t_rb = tmp(F32, [4])
        t_ones = tmp(F32, [128])
        t_rbB = tmp(F32, [4, 128])
        t_oh = tmp(F32, [512])
        t_g = tmp(F32, [4, 512])
        t_T = tmp(F32, [4, 384])
        t_Ts = tmp(F32, [384])
        t_hi = tmp(BF16, [384])
        t_w = tmp(F32, [8, 128])
        t_wb = tmp(BF16, [8, 128])
        t_l = tmp(F32, [4, 64])
        t_lp = tmp(F32, [2, 64])
        t_ls = tmp(F32, [2])
        t_le = tmp(F32, [2])

        A = P.add
        A("pool", lambda e: e.dma_start(out=idb, in_=H["ident"].ap()), writes=["idb"], dma_ch="c0")
        A("pool", lambda e: e.dma_start(out=bones, in_=H["bones"].ap()), writes=["bones"], dma_ch="c1")
        for i, nm in enumerate(("ffn1_norm", "mix_norm", "ffn2_norm")):
            A("sp", lambda e, i=i, nm=nm: e.dma_start(
                out=gcol[:, i, :], in_=H[nm].ap().rearrange("o (k p) -> p (o k)", p=128),
                allow_slow_non_contiguous=True), writes=["gcol"], dma_ch="c2")
        A("sp", lambda e: e.dma_start(out=gbias, in_=H["gate_bias"].ap().rearrange("o (k p) -> p (o k)", p=128),
                                      allow_slow_non_contiguous=True), writes=["gbias"], dma_ch="c3")
        A("sp", lambda e: e.dma_start(out=bsT, in_=H["sgu_b"].ap().rearrange("g t -> t g"),
                                      allow_slow_non_contiguous=True), writes=["bsT"], dma_ch="c3")
        A("sp", lambda e: e.dma_start(out=gfin, in_=bc(H["final_norm"], D)), writes=["gfin"], dma_ch="c4")
        A("sp", lambda e: e.dma_start(out=gsgu, in_=bc(H["sgu_norm"], 512)), writes=["gsgu"], dma_ch="c4")
        A("sp", lambda e: e.dma_start(out=gsub, in_=bc(H["diff_subln"], 128)), writes=["gsub"], dma_ch="c4")
        A("sp", lambda e: e.dma_start(out=parc, in_=H["par"].ap()), writes=["parc"], dma_ch="c4")
        for half in range(2):
            A("sp", lambda e, half=half: e.dma_start(
                out=gq[half * 64:(half + 1) * 64, :], in_=H["q_norm"].ap().rearrange("o d -> d o")),
                writes=["gq"], dma_ch="c5")
            A("sp", lambda e, half=half: e.dma_start(
                out=gk[half * 64:(half + 1) * 64, :], in_=H["k_norm"].ap().rearrange("o d -> d o")),
                writes=["gk"], dma_ch="c5")
        A("dve", lambda e: e.tensor_scalar(out=gq, in0=gq, scalar1=0.125, scalar2=None, op0=ALU.mult),
          reads=["gq"], writes=["gq"])
        A("dve", lambda e: e.tensor_scalar(out=gsub, in0=gsub, scalar1=1.0 - LAMBDA_INIT, scalar2=None, op0=ALU.mult),
          reads=["gsub"], writes=["gsub"])
        A("dve", lambda e: e.memset(epsc, EPS), writes=["epsc"])
        A("dve", lambda e: e.memset(Vo[:, :, :, 128:129], 1.0), writes=[("V", b) for b in range(16)])
        A("sp", lambda e: e.dma_start(out=cpos, in_=bc(H["rel_bias"], 4, off=31 * 4)), writes=["cpos"], dma_ch="c6")
        A("sp", lambda e: e.dma_start(out=cneg, in_=bc(H["rel_bias"], 4, off=15 * 4)), writes=["cneg"], dma_ch="c6")
        A("dve", lambda e: e.tensor_scalar(out=coth, in0=cpos, scalar1=parc[:, 0:1], scalar2=None, op0=ALU.mult),
          reads=["cpos", "parc"], writes=["coth"])
        A("dve", lambda e: e.scalar_tensor_tensor(out=coth, in0=cneg, scalar=parc[:, 1:2], in1=coth,
                                                  op0=ALU.mult, op1=ALU.add),
          reads=["cneg", "parc", "coth"], writes=["coth"])
        for i, nm in enumerate(("lambda_q1", "lambda_k1", "lambda_q2", "lambda_k2")):
            A("sp", lambda e, i=i, nm=nm: e.dma_start(out=t_l[:, i, :], in_=bc(H[nm], 64)), writes=["t_l"], dma_ch="c7")
        A("dve", lambda e: e.tensor_tensor(out=t_lp[:, 0, :], in0=t_l[:, 0, :], in1=t_l[:, 1, :], op=ALU.mult),
          reads=["t_l"], writes=["t_lp"])
        A("dve", lambda e: e.tensor_tensor(out=t_lp[:, 1, :], in0=t_l[:, 2, :], in1=t_l[:, 3, :], op=ALU.mult),
          reads=["t_l", "t_lp"], writes=["t_lp"])
        A("dve", lambda e: e.reduce_sum(out=t_ls, in_=t_lp, axis=AX.X), reads=["t_lp"], writes=["t_ls"])
        A("act", lambda e: e.activation(out=t_le, in_=t_ls, func=AF.Exp), reads=["t_ls"], writes=["t_le"])
        A("dve", lambda e: e.scalar_tensor_tensor(out=nlam, in0=t_le[:, 1:2], scalar=-LAMBDA_INIT, in1=t_le[:, 0:1],
                                                  op0=ALU.add, op1=ALU.subtract),
          reads=["t_le"], writes=["nlam"])
        A("sp", lambda e: e.dma_start(out=t_w, in_=H["sgu_w"].ap().rearrange("g t s -> t g s")), writes=["t_w"], dma_ch="c8")
        A("dve", lambda e: e.tensor_copy(out=t_wb, in_=t_w), reads=["t_w"], writes=["t_wb"])
        for g in range(8):
            A("pe", lambda e, g=g: e.transpose(out=psbank_bf(0)[:, g * 128:(g + 1) * 128], in_=t_wb[:, g, :], identity=idb),
              reads=["t_wb", "idb"], writes=[("ps", 0)])
        A("dve", lambda e: e.tensor_copy(out=wsT, in_=psbank_bf(0).rearrange("p (g t) -> p g t", g=8)),
          reads=[("ps", 0)], writes=["wsT"])
        A("sp", lambda e: e.dma_start(out=t_rb[0:32, :], in_=H["rel_bias"].ap()), writes=["t_rb"], dma_ch="c9")
        A("sp", lambda e: e.dma_start(out=t_oh[0:32, :], in_=H["oh"].ap()), writes=["t_oh"], dma_ch="c9")
        A("dve", lambda e: e.memset(t_ones, 1.0), writes=["t_ones"])
        for h in range(4):
            A("dve", lambda e, h=h: e.tensor_scalar(out=t_rbB[0:32, h, :], in0=t_ones[0:32, :], scalar1=t_rb[0:32, h:h + 1],
                                                    scalar2=None, op0=ALU.mult),
              reads=["t_ones", "t_rb"], writes=[("t_rbB", h)])
            A("pe", lambda e, h=h: e.matmul(out=psbank(2 + (h % 2)), lhsT=t_rbB[0:32, h, :], rhs=t_oh[0:32, :],
                                            start=True, stop=True),
              reads=[("t_rbB", h), "t_oh"], writes=[("ps", 2 + (h % 2))])
            A("act", lambda e, h=h: e.copy(out=t_g[:, h, :], in_=psbank(2 + (h % 2))),
              reads=[("ps", 2 + (h % 2))], writes=[("t_g", h)])
            A("sp", lambda e, h=h: e.dma_start(out=gscr.ap()[h], in_=t_g[:, h, :]), reads=[("t_g", h)],
              writes=[("gscr", h)], dma_ch="c10")
            A("sp", lambda e, h=h: e.dma_start(out=t_T[:, h, :],
                                               in_=bass.AP(gscr, h * 128 * 512 + 127, [[511, 128], [1, 384]])),
              reads=[("gscr", h)], writes=[("t_T", h)], dma_ch="c11")

            def mk_strip(src, ccol, mask, dst_hi, dst_lo, h=h):
                n = src.shape[-1]
                A("dve", lambda e: e.tensor_scalar(out=t_Ts[:, 0:n], in0=src, scalar1=ccol, scalar2=None, op0=ALU.subtract),
                  reads=[("t_T", h), "cpos", "cneg"], writes=["t_Ts"])
                if mask is not None:
                    A("dve", lambda e: e.tensor_scalar(out=t_Ts[:, 0:n], in0=t_Ts[:, 0:n], scalar1=mask, scalar2=None,
                                                       op0=ALU.mult),
                      reads=["t_Ts", "parc"], writes=["t_Ts"])
                A("dve", lambda e: e.tensor_copy(out=dst_hi, in_=t_Ts[:, 0:n]), reads=["t_Ts"], writes=["strips", "t_hi"])
                A("dve", lambda e: e.tensor_tensor(out=dst_lo, in0=t_Ts[:, 0:n], in1=dst_hi, op=ALU.subtract),
                  reads=["t_Ts", "strips", "t_hi"], writes=["strips"])

            mk_strip(t_T[:, h, :], cneg[:, h:h + 1], None, strips[("neg", "hi")][:, h, :], strips[("neg", "lo")][:, h, :])
            mk_strip(t_T[:, h, :], cpos[:, h:h + 1], None, strips[("pos", "hi")][:, h, :], strips[("pos", "lo")][:, h, :])
            mk_strip(t_T[:, h, 0:128], cpos[:, h:h + 1], parc[:, 0:1], bstr[("A", "hi")][:, h, :], bstr[("A", "lo")][:, h, :])
            mk_strip(t_T[:, h, 256:384], cneg[:, h:h + 1], parc[:, 1:2], bstr[("B", "hi")][:, h, :], bstr[("B", "lo")][:, h, :])

        bar_t = fx(F32, [1]) if False else epsc

        def barrier():
            P.barrier("dve", lambda e: e.memset(epsc, EPS))

        if (debug or "full") == "full":
            v0 = H["xp"].ap()[0].rearrange("(b p) d -> p b d", p=128)
            for b4 in range(4):
                A("sp", lambda e, b4=b4: e.dma_start(out=xres[:, b4 * 4:(b4 + 1) * 4, :], in_=v0[:, b4 * 4:(b4 + 1) * 4, :]),
                  writes=[("xr", b) for b in range(b4 * 4, b4 * 4 + 4)], dma_ch=("x", b4))
        barrier()

        fv = Alloc(RBASE, RLIM)
        xnT = fv(BF16, [8, 2048])
        actT = fv(BF16, [4, 2048])
        wo = fv(BF16, [4, D])
        f_sg = [fv(F32, [512]) for _ in range(2)]
        f_xnb = [fv(BF16, [D]) for _ in range(2)]
        f_junk = fv(BF16, [D])
        f_ss = fv(F32, [16])
        f_rs = fv(F32, [16])
        print("ffn view end", fv.off, RLIM)
        _al = Alloc(RBASE + 32768, RBASE + 32768 + 16384)
        f_sq = [_al(BF16, [512]) for _ in range(2)]
        f_krs = [_al(F32, [512]) for _ in range(2)]

        tv = Alloc(RBASE, RLIM)
        hT = tv(BF16, [8, 512])
        QT = tv(BF16, [4, 512])
        PT2 = [tv(BF16, [1024]) for _ in range(2)]
        kring = [tv(BF16, [1024]) for _ in range(2)]
        vring = [tv(BF16, [8, 129]) for _ in range(2)]
        t_gg = tv(F32, [2 * D])
        t_ga = t_gg[:, 0:D]
        t_gu = t_gg[:, D:2 * D]
        o_sb = t_gg[:, 0:1032].rearrange("p (c q d) -> p c q d", c=2, q=4)
        t_o0 = t_gg[:, 1032:1544].rearrange("p (q d) -> p q d", q=4)
        t_junk = tv(BF16, [D])
        PT2 = PT2 + [t_junk]
        PT = [PT2[i // 2][:, (i % 2) * 512:(i % 2) * 512 + 512] for i in range(6)]
        t_vn2 = [tv(BF16, [512]) for _ in range(2)]
        t_s1 = t_ga[:, 0:512]
        t_A2 = [tv(BF16, [512]) for _ in range(2)]
        AT = tv(BF16, [4, 512])
        t_r8 = tv(F32, [8])
        obn = tv(BF16, [4, 512])
        boT = tv(BF16, [4, 512])
        t_sa = tv(F32, [512])
        t_sb = tv(F32, [512])
        t_m1 = t_sa
        t_m2 = t_sb
        mT = tv(BF16, [8, 512])
        t_sq2 = [tv(BF16, [512]), t_A2[1]]
        t_rs2 = [tv(F32, [512]), t_sb]
        q_sqres = [("qsq", 0), ("A", 1)]
        q_rsres = [("qrs", 0), "sb"]
        t_xnb = tv(BF16, [D])
        t_ss = tv(F32, [16])
        t_r = tv(F32, [16])

        print("tail view end", tv.off, RLIM)
        cnt = {"x": 0, "wp": 0, "ps": 0, "o": 0}

        def load_x(src_rows, part="both"):
            v = src_rows.rearrange("(b p) d -> p b d", p=128)
            if part != "sq":
                for b4 in range(4):
                    A("sp", lambda e, b4=b4: e.dma_start(out=xres[:, b4 * 4:(b4 + 1) * 4, :], in_=v[:, b4 * 4:(b4 + 1) * 4, :]),
                      writes=[("xr", b) for b in range(b4 * 4, b4 * 4 + 4)], dma_ch=("x", b4))
            if part != "dma":
                for b4 in range(4):
                    sq_blocks(list(range(b4 * 4, b4 * 4 + 4)), f_junk)

        def sq_blocks(blks, junk):
            c = blks[0] // 4
            A("dve", lambda e: e.memset(pss[:, blks[0]:blks[-1] + 1], 0.0), writes=[("pss", c)])
            for b in blks:
                A("act", lambda e, b=b: e.activation(out=junk, in_=xres[:, b, :], func=AF.Square, accum_out=pss[:, b:b + 1]),
                  reads=[("xr", b), ("pss", c)], writes=["njunk", ("pss", c)])

        def norm_to_T(blks, gi, dstT, xnb, tbank, col0=0, reuse_rs=False):
            c = blks[0] // 4
            lo, hi = blks[0], blks[-1] + 1
            if not reuse_rs:
                A("act", lambda e: e.activation(out=prs[:, lo:hi], in_=pss[:, lo:hi], func=AF.Ln, scale=1.0 / D, bias=epsc),
                  reads=[("pss", c), "epsc"], writes=[("prs", c)])
                A("act", lambda e: e.activation(out=prs[:, lo:hi], in_=prs[:, lo:hi], func=AF.Exp, scale=-0.5),
                  reads=[("prs", c)], writes=[("prs", c)])
            for i, b in enumerate(blks):
                xb = xnb[i % len(xnb)]
                rn = ("nxnb", i % len(xnb))
                A("act", lambda e, b=b, xb=xb: e.mul(out=xb, in_=xres[:, b, :], mul=prs[:, b:b + 1]),
                  reads=[("xr", b), ("prs", c)], writes=[rn])
                bank = tbank[i % len(tbank)]
                for k in range(8):
                    A("pe", lambda e, k=k, xb=xb, bank=bank: e.transpose(
                        out=psbank_bf(bank)[:, k * 128:(k + 1) * 128], in_=xb[:, k * 128:(k + 1) * 128], identity=idb),
                      reads=[rn, "idb"], writes=[("ps", bank)])
                cc = col0 + i * 128
                A("dve", lambda e, bank=bank, cc=cc: e.tensor_tensor(
                    out=dstT[:, :, cc:cc + 128], in0=psbank_bf(bank).rearrange("p (k t) -> p k t", k=8),
                    in1=gcol[:, gi, :].unsqueeze(2).to_broadcast([128, 8, 128]), op=ALU.mult),
                  reads=[("ps", bank), "gcol"], writes=[("xT", (col0 // 128) + i)])

        def wload(dst, src, res, chname):
            war = [("wpR", res[1])] if isinstance(res, tuple) else []
            A("pool", lambda e: e.dma_start(out=dst, in_=src), writes=[res], dma_ch=chname, war=war)

        def ffn(w_in_h, w_out_h, gi):
            win = w_in_h.ap().rearrange("(k p) c -> p k c", p=128)
            wout = w_out_h.ap().rearrange("(k p) c -> p k c", p=128)
            for ch in range(4):
                norm_to_T(list(range(4 * ch, 4 * ch + 4)), gi, xnT, f_xnb, [6, 7], col0=512 * ch)
            groups = [(0, 4), (4, 4), (8, 4), (12, 4), (16, 4), (20, 2)]
            for (c0, n) in groups:
                for pc in range(n // 2):
                    wi = cnt["wp"] % 2
                    cnt["wp"] += 1
                    w = wp[wi].rearrange("p (k c) -> p k c", k=8)
                    cc = (c0 + 2 * pc) * 128
                    wload(w[:, :, 0:256], win[:, :, cc:cc + 256], ("wpa", wi), ("wpa", wi))
                    wload(w[:, :, 256:512], win[:, :, DFF + cc:DFF + cc + 256], ("wpb", wi), ("wpb", wi))
                    if pc == n // 2 - 1:
                        wload(wo[:, 0:n, :], wout[:, c0:c0 + n, :], "wo", "wo")
                    for pr in range(2):
                        slot = 2 * pc + pr
                        for t in range(4):
                            pi = cnt["ps"] % 2
                            cnt["ps"] += 1
                            bg, bu = 2 * pi, 2 * pi + 1
                            for k in range(8):
                                A("pe", lambda e, k=k, w=w, pr=pr, t=t, bg=bg: e.matmul(
                                    out=psbank(bg), lhsT=w[:, k, pr * 128:(pr + 1) * 128],
                                    rhs=xnT[:, k, t * 512:(t + 1) * 512], start=(k == 0), stop=(k == 7)),
                                  reads=[("wpa", wi), ("wpR", wi)] + [("xT", 4 * t + j) for j in range(4)], writes=[("ps", bg)])
                            for k in range(8):
                                A("pe", lambda e, k=k, w=w, pr=pr, t=t, bu=bu: e.matmul(
                                    out=psbank(bu), lhsT=w[:, k, 256 + pr * 128:256 + (pr + 1) * 128],
                                    rhs=xnT[:, k, t * 512:(t + 1) * 512], start=(k == 0), stop=(k == 7)),
                                  reads=[("wpb", wi), ("wpR", wi)] + [("xT", 4 * t + j) for j in range(4)], writes=[("ps", bu)])
                            sg = f_sg[pi]
                            A("act", lambda e, sg=sg, bg=bg: e.activation(out=sg, in_=psbank(bg), func=AF.Silu),
                              reads=[("ps", bg)], writes=[("sg", pi)])
                            A("dve", lambda e, sg=sg, bu=bu, slot=slot, t=t: e.tensor_tensor(
                                out=actT[:, slot, t * 512:(t + 1) * 512], in0=psbank(bu), in1=sg, op=ALU.mult),
                              reads=[("ps", bu), ("sg", pi)], writes=[("act", slot, t)])
                for b in range(16):
                    yb = 2 + (cnt["o"] % 2)
                    cnt["o"] += 1
                    for half in range(2):
                        for s in range(n):
                            A("pe", lambda e, b=b, half=half, s=s, yb=yb, n=n: e.matmul(
                                out=PS[yb][:, half * 512:(half + 1) * 512], lhsT=actT[:, s, b * 128:(b + 1) * 128],
                                rhs=wo[:, s, half * 512:(half + 1) * 512], start=(s == 0), stop=(s == n - 1)),
                              reads=[("act", s, b // 4), "wo"], writes=[("ps", 2 * yb), ("ps", 2 * yb + 1)])
                    A("dve", lambda e, b=b, yb=yb: e.scalar_tensor_tensor(
                        out=xres[:, b, :], in0=PS[yb][:, :], scalar=0.5, in1=xres[:, b, :], op0=ALU.mult, op1=ALU.add),
                      reads=[("ps", 2 * yb), ("ps", 2 * yb + 1), ("xr", b)], writes=[("xr", b)])
                    if c0 == 20 and b % 4 == 3:
                        sq_blocks(list(range(b - 3, b + 1)), f_junk)

        wmix = H["w_in"].ap().rearrange("(k p) c -> p k c", p=128)

        def qk_pre(ps_i, sq, sq_res):
            A("act", lambda e: e.activation(out=sq, in_=psbank(ps_i), func=AF.Square), reads=[("ps", ps_i)], writes=[sq_res])

        def qk_post(ps_i, ss_i, gcolq, sq, rs, dst, dst_res, rd_extra, sq_res, rs_res):
            A("pe", lambda e: e.matmul(out=psbank(ss_i), lhsT=bones, rhs=sq, start=True, stop=True),
              reads=[sq_res, "bones"], writes=[("ps", ss_i)])
            A("act", lambda e: e.activation(out=rs, in_=psbank(ss_i), func=AF.Ln, bias=epsc), reads=[("ps", ss_i), "epsc"],
              writes=[rs_res])
            A("act", lambda e: e.activation(out=rs, in_=rs, func=AF.Exp, scale=-0.5), reads=[rs_res], writes=[rs_res])
            A("dve", lambda e: e.scalar_tensor_tensor(out=dst, in0=psbank(ps_i), scalar=gcolq, in1=rs, op0=ALU.mult,
                                                      op1=ALU.mult),
              reads=[("ps", ps_i), rs_res] + rd_extra, writes=[dst_res])

        def mix_kv():
            for ch in range(4):
                norm_to_T(list(range(4 * ch, 4 * ch + 4)), 1, xnT, f_xnb, [6, 7], col0=512 * ch)
            wk = wp[0].rearrange("p (k c) -> p k c", k=8)
            wv = wp[1].rearrange("p (k c) -> p k c", k=8)
            wload(wk, wmix[:, :, 1536:2048], ("wpa", 0), ("wpa", 0))
            wload(wv, wmix[:, :, 2048:2560], ("wpa", 1), ("wpa", 1))
            items = [(h, t) for h in range(4) for t in range(4)]

            def k_mm(i):
                h, t = items[i]
                pb = i % 4
                for k in range(8):
                    A("pe", lambda e, k=k, h=h, t=t, pb=pb: e.matmul(
                        out=psbank(pb), lhsT=wk[:, k, h * 128:(h + 1) * 128], rhs=xnT[:, k, t * 512:(t + 1) * 512],
                        start=(k == 0), stop=(k == 7)),
                      reads=[("wpa", 0), ("wpR", 0)] + [("xT", 4 * t + j) for j in range(4)], writes=[("ps", pb)])
                qk_pre(pb, f_sq[i % 2], ("qsq", i % 2))

            def k_post(i):
                h, t = items[i]
                qk_post(i % 4, 4 + (i % 2), gk[:, 0:1], f_sq[i % 2], f_krs[i % 2], KTo[:, h, t * 512:(t + 1) * 512],
                        ("KT", h, t), ["gk"], ("qsq", i % 2), ("qrs", i % 2))

            k_mm(0)
            for i in range(16):
                if i + 1 < 16:
                    k_mm(i + 1)
                k_post(i)
            for b in range(16):
                pb = 6 + (b % 2)
                for k in range(8):
                    A("pe", lambda e, k=k, b=b, pb=pb: e.matmul(
                        out=psbank(pb), lhsT=xnT[:, k, b * 128:(b + 1) * 128], rhs=wv[:, k, :], start=(k == 0), stop=(k == 7)),
                      reads=[("wpa", 1), ("wpR", 1), ("xT", b)], writes=[("ps", pb)])
                A("act", lambda e, b=b, pb=pb: e.copy(out=Vo[:, b, :, 0:128], in_=psbank(pb).rearrange("p (h d) -> p h d", h=4)),
                  reads=[("ps", pb)], writes=[("V", b)])

        def spill_kv():
            A("sp", lambda e: e.dma_start(out=ktx.ap(), in_=KTo.rearrange("p h t -> p (h t)")),
              reads=[("KT", h, t) for h in range(4) for t in range(4)], writes=["ktx"], dma_ch="sk")
            A("sp", lambda e: e.dma_start(out=vx.ap(), in_=Vo.rearrange("p b h d -> p (b h d)")),
              reads=[("V", b) for b in range(16)], writes=["vx"], dma_ch="sv")

        ktxv = ktx.ap().rearrange("p (h t) -> p h t", h=4)
        vxv = vx.ap().rearrange("p (b h d) -> p b h d", b=16, h=4)

        tail_no = [0]

        def tload(pid, wi, loads):
            subs = [("wpa", wi), ("wpb", wi), ("wpc", wi), ("wpd", wi)]
            scr = wscr.ap()[:, pid * 4096:(pid + 1) * 4096]
            if tail_no[0] == 0:
                for (dst, src, sub) in loads:
                    A("pool", lambda e, dst=dst, src=src: e.dma_start(out=dst, in_=src), writes=[(sub, wi)],
                      dma_ch=(sub, wi), war=[("wpR", wi)])
                A("sp", lambda e: e.dma_start(out=scr, in_=wp[wi]), reads=[(sub, wi) for (_, _, sub) in loads],
                  writes=[("wscr", pid)], dma_ch=("wsw", wi))
            else:
                A("sp", lambda e: e.dma_start(out=wp[wi], in_=scr), reads=[("wscr", pid)], writes=subs,
                  dma_ch=("wsr", wi), war=[("wpR", wi)])

        def tail(qt, has_oth):
            blks = [4 * qt + j for j in range(4)]
            norm_to_T(blks, 1, hT, [t_xnb], [7], reuse_rs=True)
            hres = [("xT", j) for j in range(4)]
            wq = wp[cnt["wp"] % 2].rearrange("p (k c) -> p k c", k=8)
            wqi = cnt["wp"] % 2
            cnt["wp"] += 1
            tload(0, wqi, [(wq, wmix[:, :, 1024:1536], "wpa")])
            def q_mm(h):
                for k in range(8):
                    A("pe", lambda e, k=k, h=h: e.matmul(out=psbank(h), lhsT=wq[:, k, h * 128:(h + 1) * 128],
                                                          rhs=hT[:, k, :], start=(k == 0), stop=(k == 7)),
                      reads=[("wpa", wqi), ("wpR", wqi)] + hres, writes=[("ps", h)])
                qk_pre(h, t_sq2[h % 2], q_sqres[h % 2])

            def q_post(h):
                qk_post(h, 4 + (h % 2), gq[:, 0:1], t_sq2[h % 2], t_rs2[h % 2], QT[:, h, :], ("QT", h), ["gq"],
                        q_sqres[h % 2], q_rsres[h % 2])

            q_mm(0)
            for h in range(4):
                if h + 1 < 4:
                    q_mm(h + 1)
                q_post(h)
            wui = cnt["wp"] % 2
            cnt["wp"] += 1
            wu = wp[wui].rearrange("p (k c) -> p k c", k=8)
            tload(1, wui, [(wu, wmix[:, :, 0:512], "wpa")])
            wvi = cnt["wp"] % 2
            cnt["wp"] += 1
            wva = wp[wvi].rearrange("p (k c) -> p k c", k=8)
            tload(2, wvi, [(wva, wmix[:, :, 512:1024], "wpa")])
            def t3_proj(j):
                p = j % 2
                uvp = PS[p]
                ub0 = 2 * p
                gu = t_gg[:, p * D:(p + 1) * D]
                vn = t_vn2[p]
                Ab = t_A2[p]
                for k in range(8):
                    A("pe", lambda e, k=k, j=j, uvp=uvp: e.matmul(out=uvp[:, 0:512], lhsT=hT[:, k, j * 128:(j + 1) * 128],
                                                                   rhs=wu[:, k, :], start=(k == 0), stop=(k == 7)),
                      reads=[("wpa", wui), ("wpR", wui)] + hres, writes=[("ps", ub0)])
                for k in range(8):
                    A("pe", lambda e, k=k, j=j, uvp=uvp: e.matmul(out=uvp[:, 512:1024], lhsT=hT[:, k, j * 128:(j + 1) * 128],
                                                                   rhs=wva[:, k, :], start=(k == 0), stop=(k == 7)),
                      reads=[("wpa", wvi), ("wpR", wvi)] + hres, writes=[("ps", ub0 + 1)])
                uvr = [("ps", ub0), ("ps", ub0 + 1)]
                gr = "ga" if p == 0 else "gu"
                A("act", lambda e, uvp=uvp, gu=gu: e.activation(out=gu, in_=uvp[:, :], func=AF.Gelu_apprx_tanh), reads=uvr,
                  writes=[gr])
                A("dve", lambda e, p=p: e.memset(t_ss[:, 8 + p:9 + p], 0.0), writes=[("vss", p)])
                A("act", lambda e, p=p, gu=gu: e.activation(out=t_junk[:, 0:512], in_=gu[:, 512:1024], func=AF.Square,
                                                            accum_out=t_ss[:, 8 + p:9 + p]), reads=[gr, ("vss", p)],
                  writes=[("vss", p), "vjunk"])
                A("act", lambda e, p=p: e.activation(out=t_r[:, 10 + p:11 + p], in_=t_ss[:, 8 + p:9 + p], func=AF.Ln,
                                                     scale=1.0 / 512, bias=epsc), reads=[("vss", p), "epsc"], writes=[("vr", p)])
                A("act", lambda e, p=p: e.activation(out=t_r[:, 10 + p:11 + p], in_=t_r[:, 10 + p:11 + p], func=AF.Exp,
                                                     scale=-0.5), reads=[("vr", p)], writes=[("vr", p)])
                A("dve", lambda e, p=p, gu=gu, vn=vn: e.tensor_scalar(out=vn, in0=gu[:, 512:1024], scalar1=t_r[:, 10 + p:11 + p],
                                                                     scalar2=None, op0=ALU.mult), reads=[gr, ("vr", p)],
                  writes=[("vn", p)])

            def t3_post(j):
                p = j % 2
                uvp = PS[p]
                ub0 = 2 * p
                gu = t_gg[:, p * D:(p + 1) * D]
                vn = t_vn2[p]
                Ab = t_A2[p]
                gr = "ga" if p == 0 else "gu"
                for g in range(8):
                    A("pe", lambda e, g=g, vn=vn: e.matmul(out=psbank(6)[:, g * 64:(g + 1) * 64], lhsT=wsT[:, g, :],
                                                           rhs=vn[:, g * 64:(g + 1) * 64], start=True, stop=True),
                      reads=[("vn", p), "wsT"], writes=[("ps", 6)])
                s1 = gu[:, 512:1024]
                A("dve", lambda e, s1=s1: e.tensor_tensor(out=s1, in0=psbank(6), in1=gsgu, op=ALU.mult),
                  reads=[("ps", 6), "gsgu", gr], writes=[gr])
                A("dve", lambda e, s1=s1: e.tensor_tensor(out=s1.rearrange("p (g d) -> p g d", g=8),
                                                          in0=s1.rearrange("p (g d) -> p g d", g=8),
                                                          in1=bsT.unsqueeze(2).to_broadcast([128, 8, 64]), op=ALU.add),
                  reads=[gr, "bsT"], writes=[gr])
                A("dve", lambda e, s1=s1, gu=gu, Ab=Ab: e.tensor_tensor(out=Ab, in0=s1, in1=gu[:, 0:512], op=ALU.mult),
                  reads=[gr], writes=[("A", p)])

            def t3_post_b(j):
                p = j % 2
                Ab = t_A2[p]
                for c in range(4):
                    A("pe", lambda e, c=c, Ab=Ab: e.transpose(out=psbank_bf(7)[:, c * 128:(c + 1) * 128],
                                                              in_=Ab[:, c * 128:(c + 1) * 128], identity=idb),
                      reads=[("A", p), "idb"], writes=[("ps", 7)])
                A("act", lambda e, j=j: e.copy(out=AT[:, :, j * 128:(j + 1) * 128],
                                               in_=psbank_bf(7)[:, 0:512].rearrange("p (c t) -> p c t", c=4)),
                  reads=[("ps", 7)], writes=[("AT", j)])

            t3_proj(0)
            t3_proj(1)
            t3_post(0)
            t3_proj(2)
            t3_post_b(0)
            t3_post(1)
            t3_proj(3)
            t3_post_b(1)
            t3_post(2)
            t3_post_b(2)
            t3_post(3)
            t3_post_b(3)
            sctr = [0]
            pending = [None]

            def combine(h):
                o0 = o_sb[:, 0, :, 0:128]
                o1 = o_sb[:, 1, :, 0:128]
                gg = ["ga", "gu"]
                A("dve", lambda e: e.reciprocal(out=t_r8.rearrange("p (c q) -> p c q", c=2), in_=o_sb[:, :, :, 128]),
                  reads=gg, writes=["r8"])
                A("dve", lambda e: e.tensor_scalar(out=t_r8[:, 4:8], in0=t_r8[:, 4:8], scalar1=nlam[:, 0:1], scalar2=None,
                                                   op0=ALU.mult), reads=["r8", "nlam"], writes=["r8"])
                A("dve", lambda e: e.tensor_tensor(out=t_o0, in0=o0, in1=t_r8[:, 0:4].unsqueeze(2).to_broadcast([128, 4, 128]),
                                                   op=ALU.mult), reads=gg + ["r8"], writes=gg)
                A("dve", lambda e: e.tensor_tensor(out=o1, in0=o1, in1=t_r8[:, 4:8].unsqueeze(2).to_broadcast([128, 4, 128]),
                                                   op=ALU.mult), reads=gg + ["r8"], writes=gg)
                A("dve", lambda e: e.tensor_tensor(out=t_o0, in0=t_o0, in1=o1, op=ALU.add), reads=gg, writes=gg)
                A("dve", lambda e: e.tensor_tensor(out=o1, in0=t_o0, in1=t_o0, op=ALU.mult), reads=gg, writes=gg)
                A("dve", lambda e: e.reduce_sum(out=t_ss[:, 12:16], in_=o1, axis=AX.X), reads=gg, writes=["oss"])
                A("act", lambda e: e.activation(out=t_r[:, 4:8], in_=t_ss[:, 12:16], func=AF.Ln, scale=1.0 / 128, bias=epsc),
                  reads=["oss", "epsc"], writes=["orr"])
                A("act", lambda e: e.activation(out=t_r[:, 4:8], in_=t_r[:, 4:8], func=AF.Exp, scale=-0.5), reads=["orr"],
                  writes=["orr"])
                A("dve", lambda e: e.tensor_tensor(out=t_o0, in0=t_o0, in1=t_r[:, 4:8].unsqueeze(2).to_broadcast([128, 4, 128]),
                                                   op=ALU.mult), reads=gg + ["orr"], writes=gg)
                A("dve", lambda e, h=h: e.tensor_tensor(out=obn[:, :, h * 128:(h + 1) * 128], in0=t_o0,
                                                        in1=gsub.unsqueeze(1).to_broadcast([128, 4, 128]), op=ALU.mult),
                  reads=gg + ["gsub"], writes=[("obn", qb) for qb in range(4)])

            for h in range(4):
                kbs = [("own", kb) for kb in range(16)]
                if has_oth:
                    kbs += [("oth", kb) for kb in range(16)]
                nk = len(kbs)
                st_info = {}

                def stage_qk(ki, h=h, kbs=kbs, st_info=st_info):
                    src, kb = kbs[ki]
                    if src == "oth" and kb % 8 == 0:
                        ri = (kb // 8) % 2
                        A("sp", lambda e, h=h, kb=kb, ri=ri: e.dma_start(out=kring[ri], in_=ktxv[:, h, kb * 128:(kb + 8) * 128]),
                          reads=["ktx"], writes=[("kr", ri)], dma_ch=("kr", ri))
                        A("sp", lambda e, h=h, kb=kb, ri=ri: e.dma_start(out=vring[ri], in_=vxv[:, kb:kb + 8, h, :]),
                          reads=["vx"], writes=[("vr", ri)], dma_ch=("vr", ri))
                    if src == "own":
                        d = kb - 4 * qt
                        side = "neg" if d <= 1 else "pos"
                        bcol = (cneg if d <= 1 else cpos)[:, h:h + 1]
                        bres = "cneg" if d <= 1 else "cpos"
                        i0, i1 = max(0, d - 1), min(3, d + 1)
                        band = None
                        if i0 <= i1:
                            sc0 = 128 * (1 - (d - i0))
                            nb = 128 * (i1 - i0 + 1)
                            band = (128 * i0, nb, strips[(side, "hi")][:, h, sc0:sc0 + nb], strips[(side, "lo")][:, h, sc0:sc0 + nb])
                        kres = [("KT", h, kb // 4)]
                        vres = [("V", kb)]

                        def kt_ap(c, h=h, kb=kb):
                            return KTo[c * 64:(c + 1) * 64, h, kb * 128:(kb + 1) * 128]

                        v_ap = Vo[:, kb, h, :]
                    else:
                        ri = (kb // 8) % 2
                        bcol = coth[:, h:h + 1]
                        bres = "coth"
                        band = None
                        if kb == 0 and qt == 3:
                            band = (384, 128, bstr[("A", "hi")][:, h, :], bstr[("A", "lo")][:, h, :])
                        if kb == 15 and qt == 0:
                            band = (0, 128, bstr[("B", "hi")][:, h, :], bstr[("B", "lo")][:, h, :])
                        kres = [("kr", ri)]
                        vres = [("vr", ri)]

                        def kt_ap(c, ri=ri, kb=kb):
                            return kring[ri][c * 64:(c + 1) * 64, (kb % 8) * 128:(kb % 8 + 1) * 128]

                        v_ap = vring[ri][:, kb % 8, :]
                    pts = []
                    pi2 = ki % 2
                    slot = ki % 3
                    for c in range(2):
                        sb = 2 * pi2 + c
                        A("pe", lambda e, c=c, sb=sb, kt_ap=kt_ap, h=h, band=band: e.matmul(
                            out=psbank(sb), lhsT=kt_ap(c), rhs=QT[c * 64:(c + 1) * 64, h, :], start=True, stop=(band is None)),
                          reads=kres + [("QT", h)], writes=[("ps", sb)])
                        if band is not None:
                            c0, nb, shi, slo = band
                            A("pe", lambda e, sb=sb, c0=c0, nb=nb, shi=shi: e.matmul(
                                out=psbank(sb)[:, c0:c0 + nb], lhsT=idb, rhs=shi, start=False, stop=False),
                              reads=["idb", "strips"], writes=[("ps", sb)])
                            A("pe", lambda e, sb=sb, c0=c0, nb=nb, slo=slo: e.matmul(
                                out=psbank(sb)[:, c0:c0 + nb], lhsT=idb, rhs=slo, start=False, stop=True),
                              reads=["idb", "strips"], writes=[("ps", sb)])
                        pts.append(2 * slot + c)
                    A("act", lambda e, pi2=pi2, slot=slot, bcol=bcol: e.activation(out=PT2[slot], in_=PS[pi2][:, :], func=AF.Exp,
                                                                                 bias=bcol),
                      reads=[("ps", 2 * pi2), ("ps", 2 * pi2 + 1), bres], writes=[("PT", 2 * slot), ("PT", 2 * slot + 1)])
                    st_info[ki] = (pts, v_ap, vres)

                def stage_pv(ki, nk=nk, st_info=st_info):
                    pts, v_ap, vres = st_info[ki]
                    for c in range(2):
                        for qb in range(4):
                            idx = c * 4 + qb
                            ob_bank = 4 + idx // 3
                            oc = (idx % 3) * 129
                            A("pe", lambda e, qb=qb, ob_bank=ob_bank, oc=oc, v_ap=v_ap, sb=pts[c], ki=ki, nk=nk, idx=idx: e.matmul(
                                out=psbank(ob_bank)[:, oc:oc + 129], lhsT=PT[sb][:, qb * 128:(qb + 1) * 128], rhs=v_ap,
                                start=(ki == 0 and idx % 3 == 0), stop=(ki == nk - 1), skip_group_check=True),
                              reads=[("PT", pts[c])] + vres, writes=[("ps", ob_bank)])

                stage_qk(0)
                stage_qk(1)
                for ki in range(nk):
                    if ki + 2 < nk:
                        stage_qk(ki + 2)
                    stage_pv(ki)
                    if ki == 7 and pending[0] is not None:
                        combine(pending[0])
                        pending[0] = None
                for bi, ncol in ((0, 387), (1, 387), (2, 258)):
                    A("dve", lambda e, bi=bi, ncol=ncol: e.tensor_copy(out=t_gg[:, bi * 387:bi * 387 + ncol],
                                                                      in_=psbank(4 + bi)[:, 0:ncol]),
                      reads=[("ps", 4 + bi)], writes=["ga", "gu"])
                pending[0] = h
            combine(pending[0])
            for qb in range(4):
                for c in range(4):
                    A("pe", lambda e, qb=qb, c=c: e.transpose(out=psbank_bf(7)[:, c * 128:(c + 1) * 128],
                                                              in_=obn[:, qb, c * 128:(c + 1) * 128], identity=idb),
                      reads=[("obn", qb), "idb"], writes=[("ps", 7)])
                A("act", lambda e, qb=qb: e.copy(out=boT[:, :, qb * 128:(qb + 1) * 128],
                                                 in_=psbank_bf(7)[:, 0:512].rearrange("p (c t) -> p c t", c=4)),
                  reads=[("ps", 7)], writes=[("boT", qb)])
            wpa_v = H["w_proj_a"].ap().rearrange("(k p) c -> p k c", p=128)
            wpb_v = H["w_proj_b"].ap().rearrange("(k p) c -> p k c", p=128)
            atres = [("AT", j) for j in range(4)]
            bores = [("boT", j) for j in range(4)]
            for oc in range(8):
                pb0 = 4 * (oc % 2)
                wi = cnt["wp"] % 2
                cnt["wp"] += 1
                wb = wp[wi]
                w_a = wb[:, 0:512].rearrange("p (k c) -> p k c", k=4)
                w_b = wb[:, 512:1024].rearrange("p (k c) -> p k c", k=4)
                w_ga = wb[:, 1024:2048].rearrange("p (k c) -> p k c", k=8)
                w_gb = wb[:, 2048:3072].rearrange("p (k c) -> p k c", k=8)
                tload(3 + oc, wi, [(w_a, wpa_v[:, :, oc * 128:(oc + 1) * 128], "wpa"),
                                   (w_b, wpb_v[:, :, oc * 128:(oc + 1) * 128], "wpb"),
                                   (w_ga, wmix[:, :, 2560 + oc * 128:2560 + (oc + 1) * 128], "wpc"),
                                   (w_gb, wmix[:, :, 3584 + oc * 128:3584 + (oc + 1) * 128], "wpd")])
                for k in range(4):
                    A("pe", lambda e, k=k, w_a=w_a, pb0=pb0: e.matmul(out=psbank(pb0 + 0), lhsT=w_a[:, k, :], rhs=AT[:, k, :], start=(k == 0),
                                                             stop=(k == 3)), reads=[("wpa", wi), ("wpR", wi)] + atres, writes=[("ps", pb0 + 0)])
                for k in range(4):
                    A("pe", lambda e, k=k, w_b=w_b, pb0=pb0: e.matmul(out=psbank(pb0 + 1), lhsT=w_b[:, k, :], rhs=boT[:, k, :], start=(k == 0),
                                                             stop=(k == 3)), reads=[("wpb", wi), ("wpR", wi)] + bores, writes=[("ps", pb0 + 1)])
                for k in range(8):
                    A("pe", lambda e, k=k, w_ga=w_ga, pb0=pb0: e.matmul(out=psbank(pb0 + 2), lhsT=w_ga[:, k, :], rhs=hT[:, k, :], start=(k == 0),
                                                               stop=(k == 7)), reads=[("wpc", wi), ("wpR", wi)] + hres, writes=[("ps", pb0 + 2)])
                for k in range(8):
                    A("pe", lambda e, k=k, w_gb=w_gb, pb0=pb0: e.matmul(out=psbank(pb0 + 3), lhsT=w_gb[:, k, :], rhs=hT[:, k, :], start=(k == 0),
                                                               stop=(k == 7)), reads=[("wpd", wi), ("wpR", wi)] + hres, writes=[("ps", pb0 + 3)])
                A("act", lambda e, oc=oc, pb0=pb0: e.activation(out=t_sa, in_=psbank(pb0 + 2), func=AF.Sigmoid, bias=gbias[:, oc:oc + 1]),
                  reads=[("ps", pb0 + 2), "gbias"], writes=["sa"])
                A("act", lambda e, oc=oc, pb0=pb0: e.activation(out=t_sb, in_=psbank(pb0 + 3), func=AF.Sigmoid, bias=gbias[:, 8 + oc:9 + oc]),
                  reads=[("ps", pb0 + 3), "gbias"], writes=["sb"])
                A("dve", lambda e, pb0=pb0: e.tensor_tensor(out=t_m1, in0=psbank(pb0 + 0), in1=t_sa, op=ALU.mult), reads=[("ps", pb0 + 0), "sa"],
                  writes=["sa"])
                A("dve", lambda e, pb0=pb0: e.tensor_tensor(out=t_m2, in0=psbank(pb0 + 1), in1=t_sb, op=ALU.mult), reads=[("ps", pb0 + 1), "sb"],
                  writes=["sb"])
                A("dve", lambda e, oc=oc: e.tensor_tensor(out=mT[:, oc, :], in0=t_m1, in1=t_m2, op=ALU.add), reads=["sa", "sb"],
                  writes=[("mT", oc)])
            wo_v = H["w_out"].ap().rearrange("(k p) c -> p k c", p=128)
            mres = [("mT", oc) for oc in range(8)]
            for half in range(2):
                wi = cnt["wp"] % 2
                cnt["wp"] += 1
                w = wp[wi].rearrange("p (k c) -> p k c", k=8)
                tload(11 + half, wi, [(w, wo_v[:, :, half * 512:(half + 1) * 512], "wpa")])
                for j in range(4):
                    pb = 4 + (j % 2)
                    for k in range(8):
                        A("pe", lambda e, k=k, j=j, w=w, pb=pb: e.matmul(out=psbank(pb), lhsT=mT[:, k, j * 128:(j + 1) * 128],
                                                                          rhs=w[:, k, :], start=(k == 0), stop=(k == 7)),
                          reads=[("wpa", wi), ("wpR", wi)] + mres, writes=[("ps", pb)])
                    b = blks[j]
                    A("dve", lambda e, b=b, pb=pb, half=half: e.tensor_tensor(
                        out=xres[:, b, half * 512:(half + 1) * 512], in0=psbank(pb), in1=xres[:, b, half * 512:(half + 1) * 512],
                        op=ALU.add), reads=[("ps", pb), ("xr", b)], writes=[("xr", b)])
            sq_blocks(blks, t_junk)

        def final_store(dst_rows):
            v = dst_rows.rearrange("(b p) d -> p b d", p=128)
            pr = [("pss", c) for c in range(4)]
            A("act", lambda e: e.activation(out=prs, in_=pss, func=AF.Ln, scale=1.0 / D, bias=epsc), reads=pr + ["epsc"],
              writes=[("prs", c) for c in range(4)])
            A("act", lambda e: e.activation(out=prs, in_=prs, func=AF.Exp, scale=-0.5), reads=[("prs", c) for c in range(4)],
              writes=[("prs", c) for c in range(4)])
            for b in range(16):
                A("dve", lambda e, b=b: e.scalar_tensor_tensor(out=xres[:, b, :], in0=xres[:, b, :], scalar=prs[:, b:b + 1],
                                                               in1=gfin, op0=ALU.mult, op1=ALU.mult),
                  reads=[("xr", b), ("prs", b // 4), "gfin"], writes=[("xr", b)])
            for b4 in range(4):
                A("sp", lambda e, b4=b4: e.dma_start(out=v[:, b4 * 4:(b4 + 1) * 4, :], in_=xres[:, b4 * 4:(b4 + 1) * 4, :]),
                  reads=[("xr", b) for b in range(b4 * 4, b4 * 4 + 4)], writes=[("OUT", cnt["x"], b4)], dma_ch=("o", b4))
                out_res.append(("OUT", cnt["x"], b4))
            cnt["x"] += 1

        out_res = []

        def unit(src, dst, has_oth, preloaded=False):
            load_x(src, part="sq" if preloaded else "both")
            ffn(H["ffn1_w_in"], H["ffn1_w_out"], 0)
            mix_kv()
            barrier()
            for qt in range(4):
                tail(qt, has_oth)
                tail_no[0] += 1
            barrier()
            ffn(H["ffn2_w_in"], H["ffn2_w_out"], 2)
            final_store(dst)

        mode = debug or "full"
        if mode == "full":
            unit(H["xp"].ap()[0], yp.ap()[0], False, preloaded=True)
            unit(H["xp"].ap()[1], yp.ap()[1], False)
            load_x(H["xsx"].ap())
            ffn(H["ffn1_w_in"], H["ffn1_w_out"], 0)
            mix_kv()
            spill_kv()
            unit(H["xso"].ap(), ys.ap(), True)
        elif mode == "p0":
            unit(H["xp"].ap()[0], yp.ap()[0], False)
        elif mode == "s":
            load_x(H["xsx"].ap())
            ffn(H["ffn1_w_in"], H["ffn1_w_out"], 0)
            mix_kv()
            spill_kv()
            unit(H["xso"].ap(), ys.ap(), True)
        A("sp", None, reads=out_res)
        P.emit(st)
    return nc


def _bucket_table():
    rel = np.arange(-255, 256, dtype=np.int32)
    half = 16
    max_exact = 8
    try:
        import jax
        import jax.numpy as jnp
        with jax.default_device(jax.devices("cpu")[0]):
            r = jnp.asarray(rel)
            bucket = jnp.where(r > 0, half, 0).astype(jnp.int32)
            n = jnp.abs(r)
            nf = jnp.maximum(n, 1).astype(jnp.float32)
            large = max_exact + (jnp.log(nf / max_exact) / math.log(128 / max_exact) * (half - max_exact)).astype(jnp.int32)
            large = jnp.minimum(large, half - 1)
            out = np.asarray(bucket + jnp.where(n < max_exact, n, large))
        return rel, out.astype(np.int64)
    except Exception:
        bucket = np.where(rel > 0, half, 0)
        n = np.abs(rel)
        nf = np.maximum(n, 1).astype(np.float32)
        large = max_exact + (np.log(nf / np.float32(max_exact)) / np.float32(math.log(128 / max_exact))
                             * np.float32(half - max_exact)).astype(np.int32)
        large = np.minimum(large, half - 1)
        return rel, (bucket + np.where(n < max_exact, n, large)).astype(np.int64)


_CACHE = {}


def kernel(**inputs):
    f = lambda k: np.ascontiguousarray(np.asarray(inputs[k], dtype=np.float32))
    xp_all = f("x_prompt")
    xs_all = f("x_sample")
    shared = {}
    for k in ("ffn1_norm", "mix_norm", "ffn2_norm", "final_norm", "gate_bias", "sgu_norm", "q_norm", "k_norm",
              "lambda_q1", "lambda_k1", "lambda_q2", "lambda_k2", "diff_subln"):
        shared[k] = f(k).reshape(1, -1)
    for k in ("ffn1_w_in", "ffn1_w_out", "w_in", "sgu_w", "sgu_b", "w_proj_a", "w_proj_b", "w_out", "ffn2_w_in", "ffn2_w_out"):
        shared[k] = np.ascontiguousarray(f(k)[0])
    shared["rel_bias"] = f("rel_bias")
    shared["ident"] = np.eye(128, dtype=np.float32)
    bo = np.zeros((128, 128), np.float32)
    bo[:64, :64] = 1.0 / 64
    bo[64:, 64:] = 1.0 / 64
    shared["bones"] = bo
    rel, bk = _bucket_table()
    oh = np.zeros((32, 512), np.float32)
    for i in range(511):
        r = 255 - i
        oh[bk[r + 255], i] = 1.0
    shared["oh"] = oh
    in_maps = []
    for c in range(8):
        m = dict(shared)
        m["xp"] = np.ascontiguousarray(xp_all[2 * c:2 * c + 2])
        s = c // 2
        p = c % 2
        m["xso"] = np.ascontiguousarray(xs_all[s, p * 2048:(p + 1) * 2048])
        m["xsx"] = np.ascontiguousarray(xs_all[s, (1 - p) * 2048:(2 - p) * 2048])
        par = np.zeros((128, 2), np.float32)
        par[:, 0] = 1.0 - p
        par[:, 1] = float(p)
        m["par"] = par
        in_maps.append(m)
    if "nc" not in _CACHE:
        _CACHE["nc"] = build_program()
    res = run_bass_kernel_spmd(_CACHE["nc"], in_maps, core_ids=list(range(8)))
    y_prompt = np.empty((16, 2048, D), np.float32)
    y_sample = np.empty((4, 4096, D), np.float32)
    for c in range(8):
        r = res.results[c]
        y_prompt[2 * c:2 * c + 2] = r["yp"]
        s = c // 2
        p = c % 2
        y_sample[s, p * 2048:(p + 1) * 2048] = r["ys"]
    return (y_prompt, y_sample)
```
